# Optimizing a Trainium2 kernel written in Bass

```python
import math
import jax, jax.numpy as jnp
from jax import lax
import numpy as np

D_MODEL = 2048
BATCH = 4
SEQ = 2048
DEPTH = 2

CTX_LEN = 256
GRID_W = 64
N_EVEN = (DEPTH + 1) // 2
N_ODD = DEPTH // 2
F32 = jnp.float32
EPS = 1e-6

A_HEAD_DIM = 128
A_WIDTH = D_MODEL // 2
A_HEADS = A_WIDTH // A_HEAD_DIM
A_CHUNK = 64
B_WIDTH = D_MODEL - A_WIDTH
CONV_W = 3
AB_SPLITS = [A_WIDTH, 2 * A_WIDTH, 3 * A_WIDTH, 4 * A_WIDTH, 5 * A_WIDTH,
             5 * A_WIDTH + B_WIDTH, 5 * A_WIDTH + 2 * B_WIDTH]
AB_IN_WIDTH = 5 * A_WIDTH + 3 * B_WIDTH
HY_ORDER = 2
HY_EMB = 33
HY_BANDS = (HY_EMB - 1) // 2
HY_FILTER_HIDDEN = 64
HY_DECAY_TARGET = 1e-2
HY_FAST_PCT = 0.3
HY_SLOW_PCT = 1.5
D_FF = 4 * D_MODEL

kernel_name = "hybrid_hgrn2_shortconv_hyena_dit"


def rmsnorm(x, g):
    xf = x.astype(F32)
    xf = xf * lax.rsqrt(jnp.mean(xf * xf, axis=-1, keepdims=True) + EPS)
    return xf.astype(x.dtype) * g


def conv3_seq(u, w):
    up = jnp.pad(u, ((0, 0), (1, 1), (0, 0)))
    return w[0] * up[:, :-2] + w[1] * up[:, 1:-1] + w[2] * up[:, 2:]


def conv3_grid(u, w):
    b, l, ch = u.shape
    rows = l // GRID_W
    return conv3_seq(u.reshape(b * rows, GRID_W, ch), w).reshape(b, l, ch)


def heads(t):
    return t.reshape(t.shape[:-1] + (A_HEADS, A_HEAD_DIM))


def rev(t):
    return jnp.flip(t, axis=1)


def hgrn_gates(z, lb):
    f = lb + (1.0 - lb) * jax.nn.sigmoid(z.astype(F32))
    return jnp.log(f), 1.0 - f


def hgrn_state(logf, k, v):
    b = jnp.cumsum(logf, axis=1)
    return jnp.einsum("blhk,blhv->bhkv", k * jnp.exp(b[:, -1:] - b), v)


def hgrn_scan(q, logf, k, v, s0):
    bsz, l, h, dh = q.shape
    n = l // A_CHUNK

    def chunks(t):
        return t.reshape(bsz, n, A_CHUNK, h, dh).transpose(1, 0, 3, 2, 4)

    past_in_chunk = jnp.tril(jnp.ones((A_CHUNK, A_CHUNK), dtype=bool))[:, :, None]

    def step(s, inp):
        qc, gc, kc, vc = inp
        b = jnp.cumsum(gc, axis=2)
        o_inter = jnp.einsum("bhtk,bhkv->bhtv", qc * jnp.exp(b), s)
        diff = jnp.where(past_in_chunk, b[:, :, :, None, :] - b[:, :, None, :, :], -jnp.inf)
        att = jnp.einsum("bhtk,bhsk,bhtsk->bhts", qc, kc, jnp.exp(diff))
        o_intra = jnp.einsum("bhts,bhsv->bhtv", att, vc)
        b_last = b[:, :, -1:, :]
        s_new = jnp.exp(b_last[:, :, 0, :])[..., None] * s + jnp.einsum(
            "bhsk,bhsv->bhkv", kc * jnp.exp(b_last - b), vc)
        return s_new, o_inter + o_intra

    s_fin, o = lax.scan(step, s0, (chunks(q), chunks(logf), chunks(k), chunks(v)))
    return o.transpose(1, 0, 3, 2, 4).reshape(bsz, l, h, dh), s_fin


def hgrn_bidir(q, zf_f, zf_b, v, lb, s0_f, s0_b):
    lg_f, k_f = hgrn_gates(zf_f, lb[0])
    lg_b, k_b = hgrn_gates(zf_b, lb[1])
    q, v = heads(q.astype(F32)), heads(v.astype(F32))
    o_f, s_f = hgrn_scan(q, heads(lg_f), heads(k_f), v, s0_f)
    o_b, s_b = hgrn_scan(rev(q), rev(heads(lg_b)), rev(heads(k_b)), rev(v), s0_b)
    return o_f + rev(o_b), s_f, s_b


def hgrn_context_states(zf_f, zf_b, v, lb):
    lg_f, k_f = hgrn_gates(zf_f, lb[0])
    lg_b, k_b = hgrn_gates(zf_b, lb[1])
    v = heads(v.astype(F32))
    s_f = hgrn_state(heads(lg_f), heads(k_f), v)
    s_b = hgrn_state(rev(heads(lg_b)), rev(heads(k_b)), rev(v))
    return s_f, s_b


def hgrn_readout(o, g, gnorm_g):
    on = rmsnorm(o, gnorm_g.reshape(A_HEADS, A_HEAD_DIM))
    return on.reshape(g.shape).astype(g.dtype) * jax.nn.silu(g)


def hgrn_conv_mixer(h, hc, w_in, conv_w, gnorm_g, w_out, lb, ctx_out):
    zf_f, zf_b, v, q, g, u, gb, gc = jnp.split(h @ w_in, AB_SPLITS, axis=-1)
    if ctx_out:
        czf_f, czf_b, cv, cq, cg, cu, cgb, cgc = jnp.split(hc @ w_in, AB_SPLITS, axis=-1)
        zero = jnp.zeros((hc.shape[0], A_HEADS, A_HEAD_DIM, A_HEAD_DIM), F32)
        oc, s_f, s_b = hgrn_bidir(cq, czf_f, czf_b, cv, lb, zero, zero)
        yc = jnp.concatenate([hgrn_readout(oc, cg, gnorm_g),
                              cgb * conv3_seq(cgc * cu, conv_w)], axis=-1) @ w_out
    else:
        czf_f, czf_b, cv = jnp.split(hc @ w_in[:, :3 * A_WIDTH], 3, axis=-1)
        s_f, s_b = hgrn_context_states(czf_f, czf_b, cv, lb)
        yc = None
    o, _, _ = hgrn_bidir(q, zf_f, zf_b, v, lb, s_f, s_b)
    y = jnp.concatenate([hgrn_readout(o, g, gnorm_g),
                         gb * conv3_grid(gc * u, conv_w)], axis=-1) @ w_out
    return y, yc


def hyena_filters(l, fw1, fb1, fw2, fb2, fw3, fb3, fw4, freq):
    pos = jnp.arange(l, dtype=F32)
    t = jnp.linspace(0.0, 1.0, l, dtype=F32)
    w = 2.0 * math.pi * pos / l
    bands = jnp.linspace(1e-4, HY_BANDS - 1, HY_BANDS, dtype=F32)
    ang = w[:, None] * bands[None, :]
    z = jnp.concatenate([t[:, None], jnp.cos(ang), -jnp.sin(ang)], axis=-1)
    freq = freq.astype(F32)
    hdn = jnp.sin(freq * (z @ fw1.astype(F32) + fb1.astype(F32)))
    hdn = jnp.sin(freq * (hdn @ fw2.astype(F32) + fb2.astype(F32)))
    hdn = jnp.sin(freq * (hdn @ fw3.astype(F32) + fb3.astype(F32)))
    filt = (hdn @ fw4.astype(F32)).reshape(l, HY_ORDER, 2, D_MODEL)
    max_decay = math.log(HY_DECAY_TARGET) / HY_FAST_PCT
    min_decay = math.log(HY_DECAY_TARGET) / HY_SLOW_PCT
    deltas = jnp.abs(jnp.linspace(min_decay, max_decay, D_MODEL, dtype=F32))
    window = jnp.exp(-t[:, None] * deltas[None, :])
    return filt * window[:, None, None, :]


def long_conv(z, filt, skip):
    l = z.shape[1]
    fw, bw = filt[:, 0], filt[:, 1]
    taps = jnp.concatenate([(fw[0] + bw[0])[None], fw[1:], jnp.zeros_like(fw[:1]), bw[:0:-1]], axis=0)
    taps = taps / jnp.sum(jnp.abs(taps), axis=0, keepdims=True)
    zf = z.astype(F32)
    y = jnp.fft.irfft(jnp.fft.rfft(zf, n=2 * l, axis=1) * jnp.fft.rfft(taps, n=2 * l, axis=0)[None],
                      n=2 * l, axis=1)[:, :l]
    return (y + zf * skip.astype(F32)).astype(z.dtype)


def hyena_mixer(h, in_w, short_w, out_w, fparams, skip, conv_fn):
    p = conv_fn(h @ in_w, short_w)
    x1, x2, z = jnp.split(p, 3, axis=-1)
    filt = hyena_filters(h.shape[1], *fparams)
    for o, gate in enumerate((x1, x2)):
        z = gate * long_conv(z, filt[:, o], skip[o])
    return z @ out_w


def sq_relu_mlp(h, w1, w2):
    return jnp.square(jax.nn.relu(h @ w1)) @ w2


def setup_inputs(seed: int = 0) -> dict:
    key = jax.random.key(seed)
    ks = jax.random.split(key, 32)
    D = D_MODEL

    def nrm(k, shape, scale):
        return scale * jax.random.normal(k, shape, F32)

    return {
        "x": nrm(ks[0], (BATCH, SEQ, D), 1.0),
        "c": nrm(ks[1], (BATCH, D), 1.0),
        "ctx": nrm(ks[2], (BATCH, CTX_LEN, D), 1.0),
        "c_ctx": nrm(ks[3], (D,), 1.0),
        "ada_w": nrm(ks[4], (DEPTH, D, 6 * D), D ** -0.5),
        "ada_b": nrm(ks[5], (DEPTH, 6 * D), 0.02),
        "norm_g": 1.0 + nrm(ks[6], (DEPTH, 2, D), 0.02),
        "lb_logits": nrm(ks[7], (2, DEPTH + 1, A_WIDTH), 0.5),
        "ab_w_in": nrm(ks[8], (N_EVEN, D, AB_IN_WIDTH), D ** -0.5),
        "ab_conv_w": nrm(ks[9], (N_EVEN, CONV_W, B_WIDTH), CONV_W ** -0.5),
        "ab_gnorm_g": 1.0 + nrm(ks[10], (N_EVEN, A_WIDTH), 0.02),
        "ab_w_out": nrm(ks[11], (N_EVEN, D, D), D ** -0.5),
        "hy_in_w": nrm(ks[12], (N_ODD, D, 3 * D), D ** -0.5),
        "hy_short_w": nrm(ks[13], (N_ODD, CONV_W, 3 * D), CONV_W ** -0.5),
        "hy_out_w": nrm(ks[14], (N_ODD, D, D), D ** -0.5),
        "hy_fw1": nrm(ks[15], (N_ODD, HY_EMB, HY_FILTER_HIDDEN), HY_EMB ** -0.5),
        "hy_fb1": nrm(ks[16], (N_ODD, HY_FILTER_HIDDEN), 0.02),
        "hy_fw2": nrm(ks[17], (N_ODD, HY_FILTER_HIDDEN, HY_FILTER_HIDDEN), HY_FILTER_HIDDEN ** -0.5),
        "hy_fb2": nrm(ks[18], (N_ODD, HY_FILTER_HIDDEN), 0.02),
        "hy_fw3": nrm(ks[19], (N_ODD, HY_FILTER_HIDDEN, HY_FILTER_HIDDEN), HY_FILTER_HIDDEN ** -0.5),
        "hy_fb3": nrm(ks[20], (N_ODD, HY_FILTER_HIDDEN), 0.02),
        "hy_fw4": nrm(ks[21], (N_ODD, HY_FILTER_HIDDEN, HY_ORDER * 2 * D), HY_FILTER_HIDDEN ** -0.5),
        "hy_freq": 1.0 + nrm(ks[22], (N_ODD, HY_FILTER_HIDDEN), 0.02),
        "hy_skip": nrm(ks[23], (N_ODD, HY_ORDER, D), 0.1),
        "mlp_w1": nrm(ks[24], (DEPTH, D, D_FF), D ** -0.5),
        "mlp_w2": nrm(ks[25], (DEPTH, D_FF, D), D_FF ** -0.5),
        "final_g": 1.0 + nrm(ks[26], (D,), 0.02),
    }


def reference(x, c, ctx, c_ctx, ada_w, ada_b, norm_g, lb_logits, ab_w_in, ab_conv_w, ab_gnorm_g,
              ab_w_out, hy_in_w, hy_short_w, hy_out_w, hy_fw1, hy_fb1, hy_fw2, hy_fb2, hy_fw3,
              hy_fb3, hy_fw4, hy_freq, hy_skip, mlp_w1, mlp_w2, final_g):
    lb_table = jnp.cumsum(jax.nn.softmax(lb_logits.astype(F32), axis=1), axis=1)
    silu_c = jax.nn.silu(c)
    silu_cc = jax.nn.silu(c_ctx)
    xc = ctx
    for l in range(DEPTH):
        even = l % 2 == 0
        ctx_out = any(j % 2 == 0 for j in range(l + 1, DEPTH))
        i = l // 2
        sh1, sc1, g1, sh2, sc2, g2 = [m[:, None, :] for m in
                                       jnp.split(silu_c @ ada_w[l] + ada_b[l], 6, axis=-1)]
        h = rmsnorm(x, norm_g[l, 0]) * (1.0 + sc1) + sh1
        hc, cmod = None, None
        if even or ctx_out:
            n_mod = 6 if ctx_out else 2
            cmod = jnp.split(silu_cc @ ada_w[l][:, :n_mod * D_MODEL] + ada_b[l][:n_mod * D_MODEL], n_mod)
            hc = rmsnorm(xc, norm_g[l, 0]) * (1.0 + cmod[1]) + cmod[0]
        if even:
            y, yc = hgrn_conv_mixer(h, hc, ab_w_in[i], ab_conv_w[i], ab_gnorm_g[i], ab_w_out[i],
                                    lb_table[:, l], ctx_out)
        else:
            fparams = (hy_fw1[i], hy_fb1[i], hy_fw2[i], hy_fb2[i], hy_fw3[i], hy_fb3[i], hy_fw4[i], hy_freq[i])
            y = hyena_mixer(h, hy_in_w[i], hy_short_w[i], hy_out_w[i], fparams, hy_skip[i], conv3_grid)
            yc = hyena_mixer(hc, hy_in_w[i], hy_short_w[i], hy_out_w[i], fparams, hy_skip[i],
                             conv3_seq) if ctx_out else None
        x = x + g1 * y
        x = x + g2 * sq_relu_mlp(rmsnorm(x, norm_g[l, 1]) * (1.0 + sc2) + sh2, mlp_w1[l], mlp_w2[l])
        if ctx_out:
            xc = xc + cmod[2] * yc
            xc = xc + cmod[5] * sq_relu_mlp(rmsnorm(xc, norm_g[l, 1]) * (1.0 + cmod[4]) + cmod[3],
                                            mlp_w1[l], mlp_w2[l])
    return rmsnorm(x, final_g)
```

```python
import numpy as np
import ml_dtypes
import concourse.bass as bass
import concourse.mybir as mybir
from concourse.bass_utils import run_bass_kernel_spmd

F32 = mybir.dt.float32
BF16 = mybir.dt.bfloat16
AF = mybir.ActivationFunctionType
ALU = mybir.AluOpType
AX = mybir.AxisListType
NPBF = ml_dtypes.bfloat16

D = 2048
KC = 16
EPS = 1e-6
ENGS = ("pe", "act", "dve", "pool", "sp")


class T:
    __slots__ = ("name", "lastw", "readers", "sem", "dman")

    def __init__(self, name):
        self.name = name
        self.lastw = None
        self.readers = []
        self.sem = None
        self.dman = 0


def TL(name, n):
    return [T("%s%d" % (name, i)) for i in range(n)]


class Prog:
    def __init__(self, nc, pfx=""):
        self.nc = nc
        self.pfx = pfx
        self.q = {e: [] for e in ENGS}
        self.cnt = {e: 0 for e in ENGS}
        self.seen = {e: {} for e in ENGS}
        self.sems = {}
        self._stack = []
        self._dma_tiles = []

    def _enter(self, cm):
        h = cm.__enter__()
        self._stack.append(cm)
        return h

    def sem(self, name):
        return self._enter(self.nc.semaphore(self.pfx + name))

    def sbuf(self, name, shape, dt):
        return self._enter(self.nc.sbuf_tensor(self.pfx + name, list(shape), dt))

    def psum(self, name, shape, dt=F32):
        return self._enter(self.nc.psum_tensor(self.pfx + name, list(shape), dt))

    def close(self):
        while self._stack:
            self._stack.pop().__exit__(None, None, None)

    def _engsem(self, e):
        if e not in self.sems:
            self.sems[e] = self.sem("s_" + e)
        return self.sems[e]

    def _collect(self, eng, reads, writes, is_dma):
        waits = {}

        def need(ev, war=False):
            if ev is None:
                return
            key, val, e = ev
            if e == eng and not is_dma and key == eng and (war or eng == "pe"):
                return
            if waits.get(key, 0) < val:
                waits[key] = val

        for t in reads:
            need(t.lastw)
        for t in writes:
            need(t.lastw)
            for r in t.readers:
                need(r, war=True)
        out = []
        seen = self.seen[eng]
        for key, val in waits.items():
            if seen.get(key, 0) >= val:
                continue
            seen[key] = val
            out.append((key, val))
        return out

    def _semobj(self, key):
        if isinstance(key, str):
            return self._engsem(key)
        return key.sem

    def op(self, eng, fn, reads=(), writes=()):
        waits = self._collect(eng, reads, writes, False)
        self.cnt[eng] += 1
        self._engsem(eng)
        ev = (eng, self.cnt[eng], eng)
        for t in reads:
            t.readers.append(ev)
        for t in writes:
            t.lastw = ev
            t.readers = []
        self.q[eng].append((waits, fn, (eng, 1)))

    def dma(self, eng, fn, reads=(), writes=(), semtile=None):
        waits = self._collect(eng, reads, writes, True)
        st = semtile or (writes[0] if writes else reads[0])
        if st.sem is None:
            st.sem = self.sem("d_" + st.name)
            self._dma_tiles.append(st)
        st.dman += 1
        ev = (st, 16 * st.dman, "dma")
        for t in reads:
            t.readers.append(ev)
        for t in writes:
            t.lastw = ev
            t.readers = []
        self.q[eng].append((waits, fn, (st, 16)))

    def barrier(self):
        tiles = [t for t in self._dma_tiles]
        for e in ENGS:
            if e == "pe" and not self.q[e]:
                continue
            waits = []
            seen = self.seen[e]
            for e2 in ENGS:
                if e2 != e and e2 in self.sems and self.cnt[e2] > seen.get(e2, 0):
                    seen[e2] = self.cnt[e2]
                    waits.append((e2, self.cnt[e2]))
            for t in tiles:
                v = 16 * t.dman
                if v > seen.get(t, 0):
                    seen[t] = v
                    waits.append((t, v))
            self.q[e].append((waits, None, None))

    def wait_all(self, eng, tiles):
        waits = self._collect(eng, tiles, tiles, True)
        self.q[eng].append((waits, None, None))

    def emit(self):
        engobj = {"pe": "tensor", "act": "scalar", "dve": "vector", "pool": "gpsimd", "sp": "sync"}
        with self.nc.Block() as block:
            for e in ENGS:
                lst = self.q[e]
                if not lst:
                    continue

                def body(eobj, lst=lst):
                    for waits, fn, inc in lst:
                        for key, val in waits:
                            eobj.wait_ge(self._semobj(key), val)
                        if fn is None:
                            continue
                        ins = fn(eobj)
                        ins.then_inc(self._semobj(inc[0]), inc[1])

                getattr(block, engobj[e])(body)


class Pool:
    def __init__(self, P, name, n, shape, dt, psum=False):
        self.bufs = []
        for i in range(n):
            h = P.psum("%s%d" % (name, i), shape, dt) if psum else P.sbuf("%s%d" % (name, i), shape, dt)
            self.bufs.append((h, T("%s%d" % (name, i))))
        self.i = 0

    def get(self):
        b = self.bufs[self.i % len(self.bufs)]
        self.i += 1
        return b

    @classmethod
    def views(cls, name, aps):
        self = cls.__new__(cls)
        self.bufs = [(ap, T("%s%d" % (name, i))) for i, ap in enumerate(aps)]
        self.i = 0
        return self


def dram_in(nc, name, shape, dt=F32):
    return nc.dram_tensor(name, list(shape), dt, kind="ExternalInput").ap()


def dram_out(nc, name, shape, dt=F32):
    return nc.dram_tensor(name, list(shape), dt, kind="ExternalOutput").ap()


def load_w_block(P, wpool, w_ap, rows_kc, col0, ncols, queue="pool"):
    wh, wt = wpool.get()
    src = w_ap[:, col0:col0 + ncols].rearrange("(k p) n -> p k n", p=128)
    P.dma(queue, lambda e: e.dma_start(out=wh[:, 0:rows_kc, 0:ncols], in_=src), writes=[wt])
    return wh, wt


def norm_stats(P, C, xs_fn, x_ts, tok_blocks, ntok, pre=None):
    ones, sqpool, stat, stat_t, rstd, rstd_t = C["ones"], C["sqpool"], C["stat"], C["stat_t"], C["rstd"], C["rstd_t"]
    for k in range(KC):
        if pre is not None:
            pre(k)
        sq, sqt = sqpool.get()
        xap, xtk = xs_fn(k), (x_ts(k) if callable(x_ts) else x_ts[k])
        P.op("act", lambda e, xap=xap, sq=sq: e.activation(out=sq[:, 0:ntok], in_=xap, func=AF.Square),
             reads=[xtk], writes=[sqt])

        def mm(e, k=k, sq=sq):
            for bi, (t0, tn) in enumerate(tok_blocks):
                i = e.matmul(stat[bi][:, 0:tn], lhsT=ones[:], rhs=sq[:, t0:t0 + tn], start=(k == 0), stop=(k == KC - 1))
            return i
        P.op("pe", mm, reads=[sqt, C["ones_t"]], writes=stat_t)
    for bi, (t0, tn) in enumerate(tok_blocks):
        P.op("act", lambda e, bi=bi, t0=t0, tn=tn: e.activation(
            out=rstd[:, t0:t0 + tn], in_=stat[bi][:, 0:tn], func=AF.Sqrt, scale=1.0 / D, bias=C["eps"][:, 0:1]),
            reads=[stat_t[bi], C["eps_t"]], writes=[rstd_t])
    P.op("dve", lambda e: e.reciprocal(out=rstd[:, 0:ntok], in_=rstd[:, 0:ntok]), reads=[rstd_t], writes=[rstd_t])


def load_scal(P, scal, scal_t, scal_d):
    if isinstance(scal_d, (list, tuple)):
        for i, ap in enumerate(scal_d):
            P.dma("sp", lambda e, i=i, ap=ap: e.dma_start(out=scal[:, i * KC:(i + 1) * KC], in_=ap), writes=[scal_t])
    else:
        P.dma("sp", lambda e: e.dma_start(out=scal[:], in_=scal_d), writes=[scal_t])


def mod_coef(P, C, scal, scal_t, gi, sci, name):
    a = P.sbuf(name, [128, KC], F32)
    at = T(name)
    P.op("dve", lambda e: e.scalar_tensor_tensor(out=a[:], in0=scal[:, sci * KC:(sci + 1) * KC], scalar=1.0,
                                                 in1=scal[:, gi * KC:(gi + 1) * KC], op0=ALU.add, op1=ALU.mult),
         reads=[scal_t], writes=[at])
    return a, at


NMC = 3072


def emit_mod(P, nc, cT, aw, ab, out):
    cs = P.sbuf("cs", [128, KC, 5], F32)
    sil = P.sbuf("sil", [128, KC, 5], F32)
    sg = P.sbuf("sg", [128, KC, 5], F32)
    bs = P.sbuf("bs", [128, NMC // 128], F32)
    res = P.sbuf("res", [128, NMC // 128, 5], F32)
    ps = P.psum("mps", [128, 512])
    wpool = Pool(P, "mw", 2, [128, KC, 512], F32)
    tcs, tsil, tsg, tbs, tres, tps, tout = T("cs"), T("sil"), T("sg"), T("bs"), T("res"), T("mps"), T("mout")
    P.dma("sp", lambda e: e.dma_start(out=cs[:], in_=cT.rearrange("(k p) j -> p k j", p=128)), writes=[tcs])
    P.dma("sp", lambda e: e.dma_start(out=bs[:], in_=ab), writes=[tbs])
    P.op("act", lambda e: e.activation(out=sg[:], in_=cs[:], func=AF.Sigmoid), reads=[tcs], writes=[tsg])
    P.op("dve", lambda e: e.tensor_tensor(out=sil[:], in0=cs[:], in1=sg[:], op=ALU.mult), reads=[tcs, tsg], writes=[tsil])
    nblk = NMC // 512
    for blk in range(nblk):
        wh, wt = wpool.get()
        q = "sp" if blk % 2 == 0 else "act"
        P.dma(q, lambda e, wh=wh, blk=blk: e.dma_start(
            out=wh[:], in_=aw[:, blk * 512:(blk + 1) * 512].rearrange("(k p) n -> p k n", p=128)), writes=[wt])

        def mm(e, wh=wh, blk=blk):
            for ci in range(4):
                i0 = (blk * 4 + ci) * 5
                for k in range(KC):
                    ins = e.matmul(ps[:, i0:i0 + 5], lhsT=wh[:, k, ci * 128:(ci + 1) * 128], rhs=sil[:, k, :],
                                   start=(k == 0), stop=(k == KC - 1))
            return ins
        P.op("pe", mm, reads=[wt, tsil], writes=[tps])
    nch = NMC // 128
    P.op("dve", lambda e: e.tensor_tensor(out=res[:], in0=ps[:, 0:nch * 5].rearrange("p (c j) -> p c j", j=5),
                                          in1=bs[:].unsqueeze(2).to_broadcast([128, nch, 5]), op=ALU.add),
         reads=[tps, tbs], writes=[tres])
    P.dma("sp", lambda e: e.dma_start(out=out, in_=res[:].rearrange("p c j -> p (c j)")), reads=[tres], writes=[tout])
    P.wait_all("sp", [tout])


def build_mod():
    nc = bass.Bass("TRN2", target_bir_lowering=False)
    cT = dram_in(nc, "i_cT", [D, 5])
    aw = dram_in(nc, "i_aw", [D, NMC])
    ab = dram_in(nc, "i_ab", [128, NMC // 128])
    out = dram_out(nc, "o_mod", [128, (NMC // 128) * 5])
    P = Prog(nc)
    emit_mod(P, nc, cT, aw, ab, out)
    P.emit()
    P.close()
    return nc


NT = 1024
TB2 = [(0, 512), (512, 512)]


def emit_tok(P, nc, mT, xT, scal_d, w_out, w1, w2, x_out, h_out, final):
    x_sb = P.sbuf("x_sb", [128, KC, NT], F32)
    hb = P.sbuf("hb", [128, KC, NT], BF16)
    scal = P.sbuf("scal", [128, 8 * KC], F32)
    ones = P.sbuf("ones", [128, 128], BF16)
    rstd = P.sbuf("rstd", [128, NT], F32)
    tmp = P.sbuf("tmpn", [128, NT], F32)
    wpool = Pool(P, "w", 3, [128, KC, 512], BF16)
    apool = Pool(P, "ab", 2, [128, 4, NT], BF16)
    rpool = Pool(P, "rl", 2, [128, 512], F32)
    sqpool = Pool(P, "sq", 2, [128, NT], BF16)
    pp = Pool(P, "pp", 6, [128, 512], F32, psum=True)
    stat = [P.psum("st%d" % i, [128, 512]) for i in range(2)]
    x_t, hb_t = TL("x", KC), TL("hb", KC)
    scal_t, ones_t, rstd_t, tmp_t = T("scal"), T("ones"), T("rstd"), T("tmpn")
    epsb = P.sbuf("epsb", [128, 1], F32)
    eps_t = T("epsb")
    P.op("pool", lambda e: e.memset(epsb[:], EPS), writes=[eps_t])
    C = dict(ones=ones, sqpool=sqpool, stat=stat, stat_t=TL("st", 2), rstd=rstd, rstd_t=rstd_t, eps=epsb, eps_t=eps_t, ones_t=ones_t)
    tout = T("tokout")

    P.op("pool", lambda e: e.memset(ones[:], 1.0), writes=[ones_t])
    load_scal(P, scal, scal_t, scal_d)
    for k in range(KC):
        q = "sp"
        P.dma(q, lambda e, k=k: e.dma_start(out=hb[:, k, :], in_=mT[k * 128:(k + 1) * 128, :]), writes=[hb_t[k]])
        P.dma(q, lambda e, k=k: e.dma_start(out=x_sb[:, k, :], in_=xT[k * 128:(k + 1) * 128, :]), writes=[x_t[k]])

    def sc(i, c):
        return scal[:, i * KC + c:i * KC + c + 1]

    for cb in range(4):
        wh, wt = load_w_block(P, wpool, w_out, KC, cb * 512, 512)
        for ci in range(4):
            oc = cb * 4 + ci
            for (t0, tn) in TB2:
                ph, pt = pp.get()

                def mm(e, wh=wh, ci=ci, t0=t0, tn=tn, ph=ph):
                    for k in range(KC):
                        ins = e.matmul(ph[:, 0:tn], lhsT=wh[:, k, ci * 128:(ci + 1) * 128], rhs=hb[:, k, t0:t0 + tn],
                                       start=(k == 0), stop=(k == KC - 1))
                    return ins
                P.op("pe", mm, reads=[wt] + hb_t, writes=[pt])
                P.op("dve", lambda e, oc=oc, t0=t0, tn=tn, ph=ph: e.scalar_tensor_tensor(
                    out=x_sb[:, oc, t0:t0 + tn], in0=ph[:, 0:tn], scalar=sc(0, oc), in1=x_sb[:, oc, t0:t0 + tn],
                    op0=ALU.mult, op1=ALU.add), reads=[pt, scal_t, x_t[oc]], writes=[x_t[oc]])

    def norm_to_hb(gi, sci, shi, name):
        a, at = mod_coef(P, C, scal, scal_t, gi, sci, name)
        norm_stats(P, C, lambda k: x_sb[:, k, :], x_t, TB2, NT)
        for k in range(KC):
            P.op("dve", lambda e, k=k: e.scalar_tensor_tensor(out=tmp[:], in0=x_sb[:, k, :], scalar=a[:, k:k + 1],
                                                              in1=rstd[:], op0=ALU.mult, op1=ALU.mult),
                 reads=[x_t[k], at, rstd_t], writes=[tmp_t])
            P.op("act", lambda e, k=k: e.activation(out=hb[:, k, :], in_=tmp[:], func=AF.Identity, bias=sc(shi, k)),
                 reads=[tmp_t, scal_t], writes=[hb_t[k]])
    norm_to_hb(1, 2, 3, "a2")

    for fb in range(16):
        w1h, w1t = load_w_block(P, wpool, w1, KC, fb * 512, 512)
        w2h, w2t = wpool.get()
        w2v = w2h[:].rearrange("p (a b) n -> p a (b n)", a=4)
        P.dma("pool", lambda e, w2v=w2v, fb=fb: e.dma_start(
            out=w2v, in_=w2[fb * 512:(fb + 1) * 512, :].rearrange("(a p) n -> p a n", p=128)), writes=[w2t])
        ah, at_ = apool.get()
        for ci in range(4):
            for (t0, tn) in TB2:
                ph, pt = pp.get()

                def mm(e, w1h=w1h, ci=ci, t0=t0, tn=tn, ph=ph):
                    for k in range(KC):
                        ins = e.matmul(ph[:, 0:tn], lhsT=w1h[:, k, ci * 128:(ci + 1) * 128], rhs=hb[:, k, t0:t0 + tn],
                                       start=(k == 0), stop=(k == KC - 1))
                    return ins
                P.op("pe", mm, reads=[w1t] + hb_t, writes=[pt])
                rh, rt = rpool.get()
                P.op("act", lambda e, ph=ph, rh=rh, tn=tn: e.activation(out=rh[:, 0:tn], in_=ph[:, 0:tn], func=AF.Relu),
                     reads=[pt], writes=[rt])
                P.op("pool", lambda e, rh=rh, ah=ah, ci=ci, t0=t0, tn=tn: e.tensor_tensor(
                    out=ah[:, ci, t0:t0 + tn], in0=rh[:, 0:tn], in1=rh[:, 0:tn], op=ALU.mult), reads=[rt], writes=[at_])
        for oc in range(KC):
            for (t0, tn) in TB2:
                ph, pt = pp.get()

                def mm2(e, w2v=w2v, oc=oc, t0=t0, tn=tn, ph=ph, ah=ah):
                    for a in range(4):
                        ins = e.matmul(ph[:, 0:tn], lhsT=w2v[:, a, oc * 128:(oc + 1) * 128], rhs=ah[:, a, t0:t0 + tn],
                                       start=(a == 0), stop=(a == 3))
                    return ins
                P.op("pe", mm2, reads=[w2t, at_], writes=[pt])
                P.op("dve", lambda e, oc=oc, t0=t0, tn=tn, ph=ph: e.scalar_tensor_tensor(
                    out=x_sb[:, oc, t0:t0 + tn], in0=ph[:, 0:tn], scalar=sc(4, oc), in1=x_sb[:, oc, t0:t0 + tn],
                    op0=ALU.mult, op1=ALU.add), reads=[pt, scal_t, x_t[oc]], writes=[x_t[oc]])

    if not final:
        norm_to_hb(5, 6, 7, "a3")
        for k in range(KC):
            q = "sp"
            P.dma(q, lambda e, k=k: e.dma_start(out=h_out[k * 128:(k + 1) * 128, :], in_=hb[:, k, :]),
                  reads=[hb_t[k]], writes=[tout], semtile=hb_t[k])
            P.dma(q, lambda e, k=k: e.dma_start(out=x_out[k * 128:(k + 1) * 128, :], in_=x_sb[:, k, :]),
                  reads=[x_t[k]], writes=[tout], semtile=x_t[k])
        P.wait_all("sp", hb_t + x_t)
        P.wait_all("act", hb_t + x_t)
    else:
        norm_stats(P, C, lambda k: x_sb[:, k, :], x_t, TB2, NT)
        for k in range(KC):
            P.op("dve", lambda e, k=k: e.scalar_tensor_tensor(out=x_sb[:, k, :], in0=x_sb[:, k, :], scalar=sc(5, k),
                                                              in1=rstd[:], op0=ALU.mult, op1=ALU.mult),
                 reads=[x_t[k], scal_t, rstd_t], writes=[x_t[k]])
            q = "sp"
            P.dma(q, lambda e, k=k: e.dma_start(out=x_out[k * 128:(k + 1) * 128, :], in_=x_sb[:, k, :]),
                  reads=[x_t[k]], writes=[tout], semtile=x_t[k])
        P.wait_all("sp", x_t)
        P.wait_all("act", x_t)


def build_tok(final):
    nc = bass.Bass("TRN2", target_bir_lowering=False)
    mT = dram_in(nc, "i_mT", [D, NT], BF16)
    xT = dram_in(nc, "i_xT", [D, NT])
    scal_d = dram_in(nc, "i_scal", [128, 8 * KC])
    w_out = dram_in(nc, "i_wout", [D, D])
    w1 = dram_in(nc, "i_w1", [D, 4 * D])
    w2 = dram_in(nc, "i_w2", [4 * D, D])
    x_out = dram_out(nc, "o_x", [D, NT])
    h_out = None if final else dram_out(nc, "o_h", [D, NT], BF16)
    P = Prog(nc)
    emit_tok(P, nc, mT, xT, scal_d, w_out, w1, w2, x_out, h_out, final)
    P.emit()
    P.close()
    return nc


NTT = 2304
NCH = 36
TB5 = [(0, 512), (512, 512), (1024, 512), (1536, 512), (2048, 256)]
SEQ_F = [32, 33, 34, 35] + list(range(32))
SEQ_B = [35, 34, 33, 32] + list(range(31, -1, -1))


def mix0_consts():
    s = np.arange(64)
    pm = np.zeros((64, 2, 66), np.float32)
    pm[:, 0, :64] = (s[:, None] <= s[None, :]).astype(np.float32) - (s[:, None] <= 31).astype(np.float32)
    pm[:, 0, 64] = (s >= 32)
    pm[:, 0, 65] = (s <= 31)
    pm[:, 1, :64] = (s[:, None] >= s[None, :]).astype(np.float32) - (s[:, None] >= 32).astype(np.float32)
    pm[:, 1, 64] = (s <= 31)
    pm[:, 1, 65] = (s >= 32)
    mk = np.zeros((64, 2, 64), np.float32)
    mk[:, 0] = (s[:, None] <= s[None, :])
    mk[:, 1] = (s[:, None] >= s[None, :])
    return np.concatenate([pm.reshape(64, 132), mk.reshape(64, 128), np.eye(64, dtype=np.float32)], axis=1)


def emit_mix0(P, nc, xT, scal_d, w, lbr, cw_d, gn_d, cst_d, m_out, upto=99, mrow=(0, 512), h_save=None, h_load=None):
    gp_eq = Pool(P, "g_eq", 1, [128, 256], F32)
    hT = P.sbuf("hT", [128, KC, NTT], BF16)
    hT_t = TL("hT", KC)
    scal = P.sbuf("scal", [128, 5 * KC], F32)
    ones = P.sbuf("ones", [128, 128], BF16)
    epsb = P.sbuf("epsb", [128, 1], F32)
    lean = h_load is not None
    if not lean:
        rstd = P.sbuf("rstd", [128, NTT], F32)
        xpool = Pool(P, "xin", 2, [128, NTT], F32)
        sqpool = Pool(P, "sq", 1, [128, NTT], BF16)
        tmpp = Pool(P, "tmpn", 1, [128, NTT], F32)
    else:
        rstd = xpool = None
        sqpool = Pool(P, "sq", 1, [128, 512], BF16)
        tmpp = Pool(P, "tmpn", 1, [128, 512], F32)
    pp = Pool(P, "pp", 8, [128, 512], F32, psum=True)
    scal_t, ones_t, eps_t, rstd_t = T("scal"), T("ones"), T("epsb"), T("rstd")
    stat = [pp.bufs[i][0] for i in range(5)]
    stat_t = [pp.bufs[i][1] for i in range(5)]
    C = dict(ones=ones, sqpool=sqpool, stat=stat, stat_t=stat_t, rstd=rstd, rstd_t=rstd_t, eps=epsb, eps_t=eps_t, ones_t=ones_t)

    P.op("pool", lambda e: e.memset(ones[:], 1.0), writes=[ones_t])
    P.op("pool", lambda e: e.memset(epsb[:], EPS), writes=[eps_t])
    if h_load is None:
        load_scal(P, scal, scal_t, scal_d)

    xk_t = TL("xk", KC)
    xbuf = {}

    def load_x(k):
        xh, xt = xpool.get()
        q = "sp"
        P.dma(q, lambda e, xh=xh, k=k: e.dma_start(out=xh[:], in_=xT[k * 128:(k + 1) * 128, :]), writes=[xt])
        xbuf[k] = (xh, xt)

    hsv_t = []
    if h_load is not None:
        for k in range(KC):
            q = "sp"
            P.dma(q, lambda e, k=k: e.dma_start(out=hT[:, k, :], in_=h_load[k * 128:(k + 1) * 128, :]), writes=[hT_t[k]])
    else:
        norm_stats(P, C, lambda k: xbuf[k][0][:], lambda k: xbuf[k][1], TB5, NTT, pre=load_x)
        a_l, a_lt = mod_coef(P, C, scal, scal_t, 0, 1, "a_l")
        a_c, a_ct = mod_coef(P, C, scal, scal_t, 0, 3, "a_c")
        for k in range(KC):
            load_x(k)
            xh, xt = xbuf[k]
            th, tt = tmpp.get()
            for (lo, hi, a, at, shi) in ((0, 2048, a_l, a_lt, 2), (2048, NTT, a_c, a_ct, 4)):
                P.op("dve", lambda e, xh=xh, th=th, lo=lo, hi=hi, a=a, k=k: e.scalar_tensor_tensor(
                    out=th[:, lo:hi], in0=xh[:, lo:hi], scalar=a[:, k:k + 1], in1=rstd[:, lo:hi], op0=ALU.mult, op1=ALU.mult),
                    reads=[xt, at, rstd_t], writes=[tt])
                P.op("act", lambda e, th=th, lo=lo, hi=hi, k=k, shi=shi: e.activation(
                    out=hT[:, k, lo:hi], in_=th[:, lo:hi], func=AF.Identity, bias=scal[:, shi * KC + k:shi * KC + k + 1]),
                    reads=[tt, scal_t], writes=[hT_t[k]])
        if h_save is not None:
            for k in range(KC):
                q = "sp"
                hsv_t.append(T("hsv%d" % k))
                P.dma(q, lambda e, k=k: e.dma_start(out=h_save[k * 128:(k + 1) * 128, :], in_=hT[:, k, :]), reads=[hT_t[k]], writes=[hsv_t[-1]], semtile=hT_t[k])
    if upto == 0:
        for k in range(8):
            P.dma("sp", lambda e, k=k: e.dma_start(out=m_out[k * 128:(k + 1) * 128, :], in_=hT[:, k, 0:2048]), reads=[hT_t[k]], writes=[T("dbg")], semtile=hT_t[k])
        P.wait_all("sp", hT_t)
        return
    cst = P.sbuf("cst", [64, 324], F32)
    cst_t = T("cst")
    P.dma("sp", lambda e: e.dma_start(out=cst[:], in_=cst_d), writes=[cst_t])
    pm = cst[:, 0:132].rearrange("p (d n) -> p d n", d=2)
    mk = cst[:, 132:260].rearrange("p (d n) -> p d n", d=2)
    idb = P.sbuf("idb", [64, 64], BF16)
    idb_t = T("idb")
    P.op("dve", lambda e: e.tensor_copy(out=idb[:], in_=cst[:, 260:324]), reads=[cst_t], writes=[idb_t])
    cw = P.sbuf("cw", [128, 12], F32)
    gn = P.sbuf("gn", [128, 4], F32)
    cw_t, gn_t = T("cw"), T("gn")
    P.dma("sp", lambda e: e.dma_start(out=cw[:], in_=cw_d), writes=[cw_t])
    P.dma("sp", lambda e: e.dma_start(out=gn[:], in_=gn_d), writes=[gn_t])
    if not lean:
        wtm_p = Pool(P, "wtm", 1, [128, KC, 384], BF16)
        wfm_view, wfm_t = rstd[:].bitcast(BF16)[:, 0:KC * 256].rearrange("p (k n) -> p k n", k=KC), rstd_t
        qT = xpool.bufs[0][0][:, 0:2048]
        sgT = xpool.bufs[1][0][:, 0:2048]
    else:
        wtm_p = Pool(P, "wtm", 2, [128, KC, 384], BF16)
        wfm_view, wfm_t = P.sbuf("wfm", [128, KC, 256], BF16), T("wfm")
        qT = P.sbuf("qT", [128, 2048], F32)
        sgT = P.sbuf("sgT", [128, 2048], F32)
    prefetch = lean
    vtok = P.sbuf("vtok", [64, NCH, 128], BF16)
    keT = P.sbuf("keT", [64, 2, NCH, 128], BF16)
    keF = P.sbuf("keF", [128, 2, 32, 64], BF16)
    qe = P.sbuf("qe", [128, 2, 32, 64], BF16)
    ABt = P.sbuf("ABt", [128, 2, NCH, 2], F32)
    Gs = P.sbuf("Gs", [128, 2, NCH], F32)
    oT = P.sbuf("oT", [128, 2048], F32)
    Tst = [P.sbuf("Tst%d" % d, [128, 128], F32) for d in range(2)]
    sbf_p = [Pool(P, "sbf%d" % d, 2, [128, 128], BF16) for d in range(2)]
    att_p = [Pool(P, "attm%d" % d, 2, [64, 64], BF16) for d in range(2)]
    for d in range(2):
        for (ah_, at__) in att_p[d].bufs:
            P.op("pool", lambda e, ah_=ah_: e.memset(ah_[:, :], 0.0), writes=[at__])
    gp_sig = Pool(P, "g_sig", 1, [128, 512], F32)
    gp_t1 = Pool(P, "g_t1", 1, [128, 512], F32)
    gp_lf = Pool(P, "g_lf", 2, [128, 512], F32)
    gp_omf = Pool(P, "g_omf", 1, [128, 512], F32)
    gp_ee = Pool(P, "g_ee", 1, [128, 512], F32)
    mo_p = Pool(P, "mo", 1, [128, 2048], BF16)
    qT_t, sgT_t = TL("qT", 4), TL("sgT", 4)
    vtok_t = TL("vtok", NCH)
    keT_t = [TL("keT%d_" % d, NCH) for d in range(2)]
    keF_t = [TL("keF%d_" % d, 32) for d in range(2)]
    qe_t = [TL("qe%d_" % d, 32) for d in range(2)]
    AB_t, Gs_t = T("ABt"), T("Gs")
    oT_t = TL("oT", 32)
    Tst_t = TL("Tst", 2)
    mout_t = T("mout")

    lb = P.sbuf("lb", [64, 2, 512], F32)
    oml = P.sbuf("oml", [64, 2, 512], F32)
    lbl_t, lb_t, oml_t, lsm_t = T("lbl"), T("lb"), T("oml"), T("lsm")
    lbl = keF[:].rearrange("p a b c -> p (a b c)").bitcast(F32)[0:64, 0:1536].rearrange("p (s n) -> p s n", s=3)
    lsm = qe[:].rearrange("p a b c -> p (a b c)").bitcast(F32)[0:64, 0:512]
    for d in range(2):
        P.dma("sp", lambda e, d=d: e.dma_start(out=lbl, in_=lbr[:, d * 1536:(d + 1) * 1536].rearrange("p (s n) -> p s n", s=3)), writes=[lbl_t])
        P.op("dve", lambda e: e.tensor_tensor(out=lsm, in0=lbl[:, 0, :], in1=lbl[:, 1, :], op=ALU.max), reads=[lbl_t], writes=[lsm_t])
        P.op("dve", lambda e: e.tensor_tensor(out=lsm, in0=lsm, in1=lbl[:, 2, :], op=ALU.max), reads=[lbl_t, lsm_t], writes=[lsm_t])
        P.op("dve", lambda e: e.tensor_tensor(out=lbl, in0=lbl, in1=lsm.unsqueeze(1).to_broadcast([64, 3, 512]),
                                              op=ALU.subtract), reads=[lbl_t, lsm_t], writes=[lbl_t])
        P.op("act", lambda e: e.activation(out=lbl, in_=lbl, func=AF.Exp), reads=[lbl_t], writes=[lbl_t])
        P.op("dve", lambda e: e.tensor_tensor(out=lsm, in0=lbl[:, 0, :], in1=lbl[:, 1, :], op=ALU.add), reads=[lbl_t], writes=[lsm_t])
        P.op("dve", lambda e: e.tensor_tensor(out=lsm, in0=lsm, in1=lbl[:, 2, :], op=ALU.add), reads=[lbl_t, lsm_t], writes=[lsm_t])
        P.op("dve", lambda e: e.reciprocal(out=lsm, in_=lsm), reads=[lsm_t], writes=[lsm_t])
        P.op("dve", lambda e, d=d: e.tensor_tensor(out=lb[:, d, :], in0=lbl[:, 0, :], in1=lsm, op=ALU.mult), reads=[lbl_t, lsm_t], writes=[lb_t])
    P.op("dve", lambda e: e.tensor_scalar(out=oml[:], in0=lb[:], scalar1=-1.0, scalar2=1.0, op0=ALU.mult, op1=ALU.add),
         reads=[lb_t], writes=[oml_t])

    if upto == 0.5:
        P.wait_all("sp", [lb_t, oml_t, idb_t, cw_t, gn_t, cst_t])
        return

    def load_wg(dst, dst_t, g0, ng, hd):
        for gi in range(ng):
            c0 = (g0 + gi) * 512 + hd * 128
            P.dma("pool", lambda e, gi=gi, c0=c0: e.dma_start(
                out=dst[:, :, gi * 128:(gi + 1) * 128], in_=w[:, c0:c0 + 128].rearrange("(k p) n -> p k n", p=128)), writes=[dst_t])

    wseq = [(0, 3, i) for i in range(4)] + [(5, 3, i) for i in range(4)]
    wtiles = {}

    def issue_w(i):
        wt_h, wt_t = wtm_p.get()
        load_wg(wt_h, wt_t, *wseq[i])
        wtiles[i] = (wt_h, wt_t)
    wf_h, wf_t = wfm_view, wfm_t
    if prefetch:
        issue_w(0)
        load_wg(wf_h, wf_t, 3, 2, 0)
    for hd in range(4):
        if not prefetch:
            issue_w(hd)
            load_wg(wf_h, wf_t, 3, 2, hd)
        wt_h, wt_t = wtiles[hd]
        for tb in range(4):
            for gi in range(2):
                ph, pt = pp.get()

                def mm(e, ph=ph, gi=gi, tb=tb, wf_h=wf_h):
                    for k in range(KC):
                        ins = e.matmul(ph[:, :], lhsT=wf_h[:, k, gi * 128:(gi + 1) * 128], rhs=hT[:, k, tb * 512:(tb + 1) * 512],
                                       start=(k == 0), stop=(k == KC - 1))
                    return ins
                P.op("pe", mm, reads=[wf_t] + hT_t, writes=[pt])
                if gi == 0:
                    P.op("dve", lambda e, ph=ph, tb=tb: e.tensor_copy(out=qT[:, tb * 512:(tb + 1) * 512], in_=ph[:, :]),
                         reads=[pt], writes=[qT_t[tb]])
                else:
                    P.op("act", lambda e, ph=ph, tb=tb: e.activation(out=sgT[:, tb * 512:(tb + 1) * 512], in_=ph[:, :], func=AF.Silu),
                         reads=[pt], writes=[sgT_t[tb]])
        if upto == 0.7:
            P.wait_all("sp", qT_t + sgT_t)
            return
        if prefetch:
            issue_w(hd + 1)
            if hd + 1 < 4:
                load_wg(wf_h, wf_t, 3, 2, hd + 1)
        P.op("pool", lambda e: e.memset(oT[:], 0.0), writes=oT_t)
        P.op("pool", lambda e: e.memset(Gs[:], 0.0), writes=[Gs_t])
        lb_h = lb[:, :, hd * 128:(hd + 1) * 128].unsqueeze(2).to_broadcast([64, 2, 2, 128])
        oml_h = oml[:, :, hd * 128:(hd + 1) * 128].unsqueeze(2).to_broadcast([64, 2, 2, 128])

        def v4(h):
            return h[0:64, :].rearrange("p (c d n) -> p d c n", d=2, c=2)

        for j in range(18):
            c0 = 2 * j
            pa, pa_t = pp.get()
            pb, pb_t = pp.get()

            def mm(e, pa=pa, pb=pb, c0=c0, wt_h=wt_h):
                for ci in range(2):
                    tok0 = (c0 + ci) * 64
                    for k in range(KC):
                        e.matmul(pa[0:64, ci * 256:(ci + 1) * 256], lhsT=hT[:, k, tok0:tok0 + 64], rhs=wt_h[:, k, 0:256],
                                 start=(k == 0), stop=(k == KC - 1))
                    for k in range(KC):
                        ins = e.matmul(pb[0:64, ci * 128:(ci + 1) * 128], lhsT=hT[:, k, tok0:tok0 + 64], rhs=wt_h[:, k, 256:384],
                                       start=(k == 0), stop=(k == KC - 1))
                return ins
            P.op("pe", mm, reads=[wt_t] + hT_t, writes=[pa_t, pb_t])
            P.op("act", lambda e, pb=pb, c0=c0: e.activation(out=vtok[:, c0:c0 + 2, :], in_=pb[0:64, 0:256].rearrange("p (c n) -> p c n", c=2),
                                                             func=AF.Copy), reads=[pb_t], writes=vtok_t[c0:c0 + 2])
            sg_h, sg_t = gp_sig.get()
            P.op("act", lambda e, pa=pa, sg_h=sg_h: e.activation(out=sg_h[0:64, :], in_=pa[0:64, :], func=AF.Sigmoid), reads=[pa_t], writes=[sg_t])
            t1_h, t1_t = gp_t1.get()
            P.op("dve", lambda e, sg_h=sg_h, t1_h=t1_h, oml_h=oml_h: e.tensor_tensor(out=v4(t1_h), in0=v4(sg_h), in1=oml_h, op=ALU.mult),
                 reads=[sg_t, oml_t], writes=[t1_t])
            P.op("pool", lambda e, t1_h=t1_h, lb_h=lb_h: e.tensor_tensor(out=v4(t1_h), in0=v4(t1_h), in1=lb_h, op=ALU.add),
                 reads=[t1_t, lb_t], writes=[t1_t])
            lf_h, lf_t = gp_lf.get()
            P.op("act", lambda e, t1_h=t1_h, lf_h=lf_h: e.activation(out=lf_h[0:64, :], in_=t1_h[0:64, :], func=AF.Ln), reads=[t1_t], writes=[lf_t])
            om_h, om_t = gp_omf.get()
            P.op("dve", lambda e, t1_h=t1_h, om_h=om_h: e.tensor_scalar(out=om_h[0:64, :], in0=t1_h[0:64, :], scalar1=-1.0, scalar2=1.0,
                                                                        op0=ALU.mult, op1=ALU.add), reads=[t1_t], writes=[om_t])
            if upto == 0.8:
                P.wait_all("sp", [om_t, lf_t] + vtok_t[0:2])
                return
            pe2, pe2_t = pp.get()

            def mm2(e, pe2=pe2, lf_h=lf_h):
                for d in range(2):
                    for ci in range(2):
                        o0 = ci * 256 + d * 128
                        ins = e.matmul(pe2[0:64, o0:o0 + 128], lhsT=pm[:, d, 0:64], rhs=lf_h[0:64, o0:o0 + 128], start=True, stop=True)
                return ins
            P.op("pe", mm2, reads=[cst_t, lf_t], writes=[pe2_t])
            ee_h, ee_t = gp_ee.get()
            P.op("act", lambda e, pe2=pe2, ee_h=ee_h: e.activation(out=ee_h[0:64, :], in_=pe2[0:64, :], func=AF.Exp, scale=-1.0), reads=[pe2_t], writes=[ee_t])
            P.op("dve", lambda e, ee_h=ee_h, om_h=om_h, c0=c0: e.tensor_tensor(out=keT[:, :, c0:c0 + 2, :], in0=v4(ee_h), in1=v4(om_h), op=ALU.mult),
                 reads=[ee_t, om_t], writes=keT_t[0][c0:c0 + 2] + keT_t[1][c0:c0 + 2])
            pe1, pe1_t = pp.get()

            def mm1(e, pe1=pe1, lf_h=lf_h):
                for d in range(2):
                    for ci in range(2):
                        o0 = (d * 2 + ci) * 128
                        ins = e.matmul(pe1[:, o0:o0 + 66], lhsT=v4(lf_h)[:, d, ci, :], rhs=pm[:, d, :], start=True, stop=True)
                return ins
            P.op("pe", mm1, reads=[cst_t, lf_t], writes=[pe1_t])
            pe1v = pe1[:, 0:512].rearrange("p (d c n) -> p d c n", d=2, c=2)
            P.op("dve", lambda e, pe1v=pe1v, c0=c0: e.tensor_copy(out=ABt[:, :, c0:c0 + 2, :], in_=pe1v[:, :, :, 64:66]), reads=[pe1_t], writes=[AB_t])
            if upto == 0.9:
                P.wait_all("sp", [AB_t] + keT_t[0][0:2])
                return
            if j < 16:
                eq_h, eq_t = gp_eq.get()
                eqv = eq_h[:, :].rearrange("p (d c n) -> p d c n", d=2, c=2)
                P.op("dve", lambda e, pe1v=pe1v, eqv=eqv: e.tensor_copy(out=eqv, in_=pe1v[:, :, :, 0:64]), reads=[pe1_t], writes=[eq_t])
                P.op("act", lambda e, eq_h=eq_h: e.activation(out=eq_h[:, :], in_=eq_h[:, :], func=AF.Exp), reads=[eq_t], writes=[eq_t])
                if upto in (0.93, 0.931, 0.932, 0.933):
                    mo_h, mo_t = mo_p.get()
                    P.op("dve", lambda e, mo_h=mo_h, eq_h=eq_h: e.tensor_copy(out=mo_h[:, 0:256], in_=eq_h[:, :]), reads=[eq_t], writes=[mo_t])
                    P.op("dve", lambda e, mo_h=mo_h, lf_h=lf_h: e.tensor_copy(out=mo_h[0:64, 256:768], in_=lf_h[0:64, :]), reads=[lf_t], writes=[mo_t])
                    P.op("dve", lambda e, mo_h=mo_h: e.tensor_copy(out=mo_h[0:64, 768:1280].rearrange("p (d n) -> p d n", d=2), in_=lb[:, :, 0:256]), reads=[lb_t], writes=[mo_t])
                    P.dma("sp", lambda e, mo_h=mo_h: e.dma_start(out=m_out[0:128, 0:2048], in_=mo_h[:, :]), reads=[mo_t], writes=[T("dbg")], semtile=mo_t)
                    P.wait_all("sp", [mo_t])
                    return
                qv = qT[:, c0 * 64:(c0 + 2) * 64].rearrange("p (c n) -> p c n", c=2)
                for d in range(2):
                    P.op("dve", lambda e, eqv=eqv, qv=qv, c0=c0, d=d: e.tensor_tensor(out=qe[:, d, c0:c0 + 2, :], in0=eqv[:, d, :, :], in1=qv, op=ALU.mult),
                         reads=[eq_t, qT_t[j // 4]], writes=qe_t[d][c0:c0 + 2])
                if upto == 0.95:
                    P.wait_all("sp", qe_t[0][0:2])
                    return
                pk, pk_t = pp.get()

                def mmk(e, pk=pk, c0=c0):
                    for d in range(2):
                        for ci in range(2):
                            o0 = (d * 2 + ci) * 64
                            ins = e.matmul(pk[:, o0:o0 + 64], lhsT=keT[:, d, c0 + ci, :], rhs=idb[:, :], start=True, stop=True)
                    return ins
                P.op("pe", mmk, reads=[idb_t] + keT_t[0][c0:c0 + 2] + keT_t[1][c0:c0 + 2], writes=[pk_t])
                for d in range(2):
                    P.op("act", lambda e, pk=pk, c0=c0, d=d: e.activation(out=keF[:, d, c0:c0 + 2, :], in_=pk[:, d * 128:(d + 1) * 128].rearrange("p (c n) -> p c n", c=2),
                                                                      func=AF.Copy), reads=[pk_t], writes=keF_t[d][c0:c0 + 2])
        if upto == 1:
            dt_ = T("dbg")
            allt = qe_t[0] + qe_t[1] + keF_t[0] + keF_t[1]
            for i_, (src, d) in enumerate(((qe, 0), (keF, 0), (qe, 1), (keF, 1))):
                P.dma("sp", lambda e, src=src, d=d, i_=i_: e.dma_start(out=m_out[i_ * 128:(i_ + 1) * 128, :], in_=src[:, d, :, :].rearrange("p c n -> p (c n)")),
                      reads=allt, writes=[dt_], semtile=dt_)
            P.wait_all("sp", [dt_])
            return
        A_ = ABt[:, :, :, 0]
        B_ = ABt[:, :, :, 1]
        for (d, o0, o1, b0) in ((0, 32, 35, 33), (0, 35, 36, 0), (0, 0, 31, 1), (1, 33, 36, 32), (1, 32, 33, 31), (1, 1, 32, 0)):
            n = o1 - o0
            P.op("dve", lambda e, d=d, o0=o0, o1=o1, b0=b0, n=n: e.tensor_tensor(out=Gs[:, d, o0:o1], in0=A_[:, d, o0:o1], in1=B_[:, d, b0:b0 + n], op=ALU.add),
                 reads=[AB_t], writes=[Gs_t])
        P.op("act", lambda e: e.activation(out=Gs[:], in_=Gs[:], func=AF.Exp), reads=[Gs_t], writes=[Gs_t])
        seqs = (SEQ_F, SEQ_B)
        pend = []

        def stage1(i, d):
            c = seqs[d][i]
            prev = seqs[d][i - 1] if i > 0 else None
            pb_, pb_t = pp.get()
            info = None
            if c < 32:
                if d == 0:
                    blks = ((slice(0, 64), slice(32, 64)), (slice(0, 32), slice(0, 32)))
                else:
                    blks = ((slice(0, 64), slice(0, 32)), (slice(32, 64), slice(32, 64)))

                def mm1(e, pb_=pb_, d=d, c=c, blks=blks):
                    e.matmul(pb_[:, 0:128], lhsT=keT[:, d, c, :], rhs=vtok[:, c, :], start=True, stop=True)
                    for (ss, tt_) in blks:
                        ins = e.matmul(pb_[ss, 128 + tt_.start:128 + tt_.stop], lhsT=keF[:, d, c, ss], rhs=qe[:, d, c, tt_], start=True, stop=True)
                    return ins
                P.op("pe", mm1, reads=[keT_t[d][c], vtok_t[c], keF_t[d][c], qe_t[d][c]], writes=[pb_t])
                sb_h, sb_t = sbf_p[d].get()
                P.op("act", lambda e, sb_h=sb_h, d=d, prev=prev: e.activation(out=sb_h[:, :], in_=Tst[d][:, :], func=AF.Copy, scale=Gs[:, d, prev:prev + 1]),
                     reads=[Tst_t[d], Gs_t], writes=[sb_t])
                am_h, am_t = att_p[d].get()
                for (ss, tt_) in blks:
                    P.op("dve", lambda e, pb_=pb_, am_h=am_h, d=d, ss=ss, tt_=tt_: e.tensor_tensor(
                        out=am_h[ss, tt_], in0=pb_[ss, 128 + tt_.start:128 + tt_.stop], in1=mk[ss, d, tt_], op=ALU.mult), reads=[pb_t, cst_t], writes=[am_t])
                info = (pb_, pb_t, sb_h, sb_t, am_h, am_t, d, c)
            else:
                P.op("pe", lambda e, pb_=pb_, d=d, c=c: e.matmul(pb_[:, 0:128], lhsT=keT[:, d, c, :], rhs=vtok[:, c, :], start=True, stop=True),
                     reads=[keT_t[d][c], vtok_t[c]], writes=[pb_t])
            if i == 0:
                P.op("dve", lambda e, pb_=pb_, d=d: e.tensor_copy(out=Tst[d][:, :], in_=pb_[:, 0:128]), reads=[pb_t], writes=[Tst_t[d]])
            else:
                P.op("dve", lambda e, pb_=pb_, d=d, prev=prev: e.scalar_tensor_tensor(
                    out=Tst[d][:, :], in0=Tst[d][:, :], scalar=Gs[:, d, prev:prev + 1], in1=pb_[:, 0:128], op0=ALU.mult, op1=ALU.add),
                    reads=[pb_t, Tst_t[d], Gs_t], writes=[Tst_t[d]])
            return info

        def stage2(info):
            pb_, pb_t, sb_h, sb_t, am_h, am_t, d, c = info

            def mmo(e):
                e.matmul(pb_[:, 192:256], lhsT=sb_h[:, :], rhs=qe[:, d, c, :], start=True, stop=False)
                return e.matmul(pb_[:, 192:256], lhsT=vtok[:, c, :], rhs=am_h[:, :], start=False, stop=True)
            P.op("pe", mmo, reads=[sb_t, qe_t[d][c], vtok_t[c], am_t], writes=[pb_t])
            P.op("dve", lambda e: e.tensor_tensor(out=oT[:, c * 64:(c + 1) * 64], in0=pb_[:, 192:256], in1=oT[:, c * 64:(c + 1) * 64], op=ALU.add),
                 reads=[pb_t, oT_t[c]], writes=[oT_t[c]])

        for i in range(NCH):
            cur = [stage1(i, d) for d in range(2)]
            for info in pend:
                if info is not None:
                    stage2(info)
            pend = cur
        for info in pend:
            if info is not None:
                stage2(info)
        if upto == 2:
            mo_h, mo_t = mo_p.get()
            P.op("act", lambda e, mo_h=mo_h: e.activation(out=mo_h[:, :], in_=oT[:, :], func=AF.Copy), reads=oT_t, writes=[mo_t])
            P.dma("sp", lambda e, mo_h=mo_h: e.dma_start(out=m_out[0:128, :], in_=mo_h[:, :]), reads=[mo_t], writes=[T("dbg")], semtile=mo_t)
            P.wait_all("sp", [mo_t])
            return
        mo_h, mo_t = mo_p.get()
        for tb in range(4):
            sl = slice(tb * 512, (tb + 1) * 512)
            sq, sqt = sqpool.get()
            P.op("act", lambda e, sq=sq, sl=sl: e.activation(out=sq[:, 0:512], in_=oT[:, sl], func=AF.Square), reads=oT_t[tb * 8:(tb + 1) * 8], writes=[sqt])
            ph, pt = pp.get()
            P.op("pe", lambda e, ph=ph, sq=sq: e.matmul(ph[:, :], lhsT=ones[:], rhs=sq[:, 0:512], start=True, stop=True), reads=[sqt, ones_t], writes=[pt])
            th, tt = tmpp.get()
            P.op("act", lambda e, ph=ph, th=th: e.activation(out=th[:, 0:512], in_=ph[:, :], func=AF.Sqrt, scale=1.0 / 128, bias=epsb[:, 0:1]),
                 reads=[pt, eps_t], writes=[tt])
            P.op("dve", lambda e, th=th: e.reciprocal(out=th[:, 0:512], in_=th[:, 0:512]), reads=[tt], writes=[tt])
            P.op("dve", lambda e, th=th, sl=sl, hd=hd: e.scalar_tensor_tensor(out=th[:, 0:512], in0=oT[:, sl], scalar=gn[:, hd:hd + 1], in1=th[:, 0:512],
                                                                              op0=ALU.mult, op1=ALU.mult), reads=[tt, gn_t] + oT_t[tb * 8:(tb + 1) * 8], writes=[tt])
            P.op("pool", lambda e, th=th, sl=sl, mo_h=mo_h: e.tensor_tensor(out=mo_h[:, sl], in0=th[:, 0:512], in1=sgT[:, sl], op=ALU.mult),
                 reads=[tt, sgT_t[tb]], writes=[mo_t])
        P.dma("sp", lambda e, mo_h=mo_h, hd=hd: e.dma_start(out=m_out[mrow[0] + hd * 128:mrow[0] + (hd + 1) * 128, :], in_=mo_h[:, :]), reads=[mo_t], writes=[mout_t], semtile=mo_t)

    if upto == 3:
        P.wait_all("sp", [b[1] for b in mo_p.bufs])
        return
    cv_g, cv_t, cv_a = gp_sig, gp_t1, gp_lf
    for cc in range(4):
        if not prefetch:
            issue_w(4 + cc)
        elif cc + 1 < 4:
            issue_w(4 + cc + 1)
        wt_h, wt_t = wtiles[4 + cc]
        mo_h, mo_t = mo_p.get()
        for tb in range(4):
            sl = slice(tb * 512, (tb + 1) * 512)
            pss = []
            for gi in range(3):
                ph, pt = pp.get()

                def mm(e, ph=ph, gi=gi, sl=sl, wt_h=wt_h):
                    for k in range(KC):
                        ins = e.matmul(ph[:, :], lhsT=wt_h[:, k, gi * 128:(gi + 1) * 128], rhs=hT[:, k, sl], start=(k == 0), stop=(k == KC - 1))
                    return ins
                P.op("pe", mm, reads=[wt_t] + hT_t, writes=[pt])
                pss.append((ph, pt))
            (pu, pu_t), (pgb, pgb_t), (pgc, pgc_t) = pss
            g_h, g_t = cv_g.get()
            P.op("act", lambda e, pgc=pgc, g_h=g_h: e.activation(out=g_h[:, :], in_=pgc[:, :], func=AF.Copy), reads=[pgc_t], writes=[g_t])
            t_h, t_t = cv_t.get()
            P.op("dve", lambda e, pu=pu, g_h=g_h, t_h=t_h: e.tensor_tensor(out=t_h[:, :], in0=pu[:, :], in1=g_h[:, :], op=ALU.mult), reads=[pu_t, g_t], writes=[t_t])
            a_h, a_t = cv_a.get()
            P.op("act", lambda e, t_h=t_h, a_h=a_h, cc=cc: e.activation(out=a_h[:, :], in_=t_h[:, :], func=AF.Copy, scale=cw[:, cc * 3 + 1:cc * 3 + 2]),
                 reads=[t_t, cw_t], writes=[a_t])
            tv = t_h[:, :].rearrange("p (r j) -> p r j", j=64)
            av = a_h[:, :].rearrange("p (r j) -> p r j", j=64)
            P.op("dve", lambda e, tv=tv, av=av, cc=cc: e.scalar_tensor_tensor(out=av[:, :, 1:64], in0=tv[:, :, 0:63], scalar=cw[:, cc * 3:cc * 3 + 1], in1=av[:, :, 1:64],
                                                                              op0=ALU.mult, op1=ALU.add), reads=[t_t, a_t, cw_t], writes=[a_t])
            P.op("dve", lambda e, tv=tv, av=av, cc=cc: e.scalar_tensor_tensor(out=av[:, :, 0:63], in0=tv[:, :, 1:64], scalar=cw[:, cc * 3 + 2:cc * 3 + 3], in1=av[:, :, 0:63],
                                                                              op0=ALU.mult, op1=ALU.add), reads=[t_t, a_t, cw_t], writes=[a_t])
            P.op("dve", lambda e, pgb=pgb, a_h=a_h, mo_h=mo_h, sl=sl: e.tensor_tensor(out=mo_h[:, sl], in0=pgb[:, :], in1=a_h[:, :], op=ALU.mult),
                 reads=[pgb_t, a_t], writes=[mo_t])
        P.dma("sp", lambda e, mo_h=mo_h, cc=cc: e.dma_start(out=m_out[mrow[1] + cc * 128:mrow[1] + (cc + 1) * 128, :], in_=mo_h[:, :]),
              reads=[mo_t], writes=[mout_t], semtile=mo_t)
    P.wait_all("sp", [b[1] for b in mo_p.bufs] + hsv_t)
    P.wait_all("act", hsv_t)


def build_mix0(upto=99):
    nc = bass.Bass("TRN2", target_bir_lowering=False)
    xT = dram_in(nc, "i_xT", [D, NTT])
    scal_d = dram_in(nc, "i_scal", [128, 5 * KC])
    w = dram_in(nc, "i_w", [D, 4096])
    lbr = dram_in(nc, "i_lb", [64, 3072])
    cw_d = dram_in(nc, "i_cw", [128, 12])
    gn_d = dram_in(nc, "i_gn", [128, 4])
    cst_d = dram_in(nc, "i_cst", [64, 324])
    m_out = dram_out(nc, "o_m", [1024, 2048], BF16)
    P = Prog(nc)
    emit_mix0(P, nc, xT, scal_d, w, lbr, cw_d, gn_d, cst_d, m_out, upto)
    P.emit()
    P.close()
    return nc


L = 2048
NF = 4096


def hy_tables():
    t = np.arange(L, dtype=np.float64)
    k = np.arange(L, dtype=np.float64) + 0.5
    ang = 2.0 * np.pi * np.outer(t, k) / NF
    Cm, Sm = np.cos(ang), np.sin(ang)
    def fwd(M):
        return np.ascontiguousarray(M.reshape(16, 128, 16, 128).transpose(2, 1, 0, 3)).astype(NPBF)
    Ci = (2.0 / NF) * Cm.T.reshape(16, 128, 4, 512)
    Si = (2.0 / NF) * Sm.T.reshape(16, 128, 4, 512)
    inv = np.concatenate([Ci, Si], axis=0).transpose(2, 1, 0, 3)
    return fwd(Cm), fwd(Sm), np.ascontiguousarray(inv).astype(NPBF)


def hy_consts(hh):
    pos = np.arange(L, dtype=np.float32)
    t = np.linspace(0.0, 1.0, L, dtype=np.float32)
    w = (2.0 * np.pi * pos / L).astype(np.float32)
    bands = np.linspace(1e-4, 15.0, 16, dtype=np.float32)
    ang = w[:, None] * bands[None, :]
    zemb = np.concatenate([t[:, None], np.cos(ang), -np.sin(ang)], axis=-1).astype(np.float32)
    deltas = np.abs(np.linspace(np.log(1e-2) / 1.5, np.log(1e-2) / 0.3, D, dtype=np.float32))
    win = np.exp(-t[:, None] * deltas[None, hh * 1024:(hh + 1) * 1024]).astype(np.float32)
    return np.ascontiguousarray(zemb.T), win


def emit_hy(P, nc, hT_d, w, sw_d, skip_d, fw1_d, fw23_d, fvec_d, fw4_d, zemb_d, win_d, id_d, cf_d, sf_d, inv_d, z3_out, upto=99):
    PI = float(np.pi)
    x1s = nc.dram_tensor(P.pfx + "s_x1", [1024, L], BF16).ap()
    x2s = nc.dram_tensor(P.pfx + "s_x2", [1024, L], BF16).ap()
    zfs = [nc.dram_tensor(P.pfx + "s_zf%d" % i, [1024, L], BF16).ap() for i in range(2)]
    x1s_t, x2s_t = TL("sx1_", 8), TL("sx2_", 8)
    zfs_t = [TL("szf%d_" % i, 8) for i in range(2)]

    hT = P.sbuf("hT", [128, KC, L], BF16)
    hT_t = TL("hT", KC)
    Yv = hT[:].rearrange("p a (b n) -> p (a b) n", b=2)
    big = P.sbuf("big64", [128, 2, 16384], BF16)
    big_t = TL("big", 2)
    z_tm = P.sbuf("z_tm", [128, 16, 1024], BF16)
    ztm_t = TL("ztm", 16)
    arena = P.sbuf("arena", [128, 16384], BF16)

    def carve(off_kib, nbytes, dt, pat=None, **kw):
        a = arena[:, off_kib * 512:off_kib * 512 + nbytes // 2]
        if dt == F32:
            a = a.bitcast(F32)
        return a.rearrange(pat, **kw) if pat else a
    wpool = Pool.views("hw", [carve(0, 8192, BF16, "p (k n) -> p k n", k=KC), carve(8, 8192, BF16, "p (k n) -> p k n", k=KC)])
    pp = Pool(P, "pp", 8, [128, 512], F32, psum=True)
    idf = P.sbuf("idf", [128, 128], F32)
    idb = P.sbuf("idb", [128, 128], BF16)
    sw = P.sbuf("sw", [128, 72], F32)
    skip = P.sbuf("skip", [128, 16], F32)
    onesf = P.sbuf("onesf", [128, 1], F32)
    idf_t, idb_t, sw_t, skip_t, onesf_t = T("idf"), T("idb"), T("sw"), T("skip"), T("onesf")
    P.dma("sp", lambda e: e.dma_start(out=idf[:], in_=id_d), writes=[idf_t])
    P.op("dve", lambda e: e.tensor_copy(out=idb[:], in_=idf[:]), reads=[idf_t], writes=[idb_t])
    P.dma("sp", lambda e: e.dma_start(out=sw[:], in_=sw_d), writes=[sw_t])
    P.dma("sp", lambda e: e.dma_start(out=skip[:], in_=skip_d), writes=[skip_t])
    P.op("pool", lambda e: e.memset(onesf[:], 1.0), writes=[onesf_t])
    for k in range(KC):
        q = "sp"
        P.dma(q, lambda e, k=k: e.dma_start(out=hT[:, k, :], in_=hT_d[k * 128:(k + 1) * 128, :]), writes=[hT_t[k]])

    cvo = Pool.views("cvo", [carve(16, 4096, BF16), carve(20, 4096, BF16)])
    cva = Pool.views("cva", [carve(24, 2048, F32), carve(26, 2048, F32)])
    wnext = load_w_block(P, wpool, w, KC, 0, 256)
    for blk in range(12):
        wh, wt = wnext
        if blk + 1 < 12:
            wnext = load_w_block(P, wpool, w, KC, (blk + 1) * 256, 256)
        for ci in range(2):
            j = blk * 2 + ci
            grp, cc = j // 8, j % 8
            oh, ot = cvo.get()
            for tb in range(4):
                sl = slice(tb * 512, (tb + 1) * 512)
                ph, pt = pp.get()

                def mm(e, ph=ph, wh=wh, ci=ci, sl=sl):
                    for k in range(KC):
                        ins = e.matmul(ph[:, :], lhsT=wh[:, k, ci * 128:(ci + 1) * 128], rhs=hT[:, k, sl], start=(k == 0), stop=(k == KC - 1))
                    return ins
                P.op("pe", mm, reads=[wt] + hT_t, writes=[pt])
                ah, at = cva.get()
                P.op("act", lambda e, ph=ph, ah=ah, j=j: e.activation(out=ah[:, :], in_=ph[:, :], func=AF.Copy, scale=sw[:, j * 3 + 1:j * 3 + 2]),
                     reads=[pt, sw_t], writes=[at])
                pv = ph[:, :].rearrange("p (r c) -> p r c", c=64)
                av = ah[:, :].rearrange("p (r c) -> p r c", c=64)
                P.op("dve", lambda e, pv=pv, av=av, j=j: e.scalar_tensor_tensor(out=av[:, :, 1:64], in0=pv[:, :, 0:63], scalar=sw[:, j * 3:j * 3 + 1], in1=av[:, :, 1:64],
                                                                                op0=ALU.mult, op1=ALU.add), reads=[pt, at, sw_t], writes=[at])
                P.op("dve", lambda e, pv=pv, av=av, j=j: e.scalar_tensor_tensor(out=av[:, :, 0:63], in0=pv[:, :, 1:64], scalar=sw[:, j * 3 + 2:j * 3 + 3], in1=av[:, :, 0:63],
                                                                                op0=ALU.mult, op1=ALU.add), reads=[pt, at, sw_t], writes=[at])
                P.op("pool", lambda e, ah=ah, oh=oh, sl=sl: e.tensor_copy(out=oh[:, sl], in_=ah[:, :]), reads=[at], writes=[ot])
            dst, dst_t = ((x1s, x1s_t), (x2s, x2s_t), (zfs[0], zfs_t[0]))[grp]
            P.dma("sp", lambda e, oh=oh, dst=dst, cc=cc: e.dma_start(out=dst[cc * 128:(cc + 1) * 128, :], in_=oh[:, :]), reads=[ot], writes=[dst_t[cc]], semtile=ot)
            if grp == 2:
                for tq in range(4):
                    ph, pt = pp.get()

                    def mmt(e, ph=ph, oh=oh, tq=tq):
                        for i in range(4):
                            tc = tq * 4 + i
                            ins = e.matmul(ph[:, i * 128:(i + 1) * 128], lhsT=oh[:, tc * 128:(tc + 1) * 128], rhs=idb[:, :], start=True, stop=True)
                        return ins
                    P.op("pe", mmt, reads=[ot, idb_t], writes=[pt])
                    P.op("dve", lambda e, ph=ph, tq=tq, cc=cc: e.tensor_copy(out=z_tm[:, tq * 4:(tq + 1) * 4, cc * 128:(cc + 1) * 128],
                                                                             in_=ph[:, :].rearrange("p (i n) -> p i n", i=4)), reads=[pt], writes=ztm_t[tq * 4:(tq + 1) * 4])

    fw1 = P.sbuf("fw1", [33, 64], F32)
    fw23 = P.sbuf("fw23", [64, 128], F32)
    fvec = P.sbuf("fvec", [64, 5], F32)
    P.barrier()
    zemb = carve(0, 8192, F32)[0:33, :]
    hda = P.sbuf("hda", [64, L], F32)
    hdb = carve(8, 8192, F32)[0:64, :]
    fw1_t, fw23_t, fvec_t, fw4_t, zemb_t, hda_t, hdb_t = T("fw1"), T("fw23"), T("fvec"), T("fw4"), T("zemb"), T("hda"), T("hdb")
    ep_o = None
    P.dma("sp", lambda e: e.dma_start(out=fw1[:], in_=fw1_d), writes=[fw1_t])
    P.dma("sp", lambda e: e.dma_start(out=fw23[:], in_=fw23_d), writes=[fw23_t])
    P.dma("sp", lambda e: e.dma_start(out=fvec[:], in_=fvec_d), writes=[fvec_t])
    P.dma("sp", lambda e: e.dma_start(out=zemb, in_=zemb_d), writes=[zemb_t])
    f2pi = P.sbuf("f2pi", [64, 1], F32)
    f2pi_t, ri_t, rf_t = T("f2pi"), T("rint_i"), T("rint_f")
    P.op("dve", lambda e: e.tensor_scalar(out=f2pi[:, :], in0=fvec[:, 3:4], scalar1=1.0 / (2.0 * PI), scalar2=None, op0=ALU.mult), reads=[fvec_t], writes=[f2pi_t])
    rint_i = carve(16, 2048, F32)[0:64, :].bitcast(mybir.dt.int32)
    rint_f = carve(18, 2048, F32)[0:64, :]
    layers = ((fw1[:, :], fw1_t, zemb, zemb_t, 33, hda, hda_t, 0), (fw23[:, 0:64], fw23_t, hda, hda_t, 64, hdb, hdb_t, 1),
              (fw23[:, 64:128], fw23_t, hdb, hdb_t, 64, hda, hda_t, 2))
    for (wl, wl_t, src, src_t, kin, dst, dst_t, li) in layers:
        for tb in range(4):
            sl = slice(tb * 512, (tb + 1) * 512)
            ph, pt = pp.get()
            P.op("pe", lambda e, ph=ph, wl=wl, src=src, kin=kin, sl=sl: e.matmul(ph[0:64, :], lhsT=wl, rhs=src[0:kin, sl], start=True, stop=True),
                 reads=[wl_t, src_t], writes=[pt])
            P.op("dve", lambda e, ph=ph, dst=dst, sl=sl, li=li: e.tensor_scalar(out=dst[:, sl], in0=ph[0:64, :], scalar1=fvec[:, li:li + 1], scalar2=f2pi[:, 0:1],
                                                                               op0=ALU.add, op1=ALU.mult), reads=[pt, fvec_t, f2pi_t], writes=[dst_t])
            P.op("dve", lambda e, dst=dst, sl=sl: e.tensor_copy(out=rint_i[:, :], in_=dst[:, sl]), reads=[dst_t], writes=[ri_t])
            P.op("dve", lambda e: e.tensor_copy(out=rint_f[:, :], in_=rint_i[:, :]), reads=[ri_t], writes=[rf_t])
            P.op("dve", lambda e, dst=dst, sl=sl: e.tensor_tensor(out=dst[:, sl], in0=dst[:, sl], in1=rint_f[:, :], op=ALU.subtract), reads=[dst_t, rf_t], writes=[dst_t])
            P.op("act", lambda e, dst=dst, sl=sl: e.activation(out=dst[:, sl], in_=dst[:, sl], func=AF.Sin, scale=2.0 * PI * (1.0 - 1e-6)), reads=[dst_t], writes=[dst_t])
    hd3, hd3_t = hda, hda_t
    if upto == 0:
        dbg_t = T("dbg")
        P.barrier()
        th, tt = carve(16, 4096, BF16), T("dbgt")
        P.op("dve", lambda e, th=th: e.tensor_copy(out=th[0:64, :], in_=hd3[:, :]), reads=[hd3_t], writes=[tt])
        P.dma("sp", lambda e, th=th: e.dma_start(out=z3_out[0:64, :], in_=th[0:64, :]), reads=[tt], writes=[dbg_t], semtile=tt)
        P.wait_all("sp", [tt] + x1s_t + x2s_t + zfs_t[0])
        return

    acc_t = T("nacc")
    rnorm = P.sbuf("rnorm", [128, 16], F32)
    rnorm_t = T("rnorm")
    Av = big[:, 0, :].rearrange("p (c n) -> p c n", c=16)
    Bv = big[:, 1, :].rearrange("p (c n) -> p c n", c=16)
    out_t = T("z3out")
    for o in range(2):
        P.barrier()
        fw4 = carve(0, 8192, F32)
        P.dma("sp", lambda e, fw4=fw4, o=o: e.dma_start(out=fw4[0:64, :], in_=fw4_d[:, o * 2048:(o + 1) * 2048]), writes=[fw4_t])
        wn_p = Pool.views("wn%d" % o, [carve(8, 4096, F32), carve(12, 4096, F32)])
        fwv_p = Pool.views("fwv%d" % o, [carve(16, 4096, F32)])
        bwv_p = Pool.views("bwv%d" % o, [carve(20, 4096, F32)])
        abs_p = Pool.views("abs%d" % o, [carve(24, 4096, F32)])
        acc = carve(28, 4096, F32)
        for pc in range(16):
            wnh, wnt = wn_p.get()
            P.dma("sp", lambda e, wnh=wnh, pc=pc: e.dma_start(out=wnh[:, :], in_=win_d[pc * 128:(pc + 1) * 128, :]), writes=[wnt])
            vals = []
            for side, pool_ in ((0, fwv_p), (1, bwv_p)):
                vh, vt = pool_.get()
                for cb in range(2):
                    ph, pt = pp.get()
                    c0 = side * 1024 + cb * 512
                    P.op("pe", lambda e, ph=ph, pc=pc, c0=c0, fw4=fw4: e.matmul(ph[:, :], lhsT=hd3[:, pc * 128:(pc + 1) * 128], rhs=fw4[0:64, c0:c0 + 512], start=True, stop=True),
                         reads=[hd3_t, fw4_t], writes=[pt])
                    P.op("dve", lambda e, ph=ph, vh=vh, wnh=wnh, cb=cb: e.tensor_tensor(out=vh[:, cb * 512:(cb + 1) * 512], in0=ph[:, :], in1=wnh[:, cb * 512:(cb + 1) * 512], op=ALU.mult),
                         reads=[pt, wnt], writes=[vt])
                vals.append((vh, vt))
            (fh, ft), (bh, bt) = vals
            P.op("pool", lambda e, fh=fh, bh=bh, pc=pc: e.tensor_tensor(out=Av[:, pc, :], in0=fh[:, :], in1=bh[:, :], op=ALU.add), reads=[ft, bt], writes=[big_t[0]])
            P.op("pool", lambda e, fh=fh, bh=bh, pc=pc: e.tensor_tensor(out=Bv[:, pc, :], in0=fh[:, :], in1=bh[:, :], op=ALU.subtract), reads=[ft, bt], writes=[big_t[1]])
            abh, abt = abs_p.get()
            P.op("act", lambda e, fh=fh, abh=abh: e.activation(out=abh[:, :], in_=fh[:, :], func=AF.Abs), reads=[ft], writes=[abt])
            P.op("act", lambda e, bh=bh: e.activation(out=bh[:, :], in_=bh[:, :], func=AF.Abs), reads=[bt], writes=[bt])
            if pc == 0:
                P.op("dve", lambda e, fh=fh, bh=bh, abh=abh: e.tensor_tensor(out=abh[:, :], in0=abh[:, :], in1=bh[:, :], op=ALU.add), reads=[abt, bt], writes=[abt])
                P.op("act", lambda e, abh=abh: e.activation(out=abh[0:1, :], in_=Av[0:1, 0, :], func=AF.Abs), reads=[big_t[0], abt], writes=[abt])
                P.op("dve", lambda e, abh=abh: e.tensor_copy(out=acc[:, :], in_=abh[:, :]), reads=[abt], writes=[acc_t])
            else:
                P.op("dve", lambda e, fh=fh, bh=bh, abh=abh: e.tensor_tensor(out=abh[:, :], in0=abh[:, :], in1=bh[:, :], op=ALU.add), reads=[abt, bt], writes=[abt])
                P.op("dve", lambda e, abh=abh: e.tensor_tensor(out=acc[:, :], in0=acc[:, :], in1=abh[:, :], op=ALU.add), reads=[abt, acc_t], writes=[acc_t])
        ph, pt = pp.get()

        def mmn(e, ph=ph):
            for cc in range(8):
                ins = e.matmul(ph[:, cc:cc + 1], lhsT=acc[:, cc * 128:(cc + 1) * 128], rhs=onesf[:, 0:1], start=True, stop=True)
            return ins
        P.op("pe", mmn, reads=[acc_t, onesf_t], writes=[pt])
        P.op("dve", lambda e, ph=ph, o=o: e.reciprocal(out=rnorm[:, o * 8:(o + 1) * 8], in_=ph[:, 0:8]), reads=[pt], writes=[rnorm_t])
        if upto == 1:
            P.barrier()
            th, tt = carve(0, 4096, BF16), T("dbgt")
            P.op("dve", lambda e, th=th: e.tensor_copy(out=th[:, 0:1024], in_=Av[:, 0, :]), reads=[big_t[0]], writes=[tt])
            P.op("dve", lambda e, th=th: e.tensor_copy(out=th[:, 1024:1032], in_=rnorm[:, 0:8]), reads=[rnorm_t], writes=[tt])
            P.dma("sp", lambda e, th=th: e.dma_start(out=z3_out[0:128, :], in_=th[:, :]), reads=[tt], writes=[T("dbg")], semtile=tt)
            P.wait_all("sp", [tt] + x1s_t + x2s_t + zfs_t[0])
            return
        P.barrier()
        tab_p = Pool.views("ftab%d" % o, [carve(0, 8192, BF16, "p (a c n) -> p a c n", a=2, c=16), carve(8, 8192, BF16, "p (a c n) -> p a c n", a=2, c=16)])
        hsb_p = Pool.views("hsb%d" % o, [carve(16, 8192, F32, "p (a n) -> p a n", a=2)])
        tm_p = Pool.views("tmul%d" % o, [carve(24, 8192, F32, "p (a n) -> p a n", a=2)])
        for kc in range(16):
            th_, tt_ = tab_p.get()
            P.dma("sp", lambda e, th_=th_, kc=kc: e.dma_start(out=th_[:, 0, :, :], in_=cf_d[kc]), writes=[tt_])
            P.dma("sp", lambda e, th_=th_, kc=kc: e.dma_start(out=th_[:, 1, :, :], in_=sf_d[kc]), writes=[tt_])
            banks = [pp.get() for _ in range(8)]

            def mmf(e, th_=th_, banks=banks, lo=0, hi=8):
                for bi in range(lo, hi):
                    which, trig, cb = bi // 4, (bi // 2) % 2, bi % 2
                    for tc in range(16):
                        if which == 0:
                            rhs = z_tm[:, tc, cb * 512:(cb + 1) * 512]
                        else:
                            rhs = (Av if trig == 0 else Bv)[:, tc, cb * 512:(cb + 1) * 512]
                        ins = e.matmul(banks[bi][0][:, :], lhsT=th_[:, trig, tc, :], rhs=rhs, start=(tc == 0), stop=(tc == 15))
                return ins
            P.op("pe", lambda e, mmf=mmf: mmf(e, lo=4, hi=8), reads=[tt_] + big_t, writes=[b_[1] for b_ in banks[4:8]])
            P.op("pe", lambda e, mmf=mmf: mmf(e, lo=0, hi=4), reads=[tt_] + ztm_t, writes=[b_[1] for b_ in banks[0:4]])
            hh_, ht_ = hsb_p.get()
            for trig in range(2):
                for cb in range(2):
                    P.op("act", lambda e, hh_=hh_, trig=trig, cb=cb, banks=banks: e.activation(out=hh_[:, trig, cb * 512:(cb + 1) * 512], in_=banks[4 + trig * 2 + cb][0][:, :], func=AF.Copy),
                         reads=[banks[4 + trig * 2 + cb][1]], writes=[ht_])
            tmh, tmt = tm_p.get()
            for half, pairs, op_, yk in ((0, ((0, 0), (1, 1)), ALU.subtract, kc), (1, ((0, 1), (1, 0)), ALU.add, 16 + kc)):
                for ti, (zi, hi) in enumerate(pairs):
                    for cb in range(2):
                        P.op("dve", lambda e, tmh=tmh, ti=ti, zi=zi, hi=hi, cb=cb, banks=banks, hh_=hh_: e.tensor_tensor(
                            out=tmh[:, ti, cb * 512:(cb + 1) * 512], in0=banks[zi * 2 + cb][0][:, :], in1=hh_[:, hi, cb * 512:(cb + 1) * 512], op=ALU.mult),
                            reads=[banks[zi * 2 + cb][1], ht_], writes=[tmt])
                P.op("pool", lambda e, tmh=tmh, yk=yk, op_=op_: e.tensor_tensor(out=Yv[:, yk, :], in0=tmh[:, 0, :], in1=tmh[:, 1, :], op=op_), reads=[tmt], writes=[hT_t[yk // 2]])
        P.barrier()
        ep_z = Pool.views("ep_z%d" % o, [carve(0, 1024, BF16), carve(1, 1024, BF16)])
        ep_g = Pool.views("ep_g%d" % o, [carve(2, 1024, BF16), carve(3, 1024, BF16)])
        ep_f = Pool.views("ep_f%d" % o, [carve(4, 2048, F32), carve(6, 2048, F32)])
        ep_u = Pool.views("ep_u%d" % o, [carve(8, 2048, F32), carve(10, 2048, F32)])
        ep_o = Pool.views("ep_o%d" % o, [carve(12, 1024, BF16), carve(13, 1024, BF16)])
        gsrc, gsrc_t = (x1s, x1s_t) if o == 0 else (x2s, x2s_t)
        pend_tr = []

        def do_transposes(oh, ot, nb, cc):
            ph2, pt2 = pp.get()

            def mmt2(e, ph2=ph2, oh=oh):
                for i in range(4):
                    ins = e.matmul(ph2[:, i * 128:(i + 1) * 128], lhsT=oh[:, i * 128:(i + 1) * 128], rhs=idb[:, :], start=True, stop=True)
                return ins
            P.op("pe", mmt2, reads=[ot, idb_t], writes=[pt2])
            P.op("act", lambda e, ph2=ph2, nb=nb, cc=cc: e.activation(out=z_tm[:, nb * 4:(nb + 1) * 4, cc * 128:(cc + 1) * 128],
                                                                     in_=ph2[:, :].rearrange("p (i n) -> p i n", i=4), func=AF.Copy), reads=[pt2], writes=ztm_t[nb * 4:(nb + 1) * 4])
        for nb in range(4):
            tabv = big[:, nb % 2, :].rearrange("p (c n) -> p c n", c=32)
            P.dma("sp", lambda e, tabv=tabv, nb=nb: e.dma_start(out=tabv[:, 0:16, :], in_=inv_d[nb][:, 0:16, :]), writes=[big_t[nb % 2]])
            P.dma("sp", lambda e, tabv=tabv, nb=nb: e.dma_start(out=tabv[:, 16:32, :], in_=inv_d[nb][:, 16:32, :]), writes=[big_t[nb % 2]])
            sl = slice(nb * 512, (nb + 1) * 512)
            for cc in range(8):
                zh, zt = ep_z.get()
                gh, gt = ep_g.get()
                P.dma("sp", lambda e, zh=zh, cc=cc, sl=sl, o=o: e.dma_start(out=zh[:, :], in_=zfs[o][cc * 128:(cc + 1) * 128, sl]), reads=[zfs_t[o][cc]], writes=[zt])
                P.dma("sp", lambda e, gh=gh, cc=cc, sl=sl, gsrc=gsrc: e.dma_start(out=gh[:, :], in_=gsrc[cc * 128:(cc + 1) * 128, sl]), reads=[gsrc_t[cc]], writes=[gt])
                ph, pt = pp.get()

                def mmi(e, ph=ph, tabv=tabv, cc=cc):
                    for kc in range(32):
                        ins = e.matmul(ph[:, :], lhsT=Yv[:, kc, cc * 128:(cc + 1) * 128], rhs=tabv[:, kc, :], start=(kc == 0), stop=(kc == 31))
                    return ins
                P.op("pe", mmi, reads=hT_t + [big_t[nb % 2]], writes=[pt])
                fh_, ft_ = ep_f.get()
                P.op("act", lambda e, zh=zh, fh_=fh_, o=o, cc=cc: e.activation(out=fh_[:, :], in_=zh[:, :], func=AF.Copy, scale=skip[:, o * 8 + cc:o * 8 + cc + 1]),
                     reads=[zt, skip_t], writes=[ft_])
                uh, ut = ep_u.get()
                P.op("dve", lambda e, ph=ph, uh=uh, fh_=fh_, o=o, cc=cc: e.scalar_tensor_tensor(out=uh[:, :], in0=ph[:, :], scalar=rnorm[:, o * 8 + cc:o * 8 + cc + 1], in1=fh_[:, :],
                                                                                              op0=ALU.mult, op1=ALU.add), reads=[pt, rnorm_t, ft_], writes=[ut])
                oh, ot = ep_o.get()
                P.op("pool", lambda e, uh=uh, gh=gh, oh=oh: e.tensor_tensor(out=oh[:, :], in0=uh[:, :], in1=gh[:, :], op=ALU.mult), reads=[ut, gt], writes=[ot])
                if o == 0:
                    P.dma("pool", lambda e, oh=oh, cc=cc, sl=sl: e.dma_start(out=zfs[1][cc * 128:(cc + 1) * 128, sl], in_=oh[:, :]), reads=[ot], writes=[zfs_t[1][cc]], semtile=ot)
                    pend_tr.append((oh, ot, nb, cc))
                    if len(pend_tr) > 1:
                        do_transposes(*pend_tr.pop(0))
                else:
                    P.dma("pool", lambda e, oh=oh, cc=cc, sl=sl: e.dma_start(out=z3_out[cc * 128:(cc + 1) * 128, sl], in_=oh[:, :]), reads=[ot], writes=[out_t], semtile=ot)
        while pend_tr:
            do_transposes(*pend_tr.pop(0))
        if upto == 2 and o == 0:
            P.wait_all("sp", zfs_t[1] + x1s_t + x2s_t + zfs_t[0])
            return
    P.wait_all("sp", [b_[1] for b_ in ep_o.bufs] + zfs_t[1])


def build_hy(upto=99):
    nc = bass.Bass("TRN2", target_bir_lowering=False)
    hT_d = dram_in(nc, "i_hT", [D, L], BF16)
    w = dram_in(nc, "i_w", [D, 3072])
    sw_d = dram_in(nc, "i_sw", [128, 72])
    skip_d = dram_in(nc, "i_skip", [128, 16])
    fw1_d = dram_in(nc, "i_fw1", [33, 64])
    fw23_d = dram_in(nc, "i_fw23", [64, 128])
    fvec_d = dram_in(nc, "i_fvec", [64, 5])
    fw4_d = dram_in(nc, "i_fw4", [64, 4096])
    zemb_d = dram_in(nc, "i_zemb", [33, L])
    win_d = dram_in(nc, "i_win", [L, 1024])
    id_d = dram_in(nc, "i_id", [128, 128])
    cf_d = dram_in(nc, "i_cf", [16, 128, 16, 128], BF16)
    sf_d = dram_in(nc, "i_sf", [16, 128, 16, 128], BF16)
    inv_d = dram_in(nc, "i_inv", [4, 128, 32, 512], BF16)
    z3_out = dram_out(nc, "o_z3", [1024, L], BF16)
    P = Prog(nc)
    emit_hy(P, nc, hT_d, w, sw_d, skip_d, fw1_d, fw23_d, fvec_d, fw4_d, zemb_d, win_d, id_d, cf_d, sf_d, inv_d, z3_out, upto)
    P.emit()
    P.close()
    return nc


NCHK = 192


def emit_mod_full(P, nc, cT, aws, ab, s_mod):
    cs = P.sbuf("cs", [128, KC, 2], F32)
    sil = P.sbuf("sil", [128, KC, 2], BF16)
    sg = P.sbuf("sg", [128, KC, 2], F32)
    bs = P.sbuf("bs", [128, NCHK], F32)
    res = P.sbuf("res", [128, 2, NCHK], F32)
    ps = P.psum("mps", [128, 512])
    wpool = Pool(P, "mw", 4, [128, KC, 512], BF16)
    tcs, tsil, tsg, tbs, tres, tps, tout = T("cs"), T("sil"), T("sg"), T("bs"), T("res"), T("mps"), T("mout")
    P.dma("sp", lambda e: e.dma_start(out=cs[:], in_=cT.rearrange("(k p) j -> p k j", p=128)), writes=[tcs])
    P.dma("sp", lambda e: e.dma_start(out=bs[:], in_=ab), writes=[tbs])
    P.op("act", lambda e: e.activation(out=sg[:], in_=cs[:], func=AF.Sigmoid), reads=[tcs], writes=[tsg])
    P.op("dve", lambda e: e.tensor_tensor(out=sil[:], in0=cs[:], in1=sg[:], op=ALU.mult), reads=[tcs, tsg], writes=[tsil])
    for blk in range(48):
        wh, wt = wpool.get()
        aw = aws[blk // 24]
        c0 = (blk % 24) * 512
        q = "pool"
        P.dma(q, lambda e, wh=wh, aw=aw, c0=c0: e.dma_start(out=wh[:], in_=aw[:, c0:c0 + 512].rearrange("(k p) n -> p k n", p=128)), writes=[wt])

        def mm(e, wh=wh, blk=blk):
            for ci in range(4):
                i0 = (blk * 4 + ci) * 2
                for k in range(KC):
                    ins = e.matmul(ps[:, i0:i0 + 2], lhsT=wh[:, k, ci * 128:(ci + 1) * 128], rhs=sil[:, k, :], start=(k == 0), stop=(k == KC - 1))
            return ins
        P.op("pe", mm, reads=[wt, tsil], writes=[tps])
    psv = ps[:, 0:2 * NCHK].rearrange("p (c j) -> p c j", j=2)
    for j in range(2):
        P.op("dve", lambda e, j=j: e.tensor_tensor(out=res[:, j, :], in0=psv[:, :, j], in1=bs[:, :], op=ALU.add), reads=[tps, tbs], writes=[tres])
    P.dma("sp", lambda e: e.dma_start(out=s_mod, in_=res[:]), reads=[tres], writes=[tout])
    P.wait_all("sp", [tout])


def build_fused():
    nc = bass.Bass("TRN2", target_bir_lowering=False)
    I = lambda name, shape, dt=F32: dram_in(nc, name, shape, dt)
    xT = I("i_xT", [D, NTT])
    cT = I("i_cT", [D, 2])
    aws = [I("i_aw0", [D, 6 * D]), I("i_aw1", [D, 6 * D])]
    ab = I("i_ab", [128, NCHK])
    ng = I("i_ng", [128, 6 * KC])
    wB = [I("i_w0", [D, 4096]), I("i_w1", [D, 4096])]
    lbB = [I("i_lb0", [64, 3072]), I("i_lb1", [64, 3072])]
    cwB = [I("i_cw0", [128, 12]), I("i_cw1", [128, 12])]
    gnB = [I("i_gn0", [128, 4]), I("i_gn1", [128, 4])]
    cst = I("i_cst", [64, 324])
    wout = [I("i_wout0", [D, D]), I("i_wout1", [D, D])]
    w1 = [I("i_w1_0", [D, 4 * D]), I("i_w1_1", [D, 4 * D])]
    w2 = [I("i_w2_0", [4 * D, D]), I("i_w2_1", [4 * D, D])]
    hw = [I("i_hw0", [D, 3072]), I("i_hw1", [D, 3072])]
    hsw = [I("i_sw0", [128, 72]), I("i_sw1", [128, 72])]
    hsk = [I("i_skip0", [128, 16]), I("i_skip1", [128, 16])]
    fw1 = I("i_fw1", [33, 64])
    fw23 = I("i_fw23", [64, 128])
    fvec = I("i_fvec", [64, 5])
    fw4 = [I("i_fw4_0", [64, 4096]), I("i_fw4_1", [64, 4096])]
    zemb = I("i_zemb", [33, L])
    win = [I("i_win0", [L, 1024]), I("i_win1", [L, 1024])]
    idm = I("i_id", [128, 128])
    cf = I("i_cf", [16, 128, 16, 128], BF16)
    sf = I("i_sf", [16, 128, 16, 128], BF16)
    inv = I("i_inv", [4, 128, 32, 512], BF16)
    out = dram_out(nc, "o_out", [D, L])
    s_mod = nc.dram_tensor("s_mod", [128, 2, NCHK], F32).ap()
    s_m = nc.dram_tensor("s_m", [D, L], BF16).ap()
    s_x = nc.dram_tensor("s_x", [D, L], F32).ap()
    s_h1 = nc.dram_tensor("s_h1", [D, L], BF16).ap()
    s_z3 = nc.dram_tensor("s_z3", [D, L], BF16).ap()
    s_h0 = nc.dram_tensor("s_h0", [D, NTT], BF16).ap()

    def md(l, i, j):
        c0 = l * 96 + i * 16
        return s_mod[:, j, c0:c0 + 16]

    def ngc(i):
        return ng[:, i * KC:(i + 1) * KC]

    def stage(pfx, fn):
        with nc.cleanup_on_exit():
            P = Prog(nc, pfx)
            fn(P)
            P.emit()
            nc.all_engine_barrier()

    stage("A_", lambda P: emit_mod_full(P, nc, cT, aws, ab, s_mod))
    for hh in range(2):
        stage("B%d_" % hh, lambda P, hh=hh: emit_mix0(
            P, nc, xT, [ngc(0), md(0, 1, 0), md(0, 0, 0), md(0, 1, 1), md(0, 0, 1)], wB[hh], lbB[hh], cwB[hh], gnB[hh], cst, s_m,
            mrow=(hh * 512, 1024 + hh * 512), h_save=(s_h0 if hh == 0 else None), h_load=(s_h0 if hh == 1 else None)))
    for th in range(2):
        sl = slice(th * NT, (th + 1) * NT)
        stage("C%d_" % th, lambda P, sl=sl: emit_tok(
            P, nc, s_m[:, sl], xT[:, sl], [md(0, 2, 0), ngc(1), md(0, 4, 0), md(0, 3, 0), md(0, 5, 0), ngc(2), md(1, 1, 0), md(1, 0, 0)],
            wout[0], w1[0], w2[0], s_x[:, sl], s_h1[:, sl], False))
    for hh in range(2):
        stage("D%d_" % hh, lambda P, hh=hh: emit_hy(
            P, nc, s_h1, hw[hh], hsw[hh], hsk[hh], fw1, fw23, fvec, fw4[hh], zemb, win[hh], idm, cf, sf, inv, s_z3[hh * 1024:(hh + 1) * 1024, :]))
    for th in range(2):
        sl = slice(th * NT, (th + 1) * NT)
        stage("E%d_" % th, lambda P, sl=sl: emit_tok(
            P, nc, s_z3[:, sl], s_x[:, sl], [md(1, 2, 0), ngc(3), md(1, 4, 0), md(1, 3, 0), md(1, 5, 0), ngc(4), ngc(5), ngc(5)],
            wout[1], w1[1], w2[1], out[:, sl], None, True))
    return nc


_PROGS = {}


def _pk(v):
    return np.asarray(v, np.float32).reshape(16, 128).T


def kernel(x, c, ctx, c_ctx, ada_w, ada_b, norm_g, lb_logits, ab_w_in, ab_conv_w, ab_gnorm_g, ab_w_out, hy_in_w, hy_short_w,
           hy_out_w, hy_fw1, hy_fb1, hy_fw2, hy_fb2, hy_fw3, hy_fb3, hy_fw4, hy_freq, hy_skip, mlp_w1, mlp_w2, final_g):
    f32 = lambda a: np.asarray(a, np.float32)
    C_ = np.ascontiguousarray
    x, c, ctx, c_ctx, ada_w, ada_b, norm_g = map(f32, (x, c, ctx, c_ctx, ada_w, ada_b, norm_g))
    if "fused" not in _PROGS:
        _PROGS["fused"] = build_fused()
    nc = _PROGS["fused"]
    shared = {}
    shared["i_aw0"], shared["i_aw1"] = ada_w[0], ada_w[1]
    shared["i_ab"] = C_(np.concatenate([ada_b[0], ada_b[1]]).reshape(NCHK, 128).T)
    shared["i_ng"] = C_(np.concatenate([_pk(norm_g[0, 0]), _pk(norm_g[0, 1]), _pk(norm_g[1, 0]), _pk(norm_g[1, 1]), _pk(f32(final_g)),
                                        _pk(np.zeros(D, np.float32))], 1))
    w_in = f32(ab_w_in)[0]
    in_w = f32(hy_in_w)[0]
    cf, sf, inv = hy_tables()
    shared.update({"i_cst": mix0_consts(), "i_fw1": f32(hy_fw1)[0], "i_fw23": C_(np.concatenate([f32(hy_fw2)[0], f32(hy_fw3)[0]], 1)),
                   "i_fvec": np.stack([f32(hy_fb1)[0], f32(hy_fb2)[0], f32(hy_fb3)[0], f32(hy_freq)[0], np.full(64, -np.pi, np.float32)], 1).astype(np.float32),
                   "i_id": np.eye(128, dtype=np.float32), "i_cf": cf, "i_sf": sf, "i_inv": inv,
                   "i_wout0": f32(ab_w_out)[0], "i_wout1": f32(hy_out_w)[0], "i_w1_0": f32(mlp_w1)[0], "i_w1_1": f32(mlp_w1)[1],
                   "i_w2_0": f32(mlp_w2)[0], "i_w2_1": f32(mlp_w2)[1]})
    for hh in range(2):
        cols = np.concatenate([np.arange(g * 1024 + hh * 512, g * 1024 + hh * 512 + 512) for g in range(8)])
        shared["i_w%d" % hh] = C_(w_in[:, cols])
        shared["i_lb%d" % hh] = C_(np.broadcast_to(f32(lb_logits)[:, :, hh * 512:(hh + 1) * 512].reshape(1, -1), (64, 3072)))
        shared["i_cw%d" % hh] = C_(f32(ab_conv_w)[0][:, hh * 512:(hh + 1) * 512].T.reshape(4, 128, 3).transpose(1, 0, 2).reshape(128, 12))
        shared["i_gn%d" % hh] = C_(f32(ab_gnorm_g)[0][hh * 512:(hh + 1) * 512].reshape(4, 128).T)
        cols = np.concatenate([np.arange(g * 2048 + hh * 1024, g * 2048 + hh * 1024 + 1024) for g in range(3)])
        shared["i_hw%d" % hh] = C_(in_w[:, cols])
        shared["i_sw%d" % hh] = C_(f32(hy_short_w)[0][:, cols].T.reshape(24, 128, 3).transpose(1, 0, 2).reshape(128, 72))
        shared["i_skip%d" % hh] = C_(f32(hy_skip)[0][:, hh * 1024:(hh + 1) * 1024].reshape(2, 8, 128).transpose(2, 0, 1).reshape(128, 16))
        shared["i_fw4_%d" % hh] = C_(f32(hy_fw4)[0].reshape(64, 2, 2, 2048)[:, :, :, hh * 1024:(hh + 1) * 1024].reshape(64, 4096))
        zembT, win = hy_consts(hh)
        shared["i_zemb"] = zembT
        shared["i_win%d" % hh] = win
    maps = []
    for b in range(4):
        m = dict(shared)
        m["i_xT"] = C_(np.concatenate([x[b], ctx[b]], 0).T)
        m["i_cT"] = C_(np.stack([c[b], c_ctx], 1))
        maps.append(m)
    res = run_bass_kernel_spmd(nc, maps, core_ids=list(range(4))).results
    return np.stack([C_(res[b]["o_out"].T) for b in range(4)], 0).astype(np.float32)
```

```python
import numpy as np
import ml_dtypes
import concourse.bass as bass
import concourse.mybir as mybir
from concourse.bass_utils import run_bass_kernel_spmd

F32 = mybir.dt.float32
BF16 = mybir.dt.bfloat16
AF = mybir.ActivationFunctionType
ALU = mybir.AluOpType
AX = mybir.AxisListType
NPBF = ml_dtypes.bfloat16

D = 2048
KC = 16
EPS = 1e-6
ENGS = ("pe", "act", "dve", "pool", "sp")


class T:
    __slots__ = ("name", "lastw", "readers", "sem", "dman")

    def __init__(self, name):
        self.name = name
        self.lastw = None
        self.readers = []
        self.sem = None
        self.dman = 0


def TL(name, n):
    return [T("%s%d" % (name, i)) for i in range(n)]


class Prog:
    def __init__(self, nc, pfx=""):
        self.nc = nc
        self.pfx = pfx
        self.q = {e: [] for e in ENGS}
        self.cnt = {e: 0 for e in ENGS}
        self.seen = {e: {} for e in ENGS}
        self.sems = {}
        self._stack = []
        self._dma_tiles = []

    def _enter(self, cm):
        h = cm.__enter__()
        self._stack.append(cm)
        return h

    def sem(self, name):
        return self._enter(self.nc.semaphore(self.pfx + name))

    def sbuf(self, name, shape, dt):
        return self._enter(self.nc.sbuf_tensor(self.pfx + name, list(shape), dt))

    def psum(self, name, shape, dt=F32):
        return self._enter(self.nc.psum_tensor(self.pfx + name, list(shape), dt))

    def close(self):
        while self._stack:
            self._stack.pop().__exit__(None, None, None)

    def _engsem(self, e):
        if e not in self.sems:
            self.sems[e] = self.sem("s_" + e)
        return self.sems[e]

    def _collect(self, eng, reads, writes, is_dma):
        waits = {}

        def need(ev, war=False):
            if ev is None:
                return
            key, val, e = ev
            if e == eng and not is_dma and key == eng and (war or eng == "pe"):
                return
            if waits.get(key, 0) < val:
                waits[key] = val

        for t in reads:
            need(t.lastw)
        for t in writes:
            need(t.lastw)
            for r in t.readers:
                need(r, war=True)
        out = []
        seen = self.seen[eng]
        for key, val in waits.items():
            if seen.get(key, 0) >= val:
                continue
            seen[key] = val
            out.append((key, val))
        return out

    def _semobj(self, key):
        if isinstance(key, str):
            return self._engsem(key)
        return key.sem

    def op(self, eng, fn, reads=(), writes=()):
        waits = self._collect(eng, reads, writes, False)
        self.cnt[eng] += 1
        self._engsem(eng)
        ev = (eng, self.cnt[eng], eng)
        for t in reads:
            t.readers.append(ev)
        for t in writes:
            t.lastw = ev
            t.readers = []
        self.q[eng].append((waits, fn, (eng, 1)))

    def dma(self, eng, fn, reads=(), writes=(), semtile=None):
        waits = self._collect(eng, reads, writes, True)
        st = semtile or (writes[0] if writes else reads[0])
        if st.sem is None:
            st.sem = self.sem("d_" + st.name)
            self._dma_tiles.append(st)
        st.dman += 1
        ev = (st, 16 * st.dman, "dma")
        for t in reads:
            t.readers.append(ev)
        for t in writes:
            t.lastw = ev
            t.readers = []
        self.q[eng].append((waits, fn, (st, 16)))

    def barrier(self):
        tiles = [t for t in self._dma_tiles]
        for e in ENGS:
            if e == "pe" and not self.q[e]:
                continue
            waits = []
            seen = self.seen[e]
            for e2 in ENGS:
                if e2 != e and e2 in self.sems and self.cnt[e2] > seen.get(e2, 0):
                    seen[e2] = self.cnt[e2]
                    waits.append((e2, self.cnt[e2]))
            for t in tiles:
                v = 16 * t.dman
                if v > seen.get(t, 0):
                    seen[t] = v
                    waits.append((t, v))
            self.q[e].append((waits, None, None))

    def wait_all(self, eng, tiles):
        waits = self._collect(eng, tiles, tiles, True)
        self.q[eng].append((waits, None, None))

    def emit(self):
        engobj = {"pe": "tensor", "act": "scalar", "dve": "vector", "pool": "gpsimd", "sp": "sync"}
        with self.nc.Block() as block:
            for e in ENGS:
                lst = self.q[e]
                if not lst:
                    continue

                def body(eobj, lst=lst):
                    for waits, fn, inc in lst:
                        for key, val in waits:
                            eobj.wait_ge(self._semobj(key), val)
                        if fn is None:
                            continue
                        ins = fn(eobj)
                        ins.then_inc(self._semobj(inc[0]), inc[1])

                getattr(block, engobj[e])(body)


class Pool:
    def __init__(self, P, name, n, shape, dt, psum=False):
        self.bufs = []
        for i in range(n):
            h = P.psum("%s%d" % (name, i), shape, dt) if psum else P.sbuf("%s%d" % (name, i), shape, dt)
            self.bufs.append((h, T("%s%d" % (name, i))))
        self.i = 0

    def get(self):
        b = self.bufs[self.i % len(self.bufs)]
        self.i += 1
        return b

    @classmethod
    def views(cls, name, aps):
        self = cls.__new__(cls)
        self.bufs = [(ap, T("%s%d" % (name, i))) for i, ap in enumerate(aps)]
        self.i = 0
        return self


def dram_in(nc, name, shape, dt=F32):
    return nc.dram_tensor(name, list(shape), dt, kind="ExternalInput").ap()


def dram_out(nc, name, shape, dt=F32):
    return nc.dram_tensor(name, list(shape), dt, kind="ExternalOutput").ap()


def load_w_block(P, wpool, w_ap, rows_kc, col0, ncols, queue="pool"):
    wh, wt = wpool.get()
    src = w_ap[:, col0:col0 + ncols].rearrange("(k p) n -> p k n", p=128)
    P.dma(queue, lambda e: e.dma_start(out=wh[:, 0:rows_kc, 0:ncols], in_=src), writes=[wt])
    return wh, wt


def norm_stats(P, C, xs_fn, x_ts, tok_blocks, ntok, pre=None):
    ones, sqpool, stat, stat_t, rstd, rstd_t = C["ones"], C["sqpool"], C["stat"], C["stat_t"], C["rstd"], C["rstd_t"]
    for k in range(KC):
        if pre is not None:
            pre(k)
        sq, sqt = sqpool.get()
        xap, xtk = xs_fn(k), (x_ts(k) if callable(x_ts) else x_ts[k])
        P.op("act", lambda e, xap=xap, sq=sq: e.activation(out=sq[:, 0:ntok], in_=xap, func=AF.Square),
             reads=[xtk], writes=[sqt])

        def mm(e, k=k, sq=sq):
            for bi, (t0, tn) in enumerate(tok_blocks):
                i = e.matmul(stat[bi][:, 0:tn], lhsT=ones[:], rhs=sq[:, t0:t0 + tn], start=(k == 0), stop=(k == KC - 1))
            return i
        P.op("pe", mm, reads=[sqt, C["ones_t"]], writes=stat_t)
    for bi, (t0, tn) in enumerate(tok_blocks):
        P.op("act", lambda e, bi=bi, t0=t0, tn=tn: e.activation(
            out=rstd[:, t0:t0 + tn], in_=stat[bi][:, 0:tn], func=AF.Sqrt, scale=1.0 / D, bias=C["eps"][:, 0:1]),
            reads=[stat_t[bi], C["eps_t"]], writes=[rstd_t])
    P.op("dve", lambda e: e.reciprocal(out=rstd[:, 0:ntok], in_=rstd[:, 0:ntok]), reads=[rstd_t], writes=[rstd_t])


def load_scal(P, scal, scal_t, scal_d):
    if isinstance(scal_d, (list, tuple)):
        for i, ap in enumerate(scal_d):
            P.dma("sp", lambda e, i=i, ap=ap: e.dma_start(out=scal[:, i * KC:(i + 1) * KC], in_=ap), writes=[scal_t])
    else:
        P.dma("sp", lambda e: e.dma_start(out=scal[:], in_=scal_d), writes=[scal_t])


def mod_coef(P, C, scal, scal_t, gi, sci, name):
    a = P.sbuf(name, [128, KC], F32)
    at = T(name)
    P.op("dve", lambda e: e.scalar_tensor_tensor(out=a[:], in0=scal[:, sci * KC:(sci + 1) * KC], scalar=1.0,
                                                 in1=scal[:, gi * KC:(gi + 1) * KC], op0=ALU.add, op1=ALU.mult),
         reads=[scal_t], writes=[at])
    return a, at


NMC = 3072


def emit_mod(P, nc, cT, aw, ab, out):
    cs = P.sbuf("cs", [128, KC, 5], F32)
    sil = P.sbuf("sil", [128, KC, 5], F32)
    sg = P.sbuf("sg", [128, KC, 5], F32)
    bs = P.sbuf("bs", [128, NMC // 128], F32)
    res = P.sbuf("res", [128, NMC // 128, 5], F32)
    ps = P.psum("mps", [128, 512])
    wpool = Pool(P, "mw", 2, [128, KC, 512], F32)
    tcs, tsil, tsg, tbs, tres, tps, tout = T("cs"), T("sil"), T("sg"), T("bs"), T("res"), T("mps"), T("mout")
    P.dma("sp", lambda e: e.dma_start(out=cs[:], in_=cT.rearrange("(k p) j -> p k j", p=128)), writes=[tcs])
    P.dma("sp", lambda e: e.dma_start(out=bs[:], in_=ab), writes=[tbs])
    P.op("act", lambda e: e.activation(out=sg[:], in_=cs[:], func=AF.Sigmoid), reads=[tcs], writes=[tsg])
    P.op("dve", lambda e: e.tensor_tensor(out=sil[:], in0=cs[:], in1=sg[:], op=ALU.mult), reads=[tcs, tsg], writes=[tsil])
    nblk = NMC // 512
    for blk in range(nblk):
        wh, wt = wpool.get()
        q = "sp" if blk % 2 == 0 else "act"
        P.dma(q, lambda e, wh=wh, blk=blk: e.dma_start(
            out=wh[:], in_=aw[:, blk * 512:(blk + 1) * 512].rearrange("(k p) n -> p k n", p=128)), writes=[wt])

        def mm(e, wh=wh, blk=blk):
            for ci in range(4):
                i0 = (blk * 4 + ci) * 5
                for k in range(KC):
                    ins = e.matmul(ps[:, i0:i0 + 5], lhsT=wh[:, k, ci * 128:(ci + 1) * 128], rhs=sil[:, k, :],
                                   start=(k == 0), stop=(k == KC - 1))
            return ins
        P.op("pe", mm, reads=[wt, tsil], writes=[tps])
    nch = NMC // 128
    P.op("dve", lambda e: e.tensor_tensor(out=res[:], in0=ps[:, 0:nch * 5].rearrange("p (c j) -> p c j", j=5),
                                          in1=bs[:].unsqueeze(2).to_broadcast([128, nch, 5]), op=ALU.add),
         reads=[tps, tbs], writes=[tres])
    P.dma("sp", lambda e: e.dma_start(out=out, in_=res[:].rearrange("p c j -> p (c j)")), reads=[tres], writes=[tout])
    P.wait_all("sp", [tout])


def build_mod():
    nc = bass.Bass("TRN2", target_bir_lowering=False)
    cT = dram_in(nc, "i_cT", [D, 5])
    aw = dram_in(nc, "i_aw", [D, NMC])
    ab = dram_in(nc, "i_ab", [128, NMC // 128])
    out = dram_out(nc, "o_mod", [128, (NMC // 128) * 5])
    P = Prog(nc)
    emit_mod(P, nc, cT, aw, ab, out)
    P.emit()
    P.close()
    return nc


NT = 1024
TB2 = [(0, 512), (512, 512)]


def emit_tok(P, nc, mT, xT, scal_d, w_out, w1, w2, x_out, h_out, final):
    x_sb = P.sbuf("x_sb", [128, KC, NT], F32)
    hb = P.sbuf("hb", [128, KC, NT], BF16)
    scal = P.sbuf("scal", [128, 8 * KC], F32)
    ones = P.sbuf("ones", [128, 128], BF16)
    rstd = P.sbuf("rstd", [128, NT], F32)
    tmp = P.sbuf("tmpn", [128, NT], F32)
    wpool = Pool(P, "w", 3, [128, KC, 512], BF16)
    apool = Pool(P, "ab", 2, [128, 4, NT], BF16)
    rpool = Pool(P, "rl", 2, [128, 512], F32)
    sqpool = Pool(P, "sq", 2, [128, NT], BF16)
    pp = Pool(P, "pp", 6, [128, 512], F32, psum=True)
    stat = [P.psum("st%d" % i, [128, 512]) for i in range(2)]
    x_t, hb_t = TL("x", KC), TL("hb", KC)
    scal_t, ones_t, rstd_t, tmp_t = T("scal"), T("ones"), T("rstd"), T("tmpn")
    epsb = P.sbuf("epsb", [128, 1], F32)
    eps_t = T("epsb")
    P.op("pool", lambda e: e.memset(epsb[:], EPS), writes=[eps_t])
    C = dict(ones=ones, sqpool=sqpool, stat=stat, stat_t=TL("st", 2), rstd=rstd, rstd_t=rstd_t, eps=epsb, eps_t=eps_t, ones_t=ones_t)
    tout = T("tokout")

    P.op("pool", lambda e: e.memset(ones[:], 1.0), writes=[ones_t])
    load_scal(P, scal, scal_t, scal_d)
    for k in range(KC):
        q = "sp"
        P.dma(q, lambda e, k=k: e.dma_start(out=hb[:, k, :], in_=mT[k * 128:(k + 1) * 128, :]), writes=[hb_t[k]])
        P.dma(q, lambda e, k=k: e.dma_start(out=x_sb[:, k, :], in_=xT[k * 128:(k + 1) * 128, :]), writes=[x_t[k]])

    def sc(i, c):
        return scal[:, i * KC + c:i * KC + c + 1]

    for cb in range(4):
        wh, wt = load_w_block(P, wpool, w_out, KC, cb * 512, 512)
        for ci in range(4):
            oc = cb * 4 + ci
            for (t0, tn) in TB2:
                ph, pt = pp.get()

                def mm(e, wh=wh, ci=ci, t0=t0, tn=tn, ph=ph):
                    for k in range(KC):
                        ins = e.matmul(ph[:, 0:tn], lhsT=wh[:, k, ci * 128:(ci + 1) * 128], rhs=hb[:, k, t0:t0 + tn],
                                       start=(k == 0), stop=(k == KC - 1))
                    return ins
                P.op("pe", mm, reads=[wt] + hb_t, writes=[pt])
                P.op("dve", lambda e, oc=oc, t0=t0, tn=tn, ph=ph: e.scalar_tensor_tensor(
                    out=x_sb[:, oc, t0:t0 + tn], in0=ph[:, 0:tn], scalar=sc(0, oc), in1=x_sb[:, oc, t0:t0 + tn],
                    op0=ALU.mult, op1=ALU.add), reads=[pt, scal_t, x_t[oc]], writes=[x_t[oc]])

    def norm_to_hb(gi, sci, shi, name):
        a, at = mod_coef(P, C, scal, scal_t, gi, sci, name)
        norm_stats(P, C, lambda k: x_sb[:, k, :], x_t, TB2, NT)
        for k in range(KC):
            P.op("dve", lambda e, k=k: e.scalar_tensor_tensor(out=tmp[:], in0=x_sb[:, k, :], scalar=a[:, k:k + 1],
                                                              in1=rstd[:], op0=ALU.mult, op1=ALU.mult),
                 reads=[x_t[k], at, rstd_t], writes=[tmp_t])
            P.op("act", lambda e, k=k: e.activation(out=hb[:, k, :], in_=tmp[:], func=AF.Identity, bias=sc(shi, k)),
                 reads=[tmp_t, scal_t], writes=[hb_t[k]])
    norm_to_hb(1, 2, 3, "a2")

    for fb in range(16):
        w1h, w1t = load_w_block(P, wpool, w1, KC, fb * 512, 512)
        w2h, w2t = wpool.get()
        w2v = w2h[:].rearrange("p (a b) n -> p a (b n)", a=4)
        P.dma("pool", lambda e, w2v=w2v, fb=fb: e.dma_start(
            out=w2v, in_=w2[fb * 512:(fb + 1) * 512, :].rearrange("(a p) n -> p a n", p=128)), writes=[w2t])
        ah, at_ = apool.get()
        for ci in range(4):
            for (t0, tn) in TB2:
                ph, pt = pp.get()

                def mm(e, w1h=w1h, ci=ci, t0=t0, tn=tn, ph=ph):
                    for k in range(KC):
                        ins = e.matmul(ph[:, 0:tn], lhsT=w1h[:, k, ci * 128:(ci + 1) * 128], rhs=hb[:, k, t0:t0 + tn],
                                       start=(k == 0), stop=(k == KC - 1))
                    return ins
                P.op("pe", mm, reads=[w1t] + hb_t, writes=[pt])
                rh, rt = rpool.get()
                P.op("act", lambda e, ph=ph, rh=rh, tn=tn: e.activation(out=rh[:, 0:tn], in_=ph[:, 0:tn], func=AF.Relu),
                     reads=[pt], writes=[rt])
                P.op("pool", lambda e, rh=rh, ah=ah, ci=ci, t0=t0, tn=tn: e.tensor_tensor(
                    out=ah[:, ci, t0:t0 + tn], in0=rh[:, 0:tn], in1=rh[:, 0:tn], op=ALU.mult), reads=[rt], writes=[at_])
        for oc in range(KC):
            for (t0, tn) in TB2:
                ph, pt = pp.get()

                def mm2(e, w2v=w2v, oc=oc, t0=t0, tn=tn, ph=ph, ah=ah):
                    for a in range(4):
                        ins = e.matmul(ph[:, 0:tn], lhsT=w2v[:, a, oc * 128:(oc + 1) * 128], rhs=ah[:, a, t0:t0 + tn],
                                       start=(a == 0), stop=(a == 3))
                    return ins
                P.op("pe", mm2, reads=[w2t, at_], writes=[pt])
                P.op("dve", lambda e, oc=oc, t0=t0, tn=tn, ph=ph: e.scalar_tensor_tensor(
                    out=x_sb[:, oc, t0:t0 + tn], in0=ph[:, 0:tn], scalar=sc(4, oc), in1=x_sb[:, oc, t0:t0 + tn],
                    op0=ALU.mult, op1=ALU.add), reads=[pt, scal_t, x_t[oc]], writes=[x_t[oc]])

    if not final:
        norm_to_hb(5, 6, 7, "a3")
        for k in range(KC):
            q = "sp"
            P.dma(q, lambda e, k=k: e.dma_start(out=h_out[k * 128:(k + 1) * 128, :], in_=hb[:, k, :]),
                  reads=[hb_t[k]], writes=[tout], semtile=hb_t[k])
            P.dma(q, lambda e, k=k: e.dma_start(out=x_out[k * 128:(k + 1) * 128, :], in_=x_sb[:, k, :]),
                  reads=[x_t[k]], writes=[tout], semtile=x_t[k])
        P.wait_all("sp", hb_t + x_t)
        P.wait_all("act", hb_t + x_t)
    else:
        norm_stats(P, C, lambda k: x_sb[:, k, :], x_t, TB2, NT)
        for k in range(KC):
            P.op("dve", lambda e, k=k: e.scalar_tensor_tensor(out=x_sb[:, k, :], in0=x_sb[:, k, :], scalar=sc(5, k),
                                                              in1=rstd[:], op0=ALU.mult, op1=ALU.mult),
                 reads=[x_t[k], scal_t, rstd_t], writes=[x_t[k]])
            q = "sp"
            P.dma(q, lambda e, k=k: e.dma_start(out=x_out[k * 128:(k + 1) * 128, :], in_=x_sb[:, k, :]),
                  reads=[x_t[k]], writes=[tout], semtile=x_t[k])
        P.wait_all("sp", x_t)
        P.wait_all("act", x_t)


def build_tok(final):
    nc = bass.Bass("TRN2", target_bir_lowering=False)
    mT = dram_in(nc, "i_mT", [D, NT], BF16)
    xT = dram_in(nc, "i_xT", [D, NT])
    scal_d = dram_in(nc, "i_scal", [128, 8 * KC])
    w_out = dram_in(nc, "i_wout", [D, D])
    w1 = dram_in(nc, "i_w1", [D, 4 * D])
    w2 = dram_in(nc, "i_w2", [4 * D, D])
    x_out = dram_out(nc, "o_x", [D, NT])
    h_out = None if final else dram_out(nc, "o_h", [D, NT], BF16)
    P = Prog(nc)
    emit_tok(P, nc, mT, xT, scal_d, w_out, w1, w2, x_out, h_out, final)
    P.emit()
    P.close()
    return nc


NTT = 2304
NCH = 36
TB5 = [(0, 512), (512, 512), (1024, 512), (1536, 512), (2048, 256)]
SEQ_F = [32, 33, 34, 35] + list(range(32))
SEQ_B = [35, 34, 33, 32] + list(range(31, -1, -1))


def mix0_consts():
    s = np.arange(64)
    pm = np.zeros((64, 2, 66), np.float32)
    pm[:, 0, :64] = (s[:, None] <= s[None, :]).astype(np.float32) - (s[:, None] <= 31).astype(np.float32)
    pm[:, 0, 64] = (s >= 32)
    pm[:, 0, 65] = (s <= 31)
    pm[:, 1, :64] = (s[:, None] >= s[None, :]).astype(np.float32) - (s[:, None] >= 32).astype(np.float32)
    pm[:, 1, 64] = (s <= 31)
    pm[:, 1, 65] = (s >= 32)
    mk = np.zeros((64, 2, 64), np.float32)
    mk[:, 0] = (s[:, None] <= s[None, :])
    mk[:, 1] = (s[:, None] >= s[None, :])
    return np.concatenate([pm.reshape(64, 132), mk.reshape(64, 128), np.eye(64, dtype=np.float32)], axis=1)


def emit_mix0(P, nc, xT, scal_d, w, lbr, cw_d, gn_d, cst_d, m_out, upto=99, mrow=(0, 512), h_save=None, h_load=None):
    gp_eq = Pool(P, "g_eq", 1, [128, 256], F32)
    hT = P.sbuf("hT", [128, KC, NTT], BF16)
    hT_t = TL("hT", KC)
    scal = P.sbuf("scal", [128, 5 * KC], F32)
    ones = P.sbuf("ones", [128, 128], BF16)
    epsb = P.sbuf("epsb", [128, 1], F32)
    lean = h_load is not None
    if not lean:
        rstd = P.sbuf("rstd", [128, NTT], F32)
        xpool = Pool(P, "xin", 2, [128, NTT], F32)
        sqpool = Pool(P, "sq", 1, [128, NTT], BF16)
        tmpp = Pool(P, "tmpn", 1, [128, NTT], F32)
    else:
        rstd = xpool = None
        sqpool = Pool(P, "sq", 1, [128, 512], BF16)
        tmpp = Pool(P, "tmpn", 1, [128, 512], F32)
    pp = Pool(P, "pp", 8, [128, 512], F32, psum=True)
    scal_t, ones_t, eps_t, rstd_t = T("scal"), T("ones"), T("epsb"), T("rstd")
    stat = [pp.bufs[i][0] for i in range(5)]
    stat_t = [pp.bufs[i][1] for i in range(5)]
    C = dict(ones=ones, sqpool=sqpool, stat=stat, stat_t=stat_t, rstd=rstd, rstd_t=rstd_t, eps=epsb, eps_t=eps_t, ones_t=ones_t)

    P.op("pool", lambda e: e.memset(ones[:], 1.0), writes=[ones_t])
    P.op("pool", lambda e: e.memset(epsb[:], EPS), writes=[eps_t])
    if h_load is None:
        load_scal(P, scal, scal_t, scal_d)

    xk_t = TL("xk", KC)
    xbuf = {}

    def load_x(k):
        xh, xt = xpool.get()
        q = "sp"
        P.dma(q, lambda e, xh=xh, k=k: e.dma_start(out=xh[:], in_=xT[k * 128:(k + 1) * 128, :]), writes=[xt])
        xbuf[k] = (xh, xt)

    hsv_t = []
    if h_load is not None:
        for k in range(KC):
            q = "sp"
            P.dma(q, lambda e, k=k: e.dma_start(out=hT[:, k, :], in_=h_load[k * 128:(k + 1) * 128, :]), writes=[hT_t[k]])
    else:
        norm_stats(P, C, lambda k: xbuf[k][0][:], lambda k: xbuf[k][1], TB5, NTT, pre=load_x)
        a_l, a_lt = mod_coef(P, C, scal, scal_t, 0, 1, "a_l")
        a_c, a_ct = mod_coef(P, C, scal, scal_t, 0, 3, "a_c")
        for k in range(KC):
            load_x(k)
            xh, xt = xbuf[k]
            th, tt = tmpp.get()
            for (lo, hi, a, at, shi) in ((0, 2048, a_l, a_lt, 2), (2048, NTT, a_c, a_ct, 4)):
                P.op("dve", lambda e, xh=xh, th=th, lo=lo, hi=hi, a=a, k=k: e.scalar_tensor_tensor(
                    out=th[:, lo:hi], in0=xh[:, lo:hi], scalar=a[:, k:k + 1], in1=rstd[:, lo:hi], op0=ALU.mult, op1=ALU.mult),
                    reads=[xt, at, rstd_t], writes=[tt])
                P.op("act", lambda e, th=th, lo=lo, hi=hi, k=k, shi=shi: e.activation(
                    out=hT[:, k, lo:hi], in_=th[:, lo:hi], func=AF.Identity, bias=scal[:, shi * KC + k:shi * KC + k + 1]),
                    reads=[tt, scal_t], writes=[hT_t[k]])
        if h_save is not None:
            for k in range(KC):
                q = "sp"
                hsv_t.append(T("hsv%d" % k))
                P.dma(q, lambda e, k=k: e.dma_start(out=h_save[k * 128:(k + 1) * 128, :], in_=hT[:, k, :]), reads=[hT_t[k]], writes=[hsv_t[-1]], semtile=hT_t[k])
    if upto == -1:
        P.wait_all("sp", hsv_t)
        return
    if upto == 0:
        for k in range(8):
            P.dma("sp", lambda e, k=k: e.dma_start(out=m_out[k * 128:(k + 1) * 128, :], in_=hT[:, k, 0:2048]), reads=[hT_t[k]], writes=[T("dbg")], semtile=hT_t[k])
        P.wait_all("sp", hT_t)
        return
    cst = P.sbuf("cst", [64, 324], F32)
    cst_t = T("cst")
    P.dma("sp", lambda e: e.dma_start(out=cst[:], in_=cst_d), writes=[cst_t])
    pm = cst[:, 0:132].rearrange("p (d n) -> p d n", d=2)
    mk = cst[:, 132:260].rearrange("p (d n) -> p d n", d=2)
    idb = P.sbuf("idb", [64, 64], BF16)
    idb_t = T("idb")
    P.op("dve", lambda e: e.tensor_copy(out=idb[:], in_=cst[:, 260:324]), reads=[cst_t], writes=[idb_t])
    cw = P.sbuf("cw", [128, 12], F32)
    gn = P.sbuf("gn", [128, 4], F32)
    cw_t, gn_t = T("cw"), T("gn")
    P.dma("sp", lambda e: e.dma_start(out=cw[:], in_=cw_d), writes=[cw_t])
    P.dma("sp", lambda e: e.dma_start(out=gn[:], in_=gn_d), writes=[gn_t])
    if not lean:
        wtm_p = Pool(P, "wtm", 1, [128, KC, 384], BF16)
        wfm_view, wfm_t = rstd[:].bitcast(BF16)[:, 0:KC * 256].rearrange("p (k n) -> p k n", k=KC), rstd_t
        qT = xpool.bufs[0][0][:, 0:2048]
        sgT = xpool.bufs[1][0][:, 0:2048]
    else:
        wtm_p = Pool(P, "wtm", 2, [128, KC, 384], BF16)
        wfm_view, wfm_t = P.sbuf("wfm", [128, KC, 256], BF16), T("wfm")
        qT = P.sbuf("qT", [128, 2048], F32)
        sgT = P.sbuf("sgT", [128, 2048], F32)
    prefetch = lean
    vtok = P.sbuf("vtok", [64, NCH, 128], BF16)
    keT = P.sbuf("keT", [64, 2, NCH, 128], BF16)
    keF = P.sbuf("keF", [128, 2, 32, 64], BF16)
    qe = P.sbuf("qe", [128, 2, 32, 64], BF16)
    ABt = P.sbuf("ABt", [128, 2, NCH, 2], F32)
    Gs = P.sbuf("Gs", [128, 2, NCH], F32)
    oT = P.sbuf("oT", [128, 2048], F32)
    Tst = [P.sbuf("Tst%d" % d, [128, 128], F32) for d in range(2)]
    sbf_p = [Pool(P, "sbf%d" % d, 2, [128, 128], BF16) for d in range(2)]
    att_p = [Pool(P, "attm%d" % d, 2, [64, 64], BF16) for d in range(2)]
    for d in range(2):
        for (ah_, at__) in att_p[d].bufs:
            P.op("pool", lambda e, ah_=ah_: e.memset(ah_[:, :], 0.0), writes=[at__])
    gp_sig = Pool(P, "g_sig", 1, [128, 512], F32)
    gp_t1 = Pool(P, "g_t1", 1, [128, 512], F32)
    gp_lf = Pool(P, "g_lf", 2, [128, 512], F32)
    gp_omf = Pool(P, "g_omf", 1, [128, 512], F32)
    gp_ee = Pool(P, "g_ee", 1, [128, 512], F32)
    mo_p = Pool(P, "mo", 1, [128, 2048], BF16)
    qT_t, sgT_t = TL("qT", 4), TL("sgT", 4)
    vtok_t = TL("vtok", NCH)
    keT_t = [TL("keT%d_" % d, NCH) for d in range(2)]
    keF_t = [TL("keF%d_" % d, 32) for d in range(2)]
    qe_t = [TL("qe%d_" % d, 32) for d in range(2)]
    AB_t, Gs_t = T("ABt"), T("Gs")
    oT_t = TL("oT", 32)
    Tst_t = TL("Tst", 2)
    mout_t = T("mout")

    lb = P.sbuf("lb", [64, 2, 512], F32)
    oml = P.sbuf("oml", [64, 2, 512], F32)
    lbl_t, lb_t, oml_t, lsm_t = T("lbl"), T("lb"), T("oml"), T("lsm")
    lbl = keF[:].rearrange("p a b c -> p (a b c)").bitcast(F32)[0:64, 0:1536].rearrange("p (s n) -> p s n", s=3)
    lsm = qe[:].rearrange("p a b c -> p (a b c)").bitcast(F32)[0:64, 0:512]
    for d in range(2):
        P.dma("sp", lambda e, d=d: e.dma_start(out=lbl, in_=lbr[:, d * 1536:(d + 1) * 1536].rearrange("p (s n) -> p s n", s=3)), writes=[lbl_t])
        P.op("dve", lambda e: e.tensor_tensor(out=lsm, in0=lbl[:, 0, :], in1=lbl[:, 1, :], op=ALU.max), reads=[lbl_t], writes=[lsm_t])
        P.op("dve", lambda e: e.tensor_tensor(out=lsm, in0=lsm, in1=lbl[:, 2, :], op=ALU.max), reads=[lbl_t, lsm_t], writes=[lsm_t])
        P.op("dve", lambda e: e.tensor_tensor(out=lbl, in0=lbl, in1=lsm.unsqueeze(1).to_broadcast([64, 3, 512]),
                                              op=ALU.subtract), reads=[lbl_t, lsm_t], writes=[lbl_t])
        P.op("act", lambda e: e.activation(out=lbl, in_=lbl, func=AF.Exp), reads=[lbl_t], writes=[lbl_t])
        P.op("dve", lambda e: e.tensor_tensor(out=lsm, in0=lbl[:, 0, :], in1=lbl[:, 1, :], op=ALU.add), reads=[lbl_t], writes=[lsm_t])
        P.op("dve", lambda e: e.tensor_tensor(out=lsm, in0=lsm, in1=lbl[:, 2, :], op=ALU.add), reads=[lbl_t, lsm_t], writes=[lsm_t])
        P.op("dve", lambda e: e.reciprocal(out=lsm, in_=lsm), reads=[lsm_t], writes=[lsm_t])
        P.op("dve", lambda e, d=d: e.tensor_tensor(out=lb[:, d, :], in0=lbl[:, 0, :], in1=lsm, op=ALU.mult), reads=[lbl_t, lsm_t], writes=[lb_t])
    P.op("dve", lambda e: e.tensor_scalar(out=oml[:], in0=lb[:], scalar1=-1.0, scalar2=1.0, op0=ALU.mult, op1=ALU.add),
         reads=[lb_t], writes=[oml_t])

    if upto == 0.5:
        P.wait_all("sp", [lb_t, oml_t, idb_t, cw_t, gn_t, cst_t])
        return

    def load_wg(dst, dst_t, g0, ng, hd):
        for gi in range(ng):
            c0 = (g0 + gi) * 512 + hd * 128
            P.dma("pool", lambda e, gi=gi, c0=c0: e.dma_start(
                out=dst[:, :, gi * 128:(gi + 1) * 128], in_=w[:, c0:c0 + 128].rearrange("(k p) n -> p k n", p=128)), writes=[dst_t])

    wseq = [(0, 3, i) for i in range(4)] + [(5, 3, i) for i in range(4)]
    wtiles = {}

    def issue_w(i):
        wt_h, wt_t = wtm_p.get()
        load_wg(wt_h, wt_t, *wseq[i])
        wtiles[i] = (wt_h, wt_t)
    wf_h, wf_t = wfm_view, wfm_t
    if prefetch:
        issue_w(0)
        load_wg(wf_h, wf_t, 3, 2, 0)
    for hd in range(4):
        if not prefetch:
            issue_w(hd)
            load_wg(wf_h, wf_t, 3, 2, hd)
        wt_h, wt_t = wtiles[hd]
        for tb in range(4):
            for gi in range(2):
                ph, pt = pp.get()

                def mm(e, ph=ph, gi=gi, tb=tb, wf_h=wf_h):
                    for k in range(KC):
                        ins = e.matmul(ph[:, :], lhsT=wf_h[:, k, gi * 128:(gi + 1) * 128], rhs=hT[:, k, tb * 512:(tb + 1) * 512],
                                       start=(k == 0), stop=(k == KC - 1))
                    return ins
                P.op("pe", mm, reads=[wf_t] + hT_t, writes=[pt])
                if gi == 0:
                    P.op("dve", lambda e, ph=ph, tb=tb: e.tensor_copy(out=qT[:, tb * 512:(tb + 1) * 512], in_=ph[:, :]),
                         reads=[pt], writes=[qT_t[tb]])
                else:
                    P.op("act", lambda e, ph=ph, tb=tb: e.activation(out=sgT[:, tb * 512:(tb + 1) * 512], in_=ph[:, :], func=AF.Silu),
                         reads=[pt], writes=[sgT_t[tb]])
        if upto == 0.7:
            P.wait_all("sp", qT_t + sgT_t)
            return
        if prefetch:
            issue_w(hd + 1)
            if hd + 1 < 4:
                load_wg(wf_h, wf_t, 3, 2, hd + 1)
        P.op("pool", lambda e: e.memset(oT[:], 0.0), writes=oT_t)
        P.op("pool", lambda e: e.memset(Gs[:], 0.0), writes=[Gs_t])
        lb_h = lb[:, :, hd * 128:(hd + 1) * 128].unsqueeze(2).to_broadcast([64, 2, 2, 128])
        oml_h = oml[:, :, hd * 128:(hd + 1) * 128].unsqueeze(2).to_broadcast([64, 2, 2, 128])

        def v4(h):
            return h[0:64, :].rearrange("p (c d n) -> p d c n", d=2, c=2)

        for j in range(18):
            c0 = 2 * j
            pa, pa_t = pp.get()
            pb, pb_t = pp.get()

            def mm(e, pa=pa, pb=pb, c0=c0, wt_h=wt_h):
                for ci in range(2):
                    tok0 = (c0 + ci) * 64
                    for k in range(KC):
                        e.matmul(pa[0:64, ci * 256:(ci + 1) * 256], lhsT=hT[:, k, tok0:tok0 + 64], rhs=wt_h[:, k, 0:256],
                                 start=(k == 0), stop=(k == KC - 1))
                    for k in range(KC):
                        ins = e.matmul(pb[0:64, ci * 128:(ci + 1) * 128], lhsT=hT[:, k, tok0:tok0 + 64], rhs=wt_h[:, k, 256:384],
                                       start=(k == 0), stop=(k == KC - 1))
                return ins
            P.op("pe", mm, reads=[wt_t] + hT_t, writes=[pa_t, pb_t])
            P.op("act", lambda e, pb=pb, c0=c0: e.activation(out=vtok[:, c0:c0 + 2, :], in_=pb[0:64, 0:256].rearrange("p (c n) -> p c n", c=2),
                                                             func=AF.Copy), reads=[pb_t], writes=vtok_t[c0:c0 + 2])
            sg_h, sg_t = gp_sig.get()
            P.op("act", lambda e, pa=pa, sg_h=sg_h: e.activation(out=sg_h[0:64, :], in_=pa[0:64, :], func=AF.Sigmoid), reads=[pa_t], writes=[sg_t])
            t1_h, t1_t = gp_t1.get()
            P.op("dve", lambda e, sg_h=sg_h, t1_h=t1_h, oml_h=oml_h: e.tensor_tensor(out=v4(t1_h), in0=v4(sg_h), in1=oml_h, op=ALU.mult),
                 reads=[sg_t, oml_t], writes=[t1_t])
            P.op("pool", lambda e, t1_h=t1_h, lb_h=lb_h: e.tensor_tensor(out=v4(t1_h), in0=v4(t1_h), in1=lb_h, op=ALU.add),
                 reads=[t1_t, lb_t], writes=[t1_t])
            lf_h, lf_t = gp_lf.get()
            P.op("act", lambda e, t1_h=t1_h, lf_h=lf_h: e.activation(out=lf_h[0:64, :], in_=t1_h[0:64, :], func=AF.Ln), reads=[t1_t], writes=[lf_t])
            om_h, om_t = gp_omf.get()
            P.op("dve", lambda e, t1_h=t1_h, om_h=om_h: e.tensor_scalar(out=om_h[0:64, :], in0=t1_h[0:64, :], scalar1=-1.0, scalar2=1.0,
                                                                        op0=ALU.mult, op1=ALU.add), reads=[t1_t], writes=[om_t])
            if upto == 0.8:
                P.wait_all("sp", [om_t, lf_t] + vtok_t[0:2])
                return
            pe2, pe2_t = pp.get()

            def mm2(e, pe2=pe2, lf_h=lf_h):
                for d in range(2):
                    for ci in range(2):
                        o0 = ci * 256 + d * 128
                        ins = e.matmul(pe2[0:64, o0:o0 + 128], lhsT=pm[:, d, 0:64], rhs=lf_h[0:64, o0:o0 + 128], start=True, stop=True)
                return ins
            P.op("pe", mm2, reads=[cst_t, lf_t], writes=[pe2_t])
            ee_h, ee_t = gp_ee.get()
            P.op("act", lambda e, pe2=pe2, ee_h=ee_h: e.activation(out=ee_h[0:64, :], in_=pe2[0:64, :], func=AF.Exp, scale=-1.0), reads=[pe2_t], writes=[ee_t])
            P.op("dve", lambda e, ee_h=ee_h, om_h=om_h, c0=c0: e.tensor_tensor(out=keT[:, :, c0:c0 + 2, :], in0=v4(ee_h), in1=v4(om_h), op=ALU.mult),
                 reads=[ee_t, om_t], writes=keT_t[0][c0:c0 + 2] + keT_t[1][c0:c0 + 2])
            pe1, pe1_t = pp.get()

            def mm1(e, pe1=pe1, lf_h=lf_h):
                for d in range(2):
                    for ci in range(2):
                        o0 = (d * 2 + ci) * 128
                        ins = e.matmul(pe1[:, o0:o0 + 66], lhsT=v4(lf_h)[:, d, ci, :], rhs=pm[:, d, :], start=True, stop=True)
                return ins
            P.op("pe", mm1, reads=[cst_t, lf_t], writes=[pe1_t])
            pe1v = pe1[:, 0:512].rearrange("p (d c n) -> p d c n", d=2, c=2)
            P.op("dve", lambda e, pe1v=pe1v, c0=c0: e.tensor_copy(out=ABt[:, :, c0:c0 + 2, :], in_=pe1v[:, :, :, 64:66]), reads=[pe1_t], writes=[AB_t])
            if upto == 0.9:
                P.wait_all("sp", [AB_t] + keT_t[0][0:2])
                return
            if j < 16:
                eq_h, eq_t = gp_eq.get()
                eqv = eq_h[:, :].rearrange("p (d c n) -> p d c n", d=2, c=2)
                P.op("dve", lambda e, pe1v=pe1v, eqv=eqv: e.tensor_copy(out=eqv, in_=pe1v[:, :, :, 0:64]), reads=[pe1_t], writes=[eq_t])
                P.op("act", lambda e, eq_h=eq_h: e.activation(out=eq_h[:, :], in_=eq_h[:, :], func=AF.Exp), reads=[eq_t], writes=[eq_t])
                if upto in (0.93, 0.931, 0.932, 0.933):
                    mo_h, mo_t = mo_p.get()
                    P.op("dve", lambda e, mo_h=mo_h, eq_h=eq_h: e.tensor_copy(out=mo_h[:, 0:256], in_=eq_h[:, :]), reads=[eq_t], writes=[mo_t])
                    P.op("dve", lambda e, mo_h=mo_h, lf_h=lf_h: e.tensor_copy(out=mo_h[0:64, 256:768], in_=lf_h[0:64, :]), reads=[lf_t], writes=[mo_t])
                    P.op("dve", lambda e, mo_h=mo_h: e.tensor_copy(out=mo_h[0:64, 768:1280].rearrange("p (d n) -> p d n", d=2), in_=lb[:, :, 0:256]), reads=[lb_t], writes=[mo_t])
                    P.dma("sp", lambda e, mo_h=mo_h: e.dma_start(out=m_out[0:128, 0:2048], in_=mo_h[:, :]), reads=[mo_t], writes=[T("dbg")], semtile=mo_t)
                    P.wait_all("sp", [mo_t])
                    return
                qv = qT[:, c0 * 64:(c0 + 2) * 64].rearrange("p (c n) -> p c n", c=2)
                for d in range(2):
                    P.op("dve", lambda e, eqv=eqv, qv=qv, c0=c0, d=d: e.tensor_tensor(out=qe[:, d, c0:c0 + 2, :], in0=eqv[:, d, :, :], in1=qv, op=ALU.mult),
                         reads=[eq_t, qT_t[j // 4]], writes=qe_t[d][c0:c0 + 2])
                if upto == 0.95:
                    P.wait_all("sp", qe_t[0][0:2])
                    return
                pk, pk_t = pp.get()

                def mmk(e, pk=pk, c0=c0):
                    for d in range(2):
                        for ci in range(2):
                            o0 = (d * 2 + ci) * 64
                            ins = e.matmul(pk[:, o0:o0 + 64], lhsT=keT[:, d, c0 + ci, :], rhs=idb[:, :], start=True, stop=True)
                    return ins
                P.op("pe", mmk, reads=[idb_t] + keT_t[0][c0:c0 + 2] + keT_t[1][c0:c0 + 2], writes=[pk_t])
                for d in range(2):
                    P.op("act", lambda e, pk=pk, c0=c0, d=d: e.activation(out=keF[:, d, c0:c0 + 2, :], in_=pk[:, d * 128:(d + 1) * 128].rearrange("p (c n) -> p c n", c=2),
                                                                      func=AF.Copy), reads=[pk_t], writes=keF_t[d][c0:c0 + 2])
        if upto == 1:
            dt_ = T("dbg")
            allt = qe_t[0] + qe_t[1] + keF_t[0] + keF_t[1]
            for i_, (src, d) in enumerate(((qe, 0), (keF, 0), (qe, 1), (keF, 1))):
                P.dma("sp", lambda e, src=src, d=d, i_=i_: e.dma_start(out=m_out[i_ * 128:(i_ + 1) * 128, :], in_=src[:, d, :, :].rearrange("p c n -> p (c n)")),
                      reads=allt, writes=[dt_], semtile=dt_)
            P.wait_all("sp", [dt_])
            return
        A_ = ABt[:, :, :, 0]
        B_ = ABt[:, :, :, 1]
        for (d, o0, o1, b0) in ((0, 32, 35, 33), (0, 35, 36, 0), (0, 0, 31, 1), (1, 33, 36, 32), (1, 32, 33, 31), (1, 1, 32, 0)):
            n = o1 - o0
            P.op("dve", lambda e, d=d, o0=o0, o1=o1, b0=b0, n=n: e.tensor_tensor(out=Gs[:, d, o0:o1], in0=A_[:, d, o0:o1], in1=B_[:, d, b0:b0 + n], op=ALU.add),
                 reads=[AB_t], writes=[Gs_t])
        P.op("act", lambda e: e.activation(out=Gs[:], in_=Gs[:], func=AF.Exp), reads=[Gs_t], writes=[Gs_t])
        seqs = (SEQ_F, SEQ_B)
        pend = []

        def stage1(i, d):
            c = seqs[d][i]
            prev = seqs[d][i - 1] if i > 0 else None
            pb_, pb_t = pp.get()
            info = None
            if c < 32:
                if d == 0:
                    blks = ((slice(0, 64), slice(32, 64)), (slice(0, 32), slice(0, 32)))
                else:
                    blks = ((slice(0, 64), slice(0, 32)), (slice(32, 64), slice(32, 64)))

                def mm1(e, pb_=pb_, d=d, c=c, blks=blks):
                    e.matmul(pb_[:, 0:128], lhsT=keT[:, d, c, :], rhs=vtok[:, c, :], start=True, stop=True)
                    for (ss, tt_) in blks:
                        ins = e.matmul(pb_[ss, 128 + tt_.start:128 + tt_.stop], lhsT=keF[:, d, c, ss], rhs=qe[:, d, c, tt_], start=True, stop=True)
                    return ins
                P.op("pe", mm1, reads=[keT_t[d][c], vtok_t[c], keF_t[d][c], qe_t[d][c]], writes=[pb_t])
                sb_h, sb_t = sbf_p[d].get()
                P.op("act", lambda e, sb_h=sb_h, d=d, prev=prev: e.activation(out=sb_h[:, :], in_=Tst[d][:, :], func=AF.Copy, scale=Gs[:, d, prev:prev + 1]),
                     reads=[Tst_t[d], Gs_t], writes=[sb_t])
                am_h, am_t = att_p[d].get()
                for (ss, tt_) in blks:
                    P.op("dve", lambda e, pb_=pb_, am_h=am_h, d=d, ss=ss, tt_=tt_: e.tensor_tensor(
                        out=am_h[ss, tt_], in0=pb_[ss, 128 + tt_.start:128 + tt_.stop], in1=mk[ss, d, tt_], op=ALU.mult), reads=[pb_t, cst_t], writes=[am_t])
                info = (pb_, pb_t, sb_h, sb_t, am_h, am_t, d, c)
            else:
                P.op("pe", lambda e, pb_=pb_, d=d, c=c: e.matmul(pb_[:, 0:128], lhsT=keT[:, d, c, :], rhs=vtok[:, c, :], start=True, stop=True),
                     reads=[keT_t[d][c], vtok_t[c]], writes=[pb_t])
            if i == 0:
                P.op("dve", lambda e, pb_=pb_, d=d: e.tensor_copy(out=Tst[d][:, :], in_=pb_[:, 0:128]), reads=[pb_t], writes=[Tst_t[d]])
            else:
                P.op("dve", lambda e, pb_=pb_, d=d, prev=prev: e.scalar_tensor_tensor(
                    out=Tst[d][:, :], in0=Tst[d][:, :], scalar=Gs[:, d, prev:prev + 1], in1=pb_[:, 0:128], op0=ALU.mult, op1=ALU.add),
                    reads=[pb_t, Tst_t[d], Gs_t], writes=[Tst_t[d]])
            return info

        def stage2(info):
            pb_, pb_t, sb_h, sb_t, am_h, am_t, d, c = info

            def mmo(e):
                e.matmul(pb_[:, 192:256], lhsT=sb_h[:, :], rhs=qe[:, d, c, :], start=True, stop=False)
                return e.matmul(pb_[:, 192:256], lhsT=vtok[:, c, :], rhs=am_h[:, :], start=False, stop=True)
            P.op("pe", mmo, reads=[sb_t, qe_t[d][c], vtok_t[c], am_t], writes=[pb_t])
            P.op("dve", lambda e: e.tensor_tensor(out=oT[:, c * 64:(c + 1) * 64], in0=pb_[:, 192:256], in1=oT[:, c * 64:(c + 1) * 64], op=ALU.add),
                 reads=[pb_t, oT_t[c]], writes=[oT_t[c]])

        for i in range(NCH):
            cur = [stage1(i, d) for d in range(2)]
            for info in pend:
                if info is not None:
                    stage2(info)
            pend = cur
        for info in pend:
            if info is not None:
                stage2(info)
        if upto == 2:
            mo_h, mo_t = mo_p.get()
            P.op("act", lambda e, mo_h=mo_h: e.activation(out=mo_h[:, :], in_=oT[:, :], func=AF.Copy), reads=oT_t, writes=[mo_t])
            P.dma("sp", lambda e, mo_h=mo_h: e.dma_start(out=m_out[0:128, :], in_=mo_h[:, :]), reads=[mo_t], writes=[T("dbg")], semtile=mo_t)
            P.wait_all("sp", [mo_t])
            return
        mo_h, mo_t = mo_p.get()
        for tb in range(4):
            sl = slice(tb * 512, (tb + 1) * 512)
            sq, sqt = sqpool.get()
            P.op("act", lambda e, sq=sq, sl=sl: e.activation(out=sq[:, 0:512], in_=oT[:, sl], func=AF.Square), reads=oT_t[tb * 8:(tb + 1) * 8], writes=[sqt])
            ph, pt = pp.get()
            P.op("pe", lambda e, ph=ph, sq=sq: e.matmul(ph[:, :], lhsT=ones[:], rhs=sq[:, 0:512], start=True, stop=True), reads=[sqt, ones_t], writes=[pt])
            th, tt = tmpp.get()
            P.op("act", lambda e, ph=ph, th=th: e.activation(out=th[:, 0:512], in_=ph[:, :], func=AF.Sqrt, scale=1.0 / 128, bias=epsb[:, 0:1]),
                 reads=[pt, eps_t], writes=[tt])
            P.op("dve", lambda e, th=th: e.reciprocal(out=th[:, 0:512], in_=th[:, 0:512]), reads=[tt], writes=[tt])
            P.op("dve", lambda e, th=th, sl=sl, hd=hd: e.scalar_tensor_tensor(out=th[:, 0:512], in0=oT[:, sl], scalar=gn[:, hd:hd + 1], in1=th[:, 0:512],
                                                                              op0=ALU.mult, op1=ALU.mult), reads=[tt, gn_t] + oT_t[tb * 8:(tb + 1) * 8], writes=[tt])
            P.op("pool", lambda e, th=th, sl=sl, mo_h=mo_h: e.tensor_tensor(out=mo_h[:, sl], in0=th[:, 0:512], in1=sgT[:, sl], op=ALU.mult),
                 reads=[tt, sgT_t[tb]], writes=[mo_t])
        P.dma("sp", lambda e, mo_h=mo_h, hd=hd: e.dma_start(out=m_out[mrow[0] + hd * 128:mrow[0] + (hd + 1) * 128, :], in_=mo_h[:, :]), reads=[mo_t], writes=[mout_t], semtile=mo_t)

    if upto == 3:
        P.wait_all("sp", [b[1] for b in mo_p.bufs])
        return
    cv_g, cv_t, cv_a = gp_sig, gp_t1, gp_lf
    for cc in range(4):
        if not prefetch:
            issue_w(4 + cc)
        elif cc + 1 < 4:
            issue_w(4 + cc + 1)
        wt_h, wt_t = wtiles[4 + cc]
        mo_h, mo_t = mo_p.get()
        for tb in range(4):
            sl = slice(tb * 512, (tb + 1) * 512)
            pss = []
            for gi in range(3):
                ph, pt = pp.get()

                def mm(e, ph=ph, gi=gi, sl=sl, wt_h=wt_h):
                    for k in range(KC):
                        ins = e.matmul(ph[:, :], lhsT=wt_h[:, k, gi * 128:(gi + 1) * 128], rhs=hT[:, k, sl], start=(k == 0), stop=(k == KC - 1))
                    return ins
                P.op("pe", mm, reads=[wt_t] + hT_t, writes=[pt])
                pss.append((ph, pt))
            (pu, pu_t), (pgb, pgb_t), (pgc, pgc_t) = pss
            g_h, g_t = cv_g.get()
            P.op("act", lambda e, pgc=pgc, g_h=g_h: e.activation(out=g_h[:, :], in_=pgc[:, :], func=AF.Copy), reads=[pgc_t], writes=[g_t])
            t_h, t_t = cv_t.get()
            P.op("dve", lambda e, pu=pu, g_h=g_h, t_h=t_h: e.tensor_tensor(out=t_h[:, :], in0=pu[:, :], in1=g_h[:, :], op=ALU.mult), reads=[pu_t, g_t], writes=[t_t])
            a_h, a_t = cv_a.get()
            P.op("act", lambda e, t_h=t_h, a_h=a_h, cc=cc: e.activation(out=a_h[:, :], in_=t_h[:, :], func=AF.Copy, scale=cw[:, cc * 3 + 1:cc * 3 + 2]),
                 reads=[t_t, cw_t], writes=[a_t])
            tv = t_h[:, :].rearrange("p (r j) -> p r j", j=64)
            av = a_h[:, :].rearrange("p (r j) -> p r j", j=64)
            P.op("dve", lambda e, tv=tv, av=av, cc=cc: e.scalar_tensor_tensor(out=av[:, :, 1:64], in0=tv[:, :, 0:63], scalar=cw[:, cc * 3:cc * 3 + 1], in1=av[:, :, 1:64],
                                                                              op0=ALU.mult, op1=ALU.add), reads=[t_t, a_t, cw_t], writes=[a_t])
            P.op("dve", lambda e, tv=tv, av=av, cc=cc: e.scalar_tensor_tensor(out=av[:, :, 0:63], in0=tv[:, :, 1:64], scalar=cw[:, cc * 3 + 2:cc * 3 + 3], in1=av[:, :, 0:63],
                                                                              op0=ALU.mult, op1=ALU.add), reads=[t_t, a_t, cw_t], writes=[a_t])
            P.op("dve", lambda e, pgb=pgb, a_h=a_h, mo_h=mo_h, sl=sl: e.tensor_tensor(out=mo_h[:, sl], in0=pgb[:, :], in1=a_h[:, :], op=ALU.mult),
                 reads=[pgb_t, a_t], writes=[mo_t])
        P.dma("sp", lambda e, mo_h=mo_h, cc=cc: e.dma_start(out=m_out[mrow[1] + cc * 128:mrow[1] + (cc + 1) * 128, :], in_=mo_h[:, :]),
              reads=[mo_t], writes=[mout_t], semtile=mo_t)
    P.wait_all("sp", [b[1] for b in mo_p.bufs] + hsv_t)
    P.wait_all("act", hsv_t)


def build_mix0(upto=99):
    nc = bass.Bass("TRN2", target_bir_lowering=False)
    xT = dram_in(nc, "i_xT", [D, NTT])
    scal_d = dram_in(nc, "i_scal", [128, 5 * KC])
    w = dram_in(nc, "i_w", [D, 4096])
    lbr = dram_in(nc, "i_lb", [64, 3072])
    cw_d = dram_in(nc, "i_cw", [128, 12])
    gn_d = dram_in(nc, "i_gn", [128, 4])
    cst_d = dram_in(nc, "i_cst", [64, 324])
    m_out = dram_out(nc, "o_m", [1024, 2048], BF16)
    P = Prog(nc)
    emit_mix0(P, nc, xT, scal_d, w, lbr, cw_d, gn_d, cst_d, m_out, upto)
    P.emit()
    P.close()
    return nc


L = 2048
NF = 4096


def hy_tables():
    t = np.arange(L, dtype=np.float64)
    k = np.arange(L, dtype=np.float64) + 0.5
    ang = 2.0 * np.pi * np.outer(t, k) / NF
    Cm, Sm = np.cos(ang), np.sin(ang)
    def fwd(M):
        return np.ascontiguousarray(M.reshape(16, 128, 16, 128).transpose(2, 1, 0, 3)).astype(NPBF)
    Ci = (2.0 / NF) * Cm.T.reshape(16, 128, 4, 512)
    Si = (2.0 / NF) * Sm.T.reshape(16, 128, 4, 512)
    inv = np.concatenate([Ci, Si], axis=0).transpose(2, 1, 0, 3)
    return fwd(Cm), fwd(Sm), np.ascontiguousarray(inv).astype(NPBF)


def hy_consts(hh):
    pos = np.arange(L, dtype=np.float32)
    t = np.linspace(0.0, 1.0, L, dtype=np.float32)
    w = (2.0 * np.pi * pos / L).astype(np.float32)
    bands = np.linspace(1e-4, 15.0, 16, dtype=np.float32)
    ang = w[:, None] * bands[None, :]
    zemb = np.concatenate([t[:, None], np.cos(ang), -np.sin(ang)], axis=-1).astype(np.float32)
    deltas = np.abs(np.linspace(np.log(1e-2) / 1.5, np.log(1e-2) / 0.3, D, dtype=np.float32))
    win = np.exp(-t[:, None] * deltas[None, hh * 1024:(hh + 1) * 1024]).astype(np.float32)
    return np.ascontiguousarray(zemb.T), win


def emit_hy(P, nc, hT_d, w, sw_d, skip_d, fw1_d, fw23_d, fvec_d, fw4_d, zemb_d, win_d, id_d, cf_d, sf_d, inv_d, z3_out, upto=99):
    PI = float(np.pi)
    x1s = nc.dram_tensor(P.pfx + "s_x1", [1024, L], BF16).ap()
    x2s = nc.dram_tensor(P.pfx + "s_x2", [1024, L], BF16).ap()
    zfs = [nc.dram_tensor(P.pfx + "s_zf%d" % i, [1024, L], BF16).ap() for i in range(2)]
    x1s_t, x2s_t = TL("sx1_", 8), TL("sx2_", 8)
    zfs_t = [TL("szf%d_" % i, 8) for i in range(2)]

    hT = P.sbuf("hT", [128, KC, L], BF16)
    hT_t = TL("hT", KC)
    Yv = hT[:].rearrange("p a (b n) -> p (a b) n", b=2)
    big = P.sbuf("big64", [128, 2, 16384], BF16)
    big_t = TL("big", 2)
    z_tm = P.sbuf("z_tm", [128, 16, 1024], BF16)
    ztm_t = TL("ztm", 16)
    arena = P.sbuf("arena", [128, 16384], BF16)

    def carve(off_kib, nbytes, dt, pat=None, **kw):
        a = arena[:, off_kib * 512:off_kib * 512 + nbytes // 2]
        if dt == F32:
            a = a.bitcast(F32)
        return a.rearrange(pat, **kw) if pat else a
    wpool = Pool.views("hw", [carve(0, 8192, BF16, "p (k n) -> p k n", k=KC), carve(8, 8192, BF16, "p (k n) -> p k n", k=KC)])
    pp = Pool(P, "pp", 8, [128, 512], F32, psum=True)
    idf = P.sbuf("idf", [128, 128], F32)
    idb = P.sbuf("idb", [128, 128], BF16)
    sw = P.sbuf("sw", [128, 72], F32)
    skip = P.sbuf("skip", [128, 16], F32)
    onesf = P.sbuf("onesf", [128, 1], F32)
    idf_t, idb_t, sw_t, skip_t, onesf_t = T("idf"), T("idb"), T("sw"), T("skip"), T("onesf")
    P.dma("sp", lambda e: e.dma_start(out=idf[:], in_=id_d), writes=[idf_t])
    P.op("dve", lambda e: e.tensor_copy(out=idb[:], in_=idf[:]), reads=[idf_t], writes=[idb_t])
    P.dma("sp", lambda e: e.dma_start(out=sw[:], in_=sw_d), writes=[sw_t])
    P.dma("sp", lambda e: e.dma_start(out=skip[:], in_=skip_d), writes=[skip_t])
    P.op("pool", lambda e: e.memset(onesf[:], 1.0), writes=[onesf_t])
    for k in range(KC):
        q = "sp"
        P.dma(q, lambda e, k=k: e.dma_start(out=hT[:, k, :], in_=hT_d[k * 128:(k + 1) * 128, :]), writes=[hT_t[k]])

    cvo = Pool.views("cvo", [carve(16, 4096, BF16), carve(20, 4096, BF16)])
    cva = Pool.views("cva", [carve(24, 2048, F32), carve(26, 2048, F32)])
    wnext = load_w_block(P, wpool, w, KC, 0, 256)
    for blk in range(12):
        wh, wt = wnext
        if blk + 1 < 12:
            wnext = load_w_block(P, wpool, w, KC, (blk + 1) * 256, 256)
        for ci in range(2):
            j = blk * 2 + ci
            grp, cc = j // 8, j % 8
            oh, ot = cvo.get()
            for tb in range(4):
                sl = slice(tb * 512, (tb + 1) * 512)
                ph, pt = pp.get()

                def mm(e, ph=ph, wh=wh, ci=ci, sl=sl):
                    for k in range(KC):
                        ins = e.matmul(ph[:, :], lhsT=wh[:, k, ci * 128:(ci + 1) * 128], rhs=hT[:, k, sl], start=(k == 0), stop=(k == KC - 1))
                    return ins
                P.op("pe", mm, reads=[wt] + hT_t, writes=[pt])
                ah, at = cva.get()
                P.op("act", lambda e, ph=ph, ah=ah, j=j: e.activation(out=ah[:, :], in_=ph[:, :], func=AF.Copy, scale=sw[:, j * 3 + 1:j * 3 + 2]),
                     reads=[pt, sw_t], writes=[at])
                pv = ph[:, :].rearrange("p (r c) -> p r c", c=64)
                av = ah[:, :].rearrange("p (r c) -> p r c", c=64)
                P.op("dve", lambda e, pv=pv, av=av, j=j: e.scalar_tensor_tensor(out=av[:, :, 1:64], in0=pv[:, :, 0:63], scalar=sw[:, j * 3:j * 3 + 1], in1=av[:, :, 1:64],
                                                                                op0=ALU.mult, op1=ALU.add), reads=[pt, at, sw_t], writes=[at])
                P.op("dve", lambda e, pv=pv, av=av, j=j: e.scalar_tensor_tensor(out=av[:, :, 0:63], in0=pv[:, :, 1:64], scalar=sw[:, j * 3 + 2:j * 3 + 3], in1=av[:, :, 0:63],
                                                                                op0=ALU.mult, op1=ALU.add), reads=[pt, at, sw_t], writes=[at])
                P.op("pool", lambda e, ah=ah, oh=oh, sl=sl: e.tensor_copy(out=oh[:, sl], in_=ah[:, :]), reads=[at], writes=[ot])
            dst, dst_t = ((x1s, x1s_t), (x2s, x2s_t), (zfs[0], zfs_t[0]))[grp]
            P.dma("sp", lambda e, oh=oh, dst=dst, cc=cc: e.dma_start(out=dst[cc * 128:(cc + 1) * 128, :], in_=oh[:, :]), reads=[ot], writes=[dst_t[cc]], semtile=ot)
            if grp == 2:
                for tq in range(4):
                    ph, pt = pp.get()

                    def mmt(e, ph=ph, oh=oh, tq=tq):
                        for i in range(4):
                            tc = tq * 4 + i
                            ins = e.matmul(ph[:, i * 128:(i + 1) * 128], lhsT=oh[:, tc * 128:(tc + 1) * 128], rhs=idb[:, :], start=True, stop=True)
                        return ins
                    P.op("pe", mmt, reads=[ot, idb_t], writes=[pt])
                    P.op("dve", lambda e, ph=ph, tq=tq, cc=cc: e.tensor_copy(out=z_tm[:, tq * 4:(tq + 1) * 4, cc * 128:(cc + 1) * 128],
                                                                             in_=ph[:, :].rearrange("p (i n) -> p i n", i=4)), reads=[pt], writes=ztm_t[tq * 4:(tq + 1) * 4])

    fw1 = P.sbuf("fw1", [33, 64], F32)
    fw23 = P.sbuf("fw23", [64, 128], F32)
    fvec = P.sbuf("fvec", [64, 5], F32)
    P.barrier()
    zemb = carve(0, 8192, F32)[0:33, :]
    hda = P.sbuf("hda", [64, L], F32)
    hdb = carve(8, 8192, F32)[0:64, :]
    fw1_t, fw23_t, fvec_t, fw4_t, zemb_t, hda_t, hdb_t = T("fw1"), T("fw23"), T("fvec"), T("fw4"), T("zemb"), T("hda"), T("hdb")
    ep_o = None
    P.dma("sp", lambda e: e.dma_start(out=fw1[:], in_=fw1_d), writes=[fw1_t])
    P.dma("sp", lambda e: e.dma_start(out=fw23[:], in_=fw23_d), writes=[fw23_t])
    P.dma("sp", lambda e: e.dma_start(out=fvec[:], in_=fvec_d), writes=[fvec_t])
    P.dma("sp", lambda e: e.dma_start(out=zemb, in_=zemb_d), writes=[zemb_t])
    f2pi = P.sbuf("f2pi", [64, 1], F32)
    f2pi_t, ri_t, rf_t = T("f2pi"), T("rint_i"), T("rint_f")
    P.op("dve", lambda e: e.tensor_scalar(out=f2pi[:, :], in0=fvec[:, 3:4], scalar1=1.0 / (2.0 * PI), scalar2=None, op0=ALU.mult), reads=[fvec_t], writes=[f2pi_t])
    rint_i = carve(16, 2048, F32)[0:64, :].bitcast(mybir.dt.int32)
    rint_f = carve(18, 2048, F32)[0:64, :]
    layers = ((fw1[:, :], fw1_t, zemb, zemb_t, 33, hda, hda_t, 0), (fw23[:, 0:64], fw23_t, hda, hda_t, 64, hdb, hdb_t, 1),
              (fw23[:, 64:128], fw23_t, hdb, hdb_t, 64, hda, hda_t, 2))
    for (wl, wl_t, src, src_t, kin, dst, dst_t, li) in layers:
        for tb in range(4):
            sl = slice(tb * 512, (tb + 1) * 512)
            ph, pt = pp.get()
            P.op("pe", lambda e, ph=ph, wl=wl, src=src, kin=kin, sl=sl: e.matmul(ph[0:64, :], lhsT=wl, rhs=src[0:kin, sl], start=True, stop=True),
                 reads=[wl_t, src_t], writes=[pt])
            P.op("dve", lambda e, ph=ph, dst=dst, sl=sl, li=li: e.tensor_scalar(out=dst[:, sl], in0=ph[0:64, :], scalar1=fvec[:, li:li + 1], scalar2=f2pi[:, 0:1],
                                                                               op0=ALU.add, op1=ALU.mult), reads=[pt, fvec_t, f2pi_t], writes=[dst_t])
            P.op("dve", lambda e, dst=dst, sl=sl: e.tensor_copy(out=rint_i[:, :], in_=dst[:, sl]), reads=[dst_t], writes=[ri_t])
            P.op("dve", lambda e: e.tensor_copy(out=rint_f[:, :], in_=rint_i[:, :]), reads=[ri_t], writes=[rf_t])
            P.op("dve", lambda e, dst=dst, sl=sl: e.tensor_tensor(out=dst[:, sl], in0=dst[:, sl], in1=rint_f[:, :], op=ALU.subtract), reads=[dst_t, rf_t], writes=[dst_t])
            P.op("act", lambda e, dst=dst, sl=sl: e.activation(out=dst[:, sl], in_=dst[:, sl], func=AF.Sin, scale=2.0 * PI * (1.0 - 1e-6)), reads=[dst_t], writes=[dst_t])
    hd3, hd3_t = hda, hda_t
    if upto == 0:
        dbg_t = T("dbg")
        P.barrier()
        th, tt = carve(16, 4096, BF16), T("dbgt")
        P.op("dve", lambda e, th=th: e.tensor_copy(out=th[0:64, :], in_=hd3[:, :]), reads=[hd3_t], writes=[tt])
        P.dma("sp", lambda e, th=th: e.dma_start(out=z3_out[0:64, :], in_=th[0:64, :]), reads=[tt], writes=[dbg_t], semtile=tt)
        P.wait_all("sp", [tt] + x1s_t + x2s_t + zfs_t[0])
        return

    acc_t = T("nacc")
    rnorm = P.sbuf("rnorm", [128, 16], F32)
    rnorm_t = T("rnorm")
    Av = big[:, 0, :].rearrange("p (c n) -> p c n", c=16)
    Bv = big[:, 1, :].rearrange("p (c n) -> p c n", c=16)
    out_t = T("z3out")
    for o in range(2):
        P.barrier()
        fw4 = carve(0, 8192, F32)
        P.dma("sp", lambda e, fw4=fw4, o=o: e.dma_start(out=fw4[0:64, :], in_=fw4_d[:, o * 2048:(o + 1) * 2048]), writes=[fw4_t])
        wn_p = Pool.views("wn%d" % o, [carve(8, 4096, F32), carve(12, 4096, F32)])
        fwv_p = Pool.views("fwv%d" % o, [carve(16, 4096, F32)])
        bwv_p = Pool.views("bwv%d" % o, [carve(20, 4096, F32)])
        abs_p = Pool.views("abs%d" % o, [carve(24, 4096, F32)])
        acc = carve(28, 4096, F32)
        for pc in range(16):
            wnh, wnt = wn_p.get()
            P.dma("sp", lambda e, wnh=wnh, pc=pc: e.dma_start(out=wnh[:, :], in_=win_d[pc * 128:(pc + 1) * 128, :]), writes=[wnt])
            vals = []
            for side, pool_ in ((0, fwv_p), (1, bwv_p)):
                vh, vt = pool_.get()
                for cb in range(2):
                    ph, pt = pp.get()
                    c0 = side * 1024 + cb * 512
                    P.op("pe", lambda e, ph=ph, pc=pc, c0=c0, fw4=fw4: e.matmul(ph[:, :], lhsT=hd3[:, pc * 128:(pc + 1) * 128], rhs=fw4[0:64, c0:c0 + 512], start=True, stop=True),
                         reads=[hd3_t, fw4_t], writes=[pt])
                    P.op("dve", lambda e, ph=ph, vh=vh, wnh=wnh, cb=cb: e.tensor_tensor(out=vh[:, cb * 512:(cb + 1) * 512], in0=ph[:, :], in1=wnh[:, cb * 512:(cb + 1) * 512], op=ALU.mult),
                         reads=[pt, wnt], writes=[vt])
                vals.append((vh, vt))
            (fh, ft), (bh, bt) = vals
            P.op("pool", lambda e, fh=fh, bh=bh, pc=pc: e.tensor_tensor(out=Av[:, pc, :], in0=fh[:, :], in1=bh[:, :], op=ALU.add), reads=[ft, bt], writes=[big_t[0]])
            P.op("pool", lambda e, fh=fh, bh=bh, pc=pc: e.tensor_tensor(out=Bv[:, pc, :], in0=fh[:, :], in1=bh[:, :], op=ALU.subtract), reads=[ft, bt], writes=[big_t[1]])
            abh, abt = abs_p.get()
            P.op("act", lambda e, fh=fh, abh=abh: e.activation(out=abh[:, :], in_=fh[:, :], func=AF.Abs), reads=[ft], writes=[abt])
            P.op("act", lambda e, bh=bh: e.activation(out=bh[:, :], in_=bh[:, :], func=AF.Abs), reads=[bt], writes=[bt])
            if pc == 0:
                P.op("dve", lambda e, fh=fh, bh=bh, abh=abh: e.tensor_tensor(out=abh[:, :], in0=abh[:, :], in1=bh[:, :], op=ALU.add), reads=[abt, bt], writes=[abt])
                P.op("act", lambda e, abh=abh: e.activation(out=abh[0:1, :], in_=Av[0:1, 0, :], func=AF.Abs), reads=[big_t[0], abt], writes=[abt])
                P.op("dve", lambda e, abh=abh: e.tensor_copy(out=acc[:, :], in_=abh[:, :]), reads=[abt], writes=[acc_t])
            else:
                P.op("dve", lambda e, fh=fh, bh=bh, abh=abh: e.tensor_tensor(out=abh[:, :], in0=abh[:, :], in1=bh[:, :], op=ALU.add), reads=[abt, bt], writes=[abt])
                P.op("dve", lambda e, abh=abh: e.tensor_tensor(out=acc[:, :], in0=acc[:, :], in1=abh[:, :], op=ALU.add), reads=[abt, acc_t], writes=[acc_t])
        ph, pt = pp.get()

        def mmn(e, ph=ph):
            for cc in range(8):
                ins = e.matmul(ph[:, cc:cc + 1], lhsT=acc[:, cc * 128:(cc + 1) * 128], rhs=onesf[:, 0:1], start=True, stop=True)
            return ins
        P.op("pe", mmn, reads=[acc_t, onesf_t], writes=[pt])
        P.op("dve", lambda e, ph=ph, o=o: e.reciprocal(out=rnorm[:, o * 8:(o + 1) * 8], in_=ph[:, 0:8]), reads=[pt], writes=[rnorm_t])
        if upto == 1:
            P.barrier()
            th, tt = carve(0, 4096, BF16), T("dbgt")
            P.op("dve", lambda e, th=th: e.tensor_copy(out=th[:, 0:1024], in_=Av[:, 0, :]), reads=[big_t[0]], writes=[tt])
            P.op("dve", lambda e, th=th: e.tensor_copy(out=th[:, 1024:1032], in_=rnorm[:, 0:8]), reads=[rnorm_t], writes=[tt])
            P.dma("sp", lambda e, th=th: e.dma_start(out=z3_out[0:128, :], in_=th[:, :]), reads=[tt], writes=[T("dbg")], semtile=tt)
            P.wait_all("sp", [tt] + x1s_t + x2s_t + zfs_t[0])
            return
        P.barrier()
        tab_p = Pool.views("ftab%d" % o, [carve(0, 8192, BF16, "p (a c n) -> p a c n", a=2, c=16), carve(8, 8192, BF16, "p (a c n) -> p a c n", a=2, c=16)])
        hsb_p = Pool.views("hsb%d" % o, [carve(16, 8192, F32, "p (a n) -> p a n", a=2)])
        tm_p = Pool.views("tmul%d" % o, [carve(24, 8192, F32, "p (a n) -> p a n", a=2)])
        for kc in range(16):
            th_, tt_ = tab_p.get()
            P.dma("sp", lambda e, th_=th_, kc=kc: e.dma_start(out=th_[:, 0, :, :], in_=cf_d[kc]), writes=[tt_])
            P.dma("sp", lambda e, th_=th_, kc=kc: e.dma_start(out=th_[:, 1, :, :], in_=sf_d[kc]), writes=[tt_])
            banks = [pp.get() for _ in range(8)]

            def mmf(e, th_=th_, banks=banks, lo=0, hi=8):
                for bi in range(lo, hi):
                    which, trig, cb = bi // 4, (bi // 2) % 2, bi % 2
                    for tc in range(16):
                        if which == 0:
                            rhs = z_tm[:, tc, cb * 512:(cb + 1) * 512]
                        else:
                            rhs = (Av if trig == 0 else Bv)[:, tc, cb * 512:(cb + 1) * 512]
                        ins = e.matmul(banks[bi][0][:, :], lhsT=th_[:, trig, tc, :], rhs=rhs, start=(tc == 0), stop=(tc == 15))
                return ins
            P.op("pe", lambda e, mmf=mmf: mmf(e, lo=4, hi=8), reads=[tt_] + big_t, writes=[b_[1] for b_ in banks[4:8]])
            P.op("pe", lambda e, mmf=mmf: mmf(e, lo=0, hi=4), reads=[tt_] + ztm_t, writes=[b_[1] for b_ in banks[0:4]])
            hh_, ht_ = hsb_p.get()
            for trig in range(2):
                for cb in range(2):
                    P.op("act", lambda e, hh_=hh_, trig=trig, cb=cb, banks=banks: e.activation(out=hh_[:, trig, cb * 512:(cb + 1) * 512], in_=banks[4 + trig * 2 + cb][0][:, :], func=AF.Copy),
                         reads=[banks[4 + trig * 2 + cb][1]], writes=[ht_])
            tmh, tmt = tm_p.get()
            for half, pairs, op_, yk in ((0, ((0, 0), (1, 1)), ALU.subtract, kc), (1, ((0, 1), (1, 0)), ALU.add, 16 + kc)):
                for ti, (zi, hi) in enumerate(pairs):
                    for cb in range(2):
                        P.op("dve", lambda e, tmh=tmh, ti=ti, zi=zi, hi=hi, cb=cb, banks=banks, hh_=hh_: e.tensor_tensor(
                            out=tmh[:, ti, cb * 512:(cb + 1) * 512], in0=banks[zi * 2 + cb][0][:, :], in1=hh_[:, hi, cb * 512:(cb + 1) * 512], op=ALU.mult),
                            reads=[banks[zi * 2 + cb][1], ht_], writes=[tmt])
                P.op("pool", lambda e, tmh=tmh, yk=yk, op_=op_: e.tensor_tensor(out=Yv[:, yk, :], in0=tmh[:, 0, :], in1=tmh[:, 1, :], op=op_), reads=[tmt], writes=[hT_t[yk // 2]])
        P.barrier()
        ep_z = Pool.views("ep_z%d" % o, [carve(0, 1024, BF16), carve(1, 1024, BF16)])
        ep_g = Pool.views("ep_g%d" % o, [carve(2, 1024, BF16), carve(3, 1024, BF16)])
        ep_f = Pool.views("ep_f%d" % o, [carve(4, 2048, F32), carve(6, 2048, F32)])
        ep_u = Pool.views("ep_u%d" % o, [carve(8, 2048, F32), carve(10, 2048, F32)])
        ep_o = Pool.views("ep_o%d" % o, [carve(12, 1024, BF16), carve(13, 1024, BF16)])
        gsrc, gsrc_t = (x1s, x1s_t) if o == 0 else (x2s, x2s_t)
        pend_tr = []

        def do_transposes(oh, ot, nb, cc):
            ph2, pt2 = pp.get()

            def mmt2(e, ph2=ph2, oh=oh):
                for i in range(4):
                    ins = e.matmul(ph2[:, i * 128:(i + 1) * 128], lhsT=oh[:, i * 128:(i + 1) * 128], rhs=idb[:, :], start=True, stop=True)
                return ins
            P.op("pe", mmt2, reads=[ot, idb_t], writes=[pt2])
            P.op("act", lambda e, ph2=ph2, nb=nb, cc=cc: e.activation(out=z_tm[:, nb * 4:(nb + 1) * 4, cc * 128:(cc + 1) * 128],
                                                                     in_=ph2[:, :].rearrange("p (i n) -> p i n", i=4), func=AF.Copy), reads=[pt2], writes=ztm_t[nb * 4:(nb + 1) * 4])
        for nb in range(4):
            tabv = big[:, nb % 2, :].rearrange("p (c n) -> p c n", c=32)
            P.dma("sp", lambda e, tabv=tabv, nb=nb: e.dma_start(out=tabv[:, 0:16, :], in_=inv_d[nb][:, 0:16, :]), writes=[big_t[nb % 2]])
            P.dma("sp", lambda e, tabv=tabv, nb=nb: e.dma_start(out=tabv[:, 16:32, :], in_=inv_d[nb][:, 16:32, :]), writes=[big_t[nb % 2]])
            sl = slice(nb * 512, (nb + 1) * 512)
            for cc in range(8):
                zh, zt = ep_z.get()
                gh, gt = ep_g.get()
                P.dma("sp", lambda e, zh=zh, cc=cc, sl=sl, o=o: e.dma_start(out=zh[:, :], in_=zfs[o][cc * 128:(cc + 1) * 128, sl]), reads=[zfs_t[o][cc]], writes=[zt])
                P.dma("sp", lambda e, gh=gh, cc=cc, sl=sl, gsrc=gsrc: e.dma_start(out=gh[:, :], in_=gsrc[cc * 128:(cc + 1) * 128, sl]), reads=[gsrc_t[cc]], writes=[gt])
                ph, pt = pp.get()

                def mmi(e, ph=ph, tabv=tabv, cc=cc):
                    for kc in range(32):
                        ins = e.matmul(ph[:, :], lhsT=Yv[:, kc, cc * 128:(cc + 1) * 128], rhs=tabv[:, kc, :], start=(kc == 0), stop=(kc == 31))
                    return ins
                P.op("pe", mmi, reads=hT_t + [big_t[nb % 2]], writes=[pt])
                fh_, ft_ = ep_f.get()
                P.op("act", lambda e, zh=zh, fh_=fh_, o=o, cc=cc: e.activation(out=fh_[:, :], in_=zh[:, :], func=AF.Copy, scale=skip[:, o * 8 + cc:o * 8 + cc + 1]),
                     reads=[zt, skip_t], writes=[ft_])
                uh, ut = ep_u.get()
                P.op("dve", lambda e, ph=ph, uh=uh, fh_=fh_, o=o, cc=cc: e.scalar_tensor_tensor(out=uh[:, :], in0=ph[:, :], scalar=rnorm[:, o * 8 + cc:o * 8 + cc + 1], in1=fh_[:, :],
                                                                                              op0=ALU.mult, op1=ALU.add), reads=[pt, rnorm_t, ft_], writes=[ut])
                oh, ot = ep_o.get()
                P.op("pool", lambda e, uh=uh, gh=gh, oh=oh: e.tensor_tensor(out=oh[:, :], in0=uh[:, :], in1=gh[:, :], op=ALU.mult), reads=[ut, gt], writes=[ot])
                if o == 0:
                    P.dma("pool", lambda e, oh=oh, cc=cc, sl=sl: e.dma_start(out=zfs[1][cc * 128:(cc + 1) * 128, sl], in_=oh[:, :]), reads=[ot], writes=[zfs_t[1][cc]], semtile=ot)
                    pend_tr.append((oh, ot, nb, cc))
                    if len(pend_tr) > 1:
                        do_transposes(*pend_tr.pop(0))
                else:
                    P.dma("pool", lambda e, oh=oh, cc=cc, sl=sl: e.dma_start(out=z3_out[cc * 128:(cc + 1) * 128, sl], in_=oh[:, :]), reads=[ot], writes=[out_t], semtile=ot)
        while pend_tr:
            do_transposes(*pend_tr.pop(0))
        if upto == 2 and o == 0:
            P.wait_all("sp", zfs_t[1] + x1s_t + x2s_t + zfs_t[0])
            return
    P.wait_all("sp", [b_[1] for b_ in ep_o.bufs] + zfs_t[1])


def build_hy(upto=99):
    nc = bass.Bass("TRN2", target_bir_lowering=False)
    hT_d = dram_in(nc, "i_hT", [D, L], BF16)
    w = dram_in(nc, "i_w", [D, 3072])
    sw_d = dram_in(nc, "i_sw", [128, 72])
    skip_d = dram_in(nc, "i_skip", [128, 16])
    fw1_d = dram_in(nc, "i_fw1", [33, 64])
    fw23_d = dram_in(nc, "i_fw23", [64, 128])
    fvec_d = dram_in(nc, "i_fvec", [64, 5])
    fw4_d = dram_in(nc, "i_fw4", [64, 4096])
    zemb_d = dram_in(nc, "i_zemb", [33, L])
    win_d = dram_in(nc, "i_win", [L, 1024])
    id_d = dram_in(nc, "i_id", [128, 128])
    cf_d = dram_in(nc, "i_cf", [16, 128, 16, 128], BF16)
    sf_d = dram_in(nc, "i_sf", [16, 128, 16, 128], BF16)
    inv_d = dram_in(nc, "i_inv", [4, 128, 32, 512], BF16)
    z3_out = dram_out(nc, "o_z3", [1024, L], BF16)
    P = Prog(nc)
    emit_hy(P, nc, hT_d, w, sw_d, skip_d, fw1_d, fw23_d, fvec_d, fw4_d, zemb_d, win_d, id_d, cf_d, sf_d, inv_d, z3_out, upto)
    P.emit()
    P.close()
    return nc


NCHK = 192


def emit_mod_full(P, nc, cT, aws, ab, s_mod):
    cs = P.sbuf("cs", [128, KC, 2], F32)
    sil = P.sbuf("sil", [128, KC, 2], BF16)
    sg = P.sbuf("sg", [128, KC, 2], F32)
    bs = P.sbuf("bs", [128, NCHK], F32)
    res = P.sbuf("res", [128, 2, NCHK], F32)
    ps = P.psum("mps", [128, 512])
    wpool = Pool(P, "mw", 4, [128, KC, 512], BF16)
    tcs, tsil, tsg, tbs, tres, tps, tout = T("cs"), T("sil"), T("sg"), T("bs"), T("res"), T("mps"), T("mout")
    P.dma("sp", lambda e: e.dma_start(out=cs[:], in_=cT.rearrange("(k p) j -> p k j", p=128)), writes=[tcs])
    P.dma("sp", lambda e: e.dma_start(out=bs[:], in_=ab), writes=[tbs])
    P.op("act", lambda e: e.activation(out=sg[:], in_=cs[:], func=AF.Sigmoid), reads=[tcs], writes=[tsg])
    P.op("dve", lambda e: e.tensor_tensor(out=sil[:], in0=cs[:], in1=sg[:], op=ALU.mult), reads=[tcs, tsg], writes=[tsil])
    for blk in range(48):
        wh, wt = wpool.get()
        aw = aws[blk // 24]
        c0 = (blk % 24) * 512
        q = "pool"
        P.dma(q, lambda e, wh=wh, aw=aw, c0=c0: e.dma_start(out=wh[:], in_=aw[:, c0:c0 + 512].rearrange("(k p) n -> p k n", p=128)), writes=[wt])

        def mm(e, wh=wh, blk=blk):
            for ci in range(4):
                i0 = (blk * 4 + ci) * 2
                for k in range(KC):
                    ins = e.matmul(ps[:, i0:i0 + 2], lhsT=wh[:, k, ci * 128:(ci + 1) * 128], rhs=sil[:, k, :], start=(k == 0), stop=(k == KC - 1))
            return ins
        P.op("pe", mm, reads=[wt, tsil], writes=[tps])
    psv = ps[:, 0:2 * NCHK].rearrange("p (c j) -> p c j", j=2)
    for j in range(2):
        P.op("dve", lambda e, j=j: e.tensor_tensor(out=res[:, j, :], in0=psv[:, :, j], in1=bs[:, :], op=ALU.add), reads=[tps, tbs], writes=[tres])
    P.dma("sp", lambda e: e.dma_start(out=s_mod, in_=res[:]), reads=[tres], writes=[tout])
    P.wait_all("sp", [tout])


def build_fused():
    nc = bass.Bass("TRN2", target_bir_lowering=False)
    I = lambda name, shape, dt=F32: dram_in(nc, name, shape, dt)
    xT = I("i_xT", [D, NTT])
    cT = I("i_cT", [D, 2])
    aws = [I("i_aw0", [D, 6 * D]), I("i_aw1", [D, 6 * D])]
    ab = I("i_ab", [128, NCHK])
    ng = I("i_ng", [128, 6 * KC])
    wB = [I("i_w0", [D, 4096]), I("i_w1", [D, 4096])]
    lbB = [I("i_lb0", [64, 3072]), I("i_lb1", [64, 3072])]
    cwB = [I("i_cw0", [128, 12]), I("i_cw1", [128, 12])]
    gnB = [I("i_gn0", [128, 4]), I("i_gn1", [128, 4])]
    cst = I("i_cst", [64, 324])
    wout = [I("i_wout0", [D, D]), I("i_wout1", [D, D])]
    w1 = [I("i_w1_0", [D, 4 * D]), I("i_w1_1", [D, 4 * D])]
    w2 = [I("i_w2_0", [4 * D, D]), I("i_w2_1", [4 * D, D])]
    hw = [I("i_hw0", [D, 3072]), I("i_hw1", [D, 3072])]
    hsw = [I("i_sw0", [128, 72]), I("i_sw1", [128, 72])]
    hsk = [I("i_skip0", [128, 16]), I("i_skip1", [128, 16])]
    fw1 = I("i_fw1", [33, 64])
    fw23 = I("i_fw23", [64, 128])
    fvec = I("i_fvec", [64, 5])
    fw4 = [I("i_fw4_0", [64, 4096]), I("i_fw4_1", [64, 4096])]
    zemb = I("i_zemb", [33, L])
    win = [I("i_win0", [L, 1024]), I("i_win1", [L, 1024])]
    idm = I("i_id", [128, 128])
    cf = I("i_cf", [16, 128, 16, 128], BF16)
    sf = I("i_sf", [16, 128, 16, 128], BF16)
    inv = I("i_inv", [4, 128, 32, 512], BF16)
    out = dram_out(nc, "o_out", [D, L])
    s_mod = nc.dram_tensor("s_mod", [128, 2, NCHK], F32).ap()
    s_m = nc.dram_tensor("s_m", [D, L], BF16).ap()
    s_x = nc.dram_tensor("s_x", [D, L], F32).ap()
    s_h1 = nc.dram_tensor("s_h1", [D, L], BF16).ap()
    s_z3 = nc.dram_tensor("s_z3", [D, L], BF16).ap()
    s_h0 = nc.dram_tensor("s_h0", [D, NTT], BF16).ap()

    def md(l, i, j):
        c0 = l * 96 + i * 16
        return s_mod[:, j, c0:c0 + 16]

    def ngc(i):
        return ng[:, i * KC:(i + 1) * KC]

    def stage(pfx, fn):
        with nc.cleanup_on_exit():
            P = Prog(nc, pfx)
            fn(P)
            P.emit()
            nc.all_engine_barrier()

    stage("A_", lambda P: emit_mod_full(P, nc, cT, aws, ab, s_mod))
    stage("H_", lambda P: emit_mix0(P, nc, xT, [ngc(0), md(0, 1, 0), md(0, 0, 0), md(0, 1, 1), md(0, 0, 1)], wB[0], lbB[0], cwB[0], gnB[0], cst, s_m,
                                    upto=-1, h_save=s_h0))
    for hh in range(2):
        stage("B%d_" % hh, lambda P, hh=hh: emit_mix0(
            P, nc, xT, None, wB[hh], lbB[hh], cwB[hh], gnB[hh], cst, s_m, mrow=(hh * 512, 1024 + hh * 512), h_load=s_h0))
    for th in range(2):
        sl = slice(th * NT, (th + 1) * NT)
        stage("C%d_" % th, lambda P, sl=sl: emit_tok(
            P, nc, s_m[:, sl], xT[:, sl], [md(0, 2, 0), ngc(1), md(0, 4, 0), md(0, 3, 0), md(0, 5, 0), ngc(2), md(1, 1, 0), md(1, 0, 0)],
            wout[0], w1[0], w2[0], s_x[:, sl], s_h1[:, sl], False))
    for hh in range(2):
        stage("D%d_" % hh, lambda P, hh=hh: emit_hy(
            P, nc, s_h1, hw[hh], hsw[hh], hsk[hh], fw1, fw23, fvec, fw4[hh], zemb, win[hh], idm, cf, sf, inv, s_z3[hh * 1024:(hh + 1) * 1024, :]))
    for th in range(2):
        sl = slice(th * NT, (th + 1) * NT)
        stage("E%d_" % th, lambda P, sl=sl: emit_tok(
            P, nc, s_z3[:, sl], s_x[:, sl], [md(1, 2, 0), ngc(3), md(1, 4, 0), md(1, 3, 0), md(1, 5, 0), ngc(4), ngc(5), ngc(5)],
            wout[1], w1[1], w2[1], out[:, sl], None, True))
    return nc


_PROGS = {}


def _pk(v):
    return np.asarray(v, np.float32).reshape(16, 128).T


def kernel(x, c, ctx, c_ctx, ada_w, ada_b, norm_g, lb_logits, ab_w_in, ab_conv_w, ab_gnorm_g, ab_w_out, hy_in_w, hy_short_w,
           hy_out_w, hy_fw1, hy_fb1, hy_fw2, hy_fb2, hy_fw3, hy_fb3, hy_fw4, hy_freq, hy_skip, mlp_w1, mlp_w2, final_g):
    f32 = lambda a: np.asarray(a, np.float32)
    C_ = np.ascontiguousarray
    x, c, ctx, c_ctx, ada_w, ada_b, norm_g = map(f32, (x, c, ctx, c_ctx, ada_w, ada_b, norm_g))
    if "fused" not in _PROGS:
        _PROGS["fused"] = build_fused()
    nc = _PROGS["fused"]
    shared = {}
    shared["i_aw0"], shared["i_aw1"] = ada_w[0], ada_w[1]
    shared["i_ab"] = C_(np.concatenate([ada_b[0], ada_b[1]]).reshape(NCHK, 128).T)
    shared["i_ng"] = C_(np.concatenate([_pk(norm_g[0, 0]), _pk(norm_g[0, 1]), _pk(norm_g[1, 0]), _pk(norm_g[1, 1]), _pk(f32(final_g)),
                                        _pk(np.zeros(D, np.float32))], 1))
    w_in = f32(ab_w_in)[0]
    in_w = f32(hy_in_w)[0]
    cf, sf, inv = hy_tables()
    shared.update({"i_cst": mix0_consts(), "i_fw1": f32(hy_fw1)[0], "i_fw23": C_(np.concatenate([f32(hy_fw2)[0], f32(hy_fw3)[0]], 1)),
                   "i_fvec": np.stack([f32(hy_fb1)[0], f32(hy_fb2)[0], f32(hy_fb3)[0], f32(hy_freq)[0], np.full(64, -np.pi, np.float32)], 1).astype(np.float32),
                   "i_id": np.eye(128, dtype=np.float32), "i_cf": cf, "i_sf": sf, "i_inv": inv,
                   "i_wout0": f32(ab_w_out)[0], "i_wout1": f32(hy_out_w)[0], "i_w1_0": f32(mlp_w1)[0], "i_w1_1": f32(mlp_w1)[1],
                   "i_w2_0": f32(mlp_w2)[0], "i_w2_1": f32(mlp_w2)[1]})
    for hh in range(2):
        cols = np.concatenate([np.arange(g * 1024 + hh * 512, g * 1024 + hh * 512 + 512) for g in range(8)])
        shared["i_w%d" % hh] = C_(w_in[:, cols])
        shared["i_lb%d" % hh] = C_(np.broadcast_to(f32(lb_logits)[:, :, hh * 512:(hh + 1) * 512].reshape(1, -1), (64, 3072)))
        shared["i_cw%d" % hh] = C_(f32(ab_conv_w)[0][:, hh * 512:(hh + 1) * 512].T.reshape(4, 128, 3).transpose(1, 0, 2).reshape(128, 12))
        shared["i_gn%d" % hh] = C_(f32(ab_gnorm_g)[0][hh * 512:(hh + 1) * 512].reshape(4, 128).T)
        cols = np.concatenate([np.arange(g * 2048 + hh * 1024, g * 2048 + hh * 1024 + 1024) for g in range(3)])
        shared["i_hw%d" % hh] = C_(in_w[:, cols])
        shared["i_sw%d" % hh] = C_(f32(hy_short_w)[0][:, cols].T.reshape(24, 128, 3).transpose(1, 0, 2).reshape(128, 72))
        shared["i_skip%d" % hh] = C_(f32(hy_skip)[0][:, hh * 1024:(hh + 1) * 1024].reshape(2, 8, 128).transpose(2, 0, 1).reshape(128, 16))
        shared["i_fw4_%d" % hh] = C_(f32(hy_fw4)[0].reshape(64, 2, 2, 2048)[:, :, :, hh * 1024:(hh + 1) * 1024].reshape(64, 4096))
        zembT, win = hy_consts(hh)
        shared["i_zemb"] = zembT
        shared["i_win%d" % hh] = win
    maps = []
    for b in range(4):
        m = dict(shared)
        m["i_xT"] = C_(np.concatenate([x[b], ctx[b]], 0).T)
        m["i_cT"] = C_(np.stack([c[b], c_ctx], 1))
        maps.append(m)
    res = run_bass_kernel_spmd(nc, maps, core_ids=list(range(4))).results
    return np.stack([C_(res[b]["o_out"].T) for b in range(4)], 0).astype(np.float32)
```

```python
import numpy as np
import ml_dtypes
import concourse.bass as bass
import concourse.mybir as mybir
from concourse.bass_utils import run_bass_kernel_spmd

F32 = mybir.dt.float32
BF16 = mybir.dt.bfloat16
AF = mybir.ActivationFunctionType
ALU = mybir.AluOpType
AX = mybir.AxisListType
NPBF = ml_dtypes.bfloat16

D = 2048
KC = 16
EPS = 1e-6
ENGS = ("pe", "act", "dve", "pool", "sp")


class T:
    __slots__ = ("name", "lastw", "readers", "sem", "dman")

    def __init__(self, name):
        self.name = name
        self.lastw = None
        self.readers = []
        self.sem = None
        self.dman = 0


def TL(name, n):
    return [T("%s%d" % (name, i)) for i in range(n)]


class Prog:
    def __init__(self, nc, pfx=""):
        self.nc = nc
        self.pfx = pfx
        self.q = {e: [] for e in ENGS}
        self.cnt = {e: 0 for e in ENGS}
        self.seen = {e: {} for e in ENGS}
        self.sems = {}
        self._stack = []
        self._dma_tiles = []

    def _enter(self, cm):
        h = cm.__enter__()
        self._stack.append(cm)
        return h

    def sem(self, name):
        return self._enter(self.nc.semaphore(self.pfx + name))

    def sbuf(self, name, shape, dt):
        return self._enter(self.nc.sbuf_tensor(self.pfx + name, list(shape), dt))

    def psum(self, name, shape, dt=F32):
        return self._enter(self.nc.psum_tensor(self.pfx + name, list(shape), dt))

    def close(self):
        while self._stack:
            self._stack.pop().__exit__(None, None, None)

    def _engsem(self, e):
        if e not in self.sems:
            self.sems[e] = self.sem("s_" + e)
        return self.sems[e]

    def _collect(self, eng, reads, writes, is_dma):
        waits = {}

        def need(ev, war=False):
            if ev is None:
                return
            key, val, e = ev
            if e == eng and not is_dma and key == eng and (war or eng == "pe"):
                return
            if waits.get(key, 0) < val:
                waits[key] = val

        for t in reads:
            need(t.lastw)
        for t in writes:
            need(t.lastw)
            for r in t.readers:
                need(r, war=True)
        out = []
        seen = self.seen[eng]
        for key, val in waits.items():
            if seen.get(key, 0) >= val:
                continue
            seen[key] = val
            out.append((key, val))
        return out

    def _semobj(self, key):
        if isinstance(key, str):
            return self._engsem(key)
        return key.sem

    def op(self, eng, fn, reads=(), writes=()):
        waits = self._collect(eng, reads, writes, False)
        self.cnt[eng] += 1
        self._engsem(eng)
        ev = (eng, self.cnt[eng], eng)
        for t in reads:
            t.readers.append(ev)
        for t in writes:
            t.lastw = ev
            t.readers = []
        self.q[eng].append((waits, fn, (eng, 1)))

    def dma(self, eng, fn, reads=(), writes=(), semtile=None):
        waits = self._collect(eng, reads, writes, True)
        st = semtile or (writes[0] if writes else reads[0])
        if st.sem is None:
            st.sem = self.sem("d_" + st.name)
            self._dma_tiles.append(st)
        st.dman += 1
        ev = (st, 16 * st.dman, "dma")
        for t in reads:
            t.readers.append(ev)
        for t in writes:
            t.lastw = ev
            t.readers = []
        self.q[eng].append((waits, fn, (st, 16)))

    def barrier(self):
        tiles = [t for t in self._dma_tiles]
        for e in ENGS:
            if e == "pe" and not self.q[e]:
                continue
            waits = []
            seen = self.seen[e]
            for e2 in ENGS:
                if e2 != e and e2 in self.sems and self.cnt[e2] > seen.get(e2, 0):
                    seen[e2] = self.cnt[e2]
                    waits.append((e2, self.cnt[e2]))
            for t in tiles:
                v = 16 * t.dman
                if v > seen.get(t, 0):
                    seen[t] = v
                    waits.append((t, v))
            self.q[e].append((waits, None, None))

    def wait_all(self, eng, tiles):
        waits = self._collect(eng, tiles, tiles, True)
        self.q[eng].append((waits, None, None))

    def emit(self):
        engobj = {"pe": "tensor", "act": "scalar", "dve": "vector", "pool": "gpsimd", "sp": "sync"}
        with self.nc.Block() as block:
            for e in ENGS:
                lst = self.q[e]
                if not lst:
                    continue

                def body(eobj, lst=lst):
                    for waits, fn, inc in lst:
                        for key, val in waits:
                            eobj.wait_ge(self._semobj(key), val)
                        if fn is None:
                            continue
                        ins = fn(eobj)
                        ins.then_inc(self._semobj(inc[0]), inc[1])

                getattr(block, engobj[e])(body)


class Pool:
    def __init__(self, P, name, n, shape, dt, psum=False):
        self.bufs = []
        for i in range(n):
            h = P.psum("%s%d" % (name, i), shape, dt) if psum else P.sbuf("%s%d" % (name, i), shape, dt)
            self.bufs.append((h, T("%s%d" % (name, i))))
        self.i = 0

    def get(self):
        b = self.bufs[self.i % len(self.bufs)]
        self.i += 1
        return b

    @classmethod
    def views(cls, name, aps):
        self = cls.__new__(cls)
        self.bufs = [(ap, T("%s%d" % (name, i))) for i, ap in enumerate(aps)]
        self.i = 0
        return self


def dram_in(nc, name, shape, dt=F32):
    return nc.dram_tensor(name, list(shape), dt, kind="ExternalInput").ap()


def dram_out(nc, name, shape, dt=F32):
    return nc.dram_tensor(name, list(shape), dt, kind="ExternalOutput").ap()


def load_w_block(P, wpool, w_ap, rows_kc, col0, ncols, queue="pool"):
    wh, wt = wpool.get()
    src = w_ap[:, col0:col0 + ncols].rearrange("(k p) n -> p k n", p=128)
    P.dma(queue, lambda e: e.dma_start(out=wh[:, 0:rows_kc, 0:ncols], in_=src), writes=[wt])
    return wh, wt


def norm_stats(P, C, xs_fn, x_ts, tok_blocks, ntok, pre=None):
    ones, sqpool, stat, stat_t, rstd, rstd_t = C["ones"], C["sqpool"], C["stat"], C["stat_t"], C["rstd"], C["rstd_t"]
    for k in range(KC):
        if pre is not None:
            pre(k)
        sq, sqt = sqpool.get()
        xap, xtk = xs_fn(k), (x_ts(k) if callable(x_ts) else x_ts[k])
        P.op("act", lambda e, xap=xap, sq=sq: e.activation(out=sq[:, 0:ntok], in_=xap, func=AF.Square),
             reads=[xtk], writes=[sqt])

        def mm(e, k=k, sq=sq):
            for bi, (t0, tn) in enumerate(tok_blocks):
                i = e.matmul(stat[bi][:, 0:tn], lhsT=ones[:], rhs=sq[:, t0:t0 + tn], start=(k == 0), stop=(k == KC - 1))
            return i
        P.op("pe", mm, reads=[sqt, C["ones_t"]], writes=stat_t)
    for bi, (t0, tn) in enumerate(tok_blocks):
        P.op("act", lambda e, bi=bi, t0=t0, tn=tn: e.activation(
            out=rstd[:, t0:t0 + tn], in_=stat[bi][:, 0:tn], func=AF.Sqrt, scale=1.0 / D, bias=C["eps"][:, 0:1]),
            reads=[stat_t[bi], C["eps_t"]], writes=[rstd_t])
    P.op("dve", lambda e: e.reciprocal(out=rstd[:, 0:ntok], in_=rstd[:, 0:ntok]), reads=[rstd_t], writes=[rstd_t])


def load_scal(P, scal, scal_t, scal_d):
    if isinstance(scal_d, (list, tuple)):
        for i, ap in enumerate(scal_d):
            P.dma("sp", lambda e, i=i, ap=ap: e.dma_start(out=scal[:, i * KC:(i + 1) * KC], in_=ap), writes=[scal_t])
    else:
        P.dma("sp", lambda e: e.dma_start(out=scal[:], in_=scal_d), writes=[scal_t])


def mod_coef(P, C, scal, scal_t, gi, sci, name):
    a = P.sbuf(name, [128, KC], F32)
    at = T(name)
    P.op("dve", lambda e: e.scalar_tensor_tensor(out=a[:], in0=scal[:, sci * KC:(sci + 1) * KC], scalar=1.0,
                                                 in1=scal[:, gi * KC:(gi + 1) * KC], op0=ALU.add, op1=ALU.mult),
         reads=[scal_t], writes=[at])
    return a, at


NMC = 3072


def emit_mod(P, nc, cT, aw, ab, out):
    cs = P.sbuf("cs", [128, KC, 5], F32)
    sil = P.sbuf("sil", [128, KC, 5], F32)
    sg = P.sbuf("sg", [128, KC, 5], F32)
    bs = P.sbuf("bs", [128, NMC // 128], F32)
    res = P.sbuf("res", [128, NMC // 128, 5], F32)
    ps = P.psum("mps", [128, 512])
    wpool = Pool(P, "mw", 2, [128, KC, 512], F32)
    tcs, tsil, tsg, tbs, tres, tps, tout = T("cs"), T("sil"), T("sg"), T("bs"), T("res"), T("mps"), T("mout")
    P.dma("sp", lambda e: e.dma_start(out=cs[:], in_=cT.rearrange("(k p) j -> p k j", p=128)), writes=[tcs])
    P.dma("sp", lambda e: e.dma_start(out=bs[:], in_=ab), writes=[tbs])
    P.op("act", lambda e: e.activation(out=sg[:], in_=cs[:], func=AF.Sigmoid), reads=[tcs], writes=[tsg])
    P.op("dve", lambda e: e.tensor_tensor(out=sil[:], in0=cs[:], in1=sg[:], op=ALU.mult), reads=[tcs, tsg], writes=[tsil])
    nblk = NMC // 512
    for blk in range(nblk):
        wh, wt = wpool.get()
        q = "sp" if blk % 2 == 0 else "act"
        P.dma(q, lambda e, wh=wh, blk=blk: e.dma_start(
            out=wh[:], in_=aw[:, blk * 512:(blk + 1) * 512].rearrange("(k p) n -> p k n", p=128)), writes=[wt])

        def mm(e, wh=wh, blk=blk):
            for ci in range(4):
                i0 = (blk * 4 + ci) * 5
                for k in range(KC):
                    ins = e.matmul(ps[:, i0:i0 + 5], lhsT=wh[:, k, ci * 128:(ci + 1) * 128], rhs=sil[:, k, :],
                                   start=(k == 0), stop=(k == KC - 1))
            return ins
        P.op("pe", mm, reads=[wt, tsil], writes=[tps])
    nch = NMC // 128
    P.op("dve", lambda e: e.tensor_tensor(out=res[:], in0=ps[:, 0:nch * 5].rearrange("p (c j) -> p c j", j=5),
                                          in1=bs[:].unsqueeze(2).to_broadcast([128, nch, 5]), op=ALU.add),
         reads=[tps, tbs], writes=[tres])
    P.dma("sp", lambda e: e.dma_start(out=out, in_=res[:].rearrange("p c j -> p (c j)")), reads=[tres], writes=[tout])
    P.wait_all("sp", [tout])


def build_mod():
    nc = bass.Bass("TRN2", target_bir_lowering=False)
    cT = dram_in(nc, "i_cT", [D, 5])
    aw = dram_in(nc, "i_aw", [D, NMC])
    ab = dram_in(nc, "i_ab", [128, NMC // 128])
    out = dram_out(nc, "o_mod", [128, (NMC // 128) * 5])
    P = Prog(nc)
    emit_mod(P, nc, cT, aw, ab, out)
    P.emit()
    P.close()
    return nc


NT = 1024
TB2 = [(0, 512), (512, 512)]


def emit_tok(P, nc, mT, xT, scal_d, w_out, w1, w2, x_out, h_out, final):
    x_sb = P.sbuf("x_sb", [128, KC, NT], F32)
    hb = P.sbuf("hb", [128, KC, NT], BF16)
    scal = P.sbuf("scal", [128, 8 * KC], F32)
    ones = P.sbuf("ones", [128, 128], BF16)
    rstd = P.sbuf("rstd", [128, NT], F32)
    tmp_pool = Pool(P, "tmpn", 2, [128, NT], F32)
    wpool = Pool(P, "w", 3, [128, KC, 512], BF16)
    apool = Pool(P, "ab", 2, [128, 4, NT], BF16)
    rpool = Pool(P, "rl", 2, [128, 512], F32)
    sqpool = Pool(P, "sq", 2, [128, NT], BF16)
    pp = Pool(P, "pp", 6, [128, 512], F32, psum=True)
    stat = [P.psum("st%d" % i, [128, 512]) for i in range(2)]
    x_t, hb_t = TL("x", KC), TL("hb", KC)
    scal_t, ones_t, rstd_t, tmp_t = T("scal"), T("ones"), T("rstd"), T("tmpn")
    epsb = P.sbuf("epsb", [128, 1], F32)
    eps_t = T("epsb")
    P.op("pool", lambda e: e.memset(epsb[:], EPS), writes=[eps_t])
    C = dict(ones=ones, sqpool=sqpool, stat=stat, stat_t=TL("st", 2), rstd=rstd, rstd_t=rstd_t, eps=epsb, eps_t=eps_t, ones_t=ones_t)
    tout = T("tokout")

    P.op("pool", lambda e: e.memset(ones[:], 1.0), writes=[ones_t])
    load_scal(P, scal, scal_t, scal_d)
    for k in range(KC):
        q = "sp"
        P.dma(q, lambda e, k=k: e.dma_start(out=hb[:, k, :], in_=mT[k * 128:(k + 1) * 128, :]), writes=[hb_t[k]])
        P.dma(q, lambda e, k=k: e.dma_start(out=x_sb[:, k, :], in_=xT[k * 128:(k + 1) * 128, :]), writes=[x_t[k]])

    def sc(i, c):
        return scal[:, i * KC + c:i * KC + c + 1]

    for cb in range(4):
        wh, wt = load_w_block(P, wpool, w_out, KC, cb * 512, 512)
        for ci in range(4):
            oc = cb * 4 + ci
            for (t0, tn) in TB2:
                ph, pt = pp.get()

                def mm(e, wh=wh, ci=ci, t0=t0, tn=tn, ph=ph):
                    for k in range(KC):
                        ins = e.matmul(ph[:, 0:tn], lhsT=wh[:, k, ci * 128:(ci + 1) * 128], rhs=hb[:, k, t0:t0 + tn],
                                       start=(k == 0), stop=(k == KC - 1))
                    return ins
                P.op("pe", mm, reads=[wt] + hb_t, writes=[pt])
                P.op("dve", lambda e, oc=oc, t0=t0, tn=tn, ph=ph: e.scalar_tensor_tensor(
                    out=x_sb[:, oc, t0:t0 + tn], in0=ph[:, 0:tn], scalar=sc(0, oc), in1=x_sb[:, oc, t0:t0 + tn],
                    op0=ALU.mult, op1=ALU.add), reads=[pt, scal_t, x_t[oc]], writes=[x_t[oc]])

    def norm_to_hb(gi, sci, shi, name):
        a, at = mod_coef(P, C, scal, scal_t, gi, sci, name)
        norm_stats(P, C, lambda k: x_sb[:, k, :], x_t, TB2, NT)
        for k in range(KC):
            tmp_, tmp_t_ = tmp_pool.get()
            P.op("dve", lambda e, k=k, tmp_=tmp_: e.scalar_tensor_tensor(out=tmp_[:], in0=x_sb[:, k, :], scalar=a[:, k:k + 1],
                                                                         in1=rstd[:], op0=ALU.mult, op1=ALU.mult),
                 reads=[x_t[k], at, rstd_t], writes=[tmp_t_])
            P.op("act", lambda e, k=k, tmp_=tmp_: e.activation(out=hb[:, k, :], in_=tmp_[:], func=AF.Identity, bias=sc(shi, k)),
                 reads=[tmp_t_, scal_t], writes=[hb_t[k]])
    norm_to_hb(1, 2, 3, "a2")

    for fb in range(16):
        w1h, w1t = load_w_block(P, wpool, w1, KC, fb * 512, 512)
        w2h, w2t = wpool.get()
        w2v = w2h[:].rearrange("p (a b) n -> p a (b n)", a=4)
        P.dma("pool", lambda e, w2v=w2v, fb=fb: e.dma_start(
            out=w2v, in_=w2[fb * 512:(fb + 1) * 512, :].rearrange("(a p) n -> p a n", p=128)), writes=[w2t])
        ah, at_ = apool.get()
        for ci in range(4):
            for (t0, tn) in TB2:
                ph, pt = pp.get()

                def mm(e, w1h=w1h, ci=ci, t0=t0, tn=tn, ph=ph):
                    for k in range(KC):
                        ins = e.matmul(ph[:, 0:tn], lhsT=w1h[:, k, ci * 128:(ci + 1) * 128], rhs=hb[:, k, t0:t0 + tn],
                                       start=(k == 0), stop=(k == KC - 1))
                    return ins
                P.op("pe", mm, reads=[w1t] + hb_t, writes=[pt])
                rh, rt = rpool.get()
                P.op("act", lambda e, ph=ph, rh=rh, tn=tn: e.activation(out=rh[:, 0:tn], in_=ph[:, 0:tn], func=AF.Relu),
                     reads=[pt], writes=[rt])
                P.op("pool", lambda e, rh=rh, ah=ah, ci=ci, t0=t0, tn=tn: e.tensor_tensor(
                    out=ah[:, ci, t0:t0 + tn], in0=rh[:, 0:tn], in1=rh[:, 0:tn], op=ALU.mult), reads=[rt], writes=[at_])
        for oc in range(KC):
            for (t0, tn) in TB2:
                ph, pt = pp.get()

                def mm2(e, w2v=w2v, oc=oc, t0=t0, tn=tn, ph=ph, ah=ah):
                    for a in range(4):
                        ins = e.matmul(ph[:, 0:tn], lhsT=w2v[:, a, oc * 128:(oc + 1) * 128], rhs=ah[:, a, t0:t0 + tn],
                                       start=(a == 0), stop=(a == 3))
                    return ins
                P.op("pe", mm2, reads=[w2t, at_], writes=[pt])
                P.op("dve", lambda e, oc=oc, t0=t0, tn=tn, ph=ph: e.scalar_tensor_tensor(
                    out=x_sb[:, oc, t0:t0 + tn], in0=ph[:, 0:tn], scalar=sc(4, oc), in1=x_sb[:, oc, t0:t0 + tn],
                    op0=ALU.mult, op1=ALU.add), reads=[pt, scal_t, x_t[oc]], writes=[x_t[oc]])

    if not final:
        norm_to_hb(5, 6, 7, "a3")
        for k in range(KC):
            q = "sp"
            P.dma(q, lambda e, k=k: e.dma_start(out=h_out[k * 128:(k + 1) * 128, :], in_=hb[:, k, :]),
                  reads=[hb_t[k]], writes=[tout], semtile=hb_t[k])
            P.dma(q, lambda e, k=k: e.dma_start(out=x_out[k * 128:(k + 1) * 128, :], in_=x_sb[:, k, :]),
                  reads=[x_t[k]], writes=[tout], semtile=x_t[k])
        P.wait_all("sp", hb_t + x_t)
        P.wait_all("act", hb_t + x_t)
    else:
        norm_stats(P, C, lambda k: x_sb[:, k, :], x_t, TB2, NT)
        for k in range(KC):
            P.op("dve", lambda e, k=k: e.scalar_tensor_tensor(out=x_sb[:, k, :], in0=x_sb[:, k, :], scalar=sc(5, k),
                                                              in1=rstd[:], op0=ALU.mult, op1=ALU.mult),
                 reads=[x_t[k], scal_t, rstd_t], writes=[x_t[k]])
            q = "sp"
            P.dma(q, lambda e, k=k: e.dma_start(out=x_out[k * 128:(k + 1) * 128, :], in_=x_sb[:, k, :]),
                  reads=[x_t[k]], writes=[tout], semtile=x_t[k])
        P.wait_all("sp", x_t)
        P.wait_all("act", x_t)


def build_tok(final):
    nc = bass.Bass("TRN2", target_bir_lowering=False)
    mT = dram_in(nc, "i_mT", [D, NT], BF16)
    xT = dram_in(nc, "i_xT", [D, NT])
    scal_d = dram_in(nc, "i_scal", [128, 8 * KC])
    w_out = dram_in(nc, "i_wout", [D, D])
    w1 = dram_in(nc, "i_w1", [D, 4 * D])
    w2 = dram_in(nc, "i_w2", [4 * D, D])
    x_out = dram_out(nc, "o_x", [D, NT])
    h_out = None if final else dram_out(nc, "o_h", [D, NT], BF16)
    P = Prog(nc)
    emit_tok(P, nc, mT, xT, scal_d, w_out, w1, w2, x_out, h_out, final)
    P.emit()
    P.close()
    return nc


NTT = 2304
NCH = 36
TB5 = [(0, 512), (512, 512), (1024, 512), (1536, 512), (2048, 256)]
SEQ_F = [32, 33, 34, 35] + list(range(32))
SEQ_B = [35, 34, 33, 32] + list(range(31, -1, -1))


def mix0_consts():
    s = np.arange(64)
    pm = np.zeros((64, 2, 66), np.float32)
    pm[:, 0, :64] = (s[:, None] <= s[None, :]).astype(np.float32) - (s[:, None] <= 31).astype(np.float32)
    pm[:, 0, 64] = (s >= 32)
    pm[:, 0, 65] = (s <= 31)
    pm[:, 1, :64] = (s[:, None] >= s[None, :]).astype(np.float32) - (s[:, None] >= 32).astype(np.float32)
    pm[:, 1, 64] = (s <= 31)
    pm[:, 1, 65] = (s >= 32)
    mk = np.zeros((64, 2, 64), np.float32)
    mk[:, 0] = (s[:, None] <= s[None, :])
    mk[:, 1] = (s[:, None] >= s[None, :])
    return np.concatenate([pm.reshape(64, 132), mk.reshape(64, 128), np.eye(64, dtype=np.float32)], axis=1)


def emit_mix0(P, nc, xT, scal_d, w, lbr, cw_d, gn_d, cst_d, m_out, upto=99, mrow=(0, 512), h_save=None, h_load=None):
    gp_eq = Pool(P, "g_eq", 1, [128, 256], F32)
    hT = P.sbuf("hT", [128, KC, NTT], BF16)
    hT_t = TL("hT", KC)
    scal = P.sbuf("scal", [128, 5 * KC], F32)
    ones = P.sbuf("ones", [128, 128], BF16)
    epsb = P.sbuf("epsb", [128, 1], F32)
    lean = h_load is not None
    if not lean:
        rstd = P.sbuf("rstd", [128, NTT], F32)
        xpool = Pool(P, "xin", 2, [128, NTT], F32)
        sqpool = Pool(P, "sq", 1, [128, NTT], BF16)
        tmpp = Pool(P, "tmpn", 1, [128, NTT], F32)
    else:
        rstd = xpool = None
        sqpool = Pool(P, "sq", 1, [128, 512], BF16)
        tmpp = Pool(P, "tmpn", 1, [128, 512], F32)
    pp = Pool(P, "pp", 8, [128, 512], F32, psum=True)
    scal_t, ones_t, eps_t, rstd_t = T("scal"), T("ones"), T("epsb"), T("rstd")
    stat = [pp.bufs[i][0] for i in range(5)]
    stat_t = [pp.bufs[i][1] for i in range(5)]
    C = dict(ones=ones, sqpool=sqpool, stat=stat, stat_t=stat_t, rstd=rstd, rstd_t=rstd_t, eps=epsb, eps_t=eps_t, ones_t=ones_t)

    P.op("pool", lambda e: e.memset(ones[:], 1.0), writes=[ones_t])
    P.op("pool", lambda e: e.memset(epsb[:], EPS), writes=[eps_t])
    if h_load is None:
        load_scal(P, scal, scal_t, scal_d)

    xk_t = TL("xk", KC)
    xbuf = {}

    def load_x(k):
        xh, xt = xpool.get()
        q = "sp"
        P.dma(q, lambda e, xh=xh, k=k: e.dma_start(out=xh[:], in_=xT[k * 128:(k + 1) * 128, :]), writes=[xt])
        xbuf[k] = (xh, xt)

    hsv_t = []
    if h_load is not None:
        for k in range(KC):
            q = "sp"
            P.dma(q, lambda e, k=k: e.dma_start(out=hT[:, k, :], in_=h_load[k * 128:(k + 1) * 128, :]), writes=[hT_t[k]])
    else:
        norm_stats(P, C, lambda k: xbuf[k][0][:], lambda k: xbuf[k][1], TB5, NTT, pre=load_x)
        a_l, a_lt = mod_coef(P, C, scal, scal_t, 0, 1, "a_l")
        a_c, a_ct = mod_coef(P, C, scal, scal_t, 0, 3, "a_c")
        for k in range(KC):
            load_x(k)
            xh, xt = xbuf[k]
            th, tt = tmpp.get()
            for (lo, hi, a, at, shi) in ((0, 2048, a_l, a_lt, 2), (2048, NTT, a_c, a_ct, 4)):
                P.op("dve", lambda e, xh=xh, th=th, lo=lo, hi=hi, a=a, k=k: e.scalar_tensor_tensor(
                    out=th[:, lo:hi], in0=xh[:, lo:hi], scalar=a[:, k:k + 1], in1=rstd[:, lo:hi], op0=ALU.mult, op1=ALU.mult),
                    reads=[xt, at, rstd_t], writes=[tt])
                P.op("act", lambda e, th=th, lo=lo, hi=hi, k=k, shi=shi: e.activation(
                    out=hT[:, k, lo:hi], in_=th[:, lo:hi], func=AF.Identity, bias=scal[:, shi * KC + k:shi * KC + k + 1]),
                    reads=[tt, scal_t], writes=[hT_t[k]])
        if h_save is not None:
            for k in range(KC):
                q = "sp"
                hsv_t.append(T("hsv%d" % k))
                P.dma(q, lambda e, k=k: e.dma_start(out=h_save[k * 128:(k + 1) * 128, :], in_=hT[:, k, :]), reads=[hT_t[k]], writes=[hsv_t[-1]], semtile=hT_t[k])
    if upto == -1:
        P.wait_all("sp", hsv_t)
        return
    if upto == 0:
        for k in range(8):
            P.dma("sp", lambda e, k=k: e.dma_start(out=m_out[k * 128:(k + 1) * 128, :], in_=hT[:, k, 0:2048]), reads=[hT_t[k]], writes=[T("dbg")], semtile=hT_t[k])
        P.wait_all("sp", hT_t)
        return
    cst = P.sbuf("cst", [64, 324], F32)
    cst_t = T("cst")
    P.dma("sp", lambda e: e.dma_start(out=cst[:], in_=cst_d), writes=[cst_t])
    pm = cst[:, 0:132].rearrange("p (d n) -> p d n", d=2)
    mk = cst[:, 132:260].rearrange("p (d n) -> p d n", d=2)
    idb = P.sbuf("idb", [64, 64], BF16)
    idb_t = T("idb")
    P.op("dve", lambda e: e.tensor_copy(out=idb[:], in_=cst[:, 260:324]), reads=[cst_t], writes=[idb_t])
    cw = P.sbuf("cw", [128, 12], F32)
    gn = P.sbuf("gn", [128, 4], F32)
    cw_t, gn_t = T("cw"), T("gn")
    P.dma("sp", lambda e: e.dma_start(out=cw[:], in_=cw_d), writes=[cw_t])
    P.dma("sp", lambda e: e.dma_start(out=gn[:], in_=gn_d), writes=[gn_t])
    if not lean:
        wtm_p = Pool(P, "wtm", 1, [128, KC, 384], BF16)
        wfm_view, wfm_t = rstd[:].bitcast(BF16)[:, 0:KC * 256].rearrange("p (k n) -> p k n", k=KC), rstd_t
        qT = xpool.bufs[0][0][:, 0:2048]
        sgT = xpool.bufs[1][0][:, 0:2048]
    else:
        wtm_p = Pool(P, "wtm", 2, [128, KC, 384], BF16)
        wfm_view, wfm_t = P.sbuf("wfm", [128, KC, 256], BF16), T("wfm")
        qT = P.sbuf("qT", [128, 2048], F32)
        sgT = P.sbuf("sgT", [128, 2048], F32)
    prefetch = lean
    vtok = P.sbuf("vtok", [64, NCH, 128], BF16)
    keT = P.sbuf("keT", [64, 2, NCH, 128], BF16)
    keF = P.sbuf("keF", [128, 2, 32, 64], BF16)
    qe = P.sbuf("qe", [128, 2, 32, 64], BF16)
    ABt = P.sbuf("ABt", [128, 2, NCH, 2], F32)
    Gs = P.sbuf("Gs", [128, 2, NCH], F32)
    oT = P.sbuf("oT", [128, 2048], F32)
    Tst = [P.sbuf("Tst%d" % d, [128, 128], F32) for d in range(2)]
    sbf_p = [Pool(P, "sbf%d" % d, 2, [128, 128], BF16) for d in range(2)]
    att_p = [Pool(P, "attm%d" % d, 2, [64, 64], BF16) for d in range(2)]
    for d in range(2):
        for (ah_, at__) in att_p[d].bufs:
            P.op("pool", lambda e, ah_=ah_: e.memset(ah_[:, :], 0.0), writes=[at__])
    gp_sig = Pool(P, "g_sig", 1, [128, 512], F32)
    gp_t1 = Pool(P, "g_t1", 1, [128, 512], F32)
    gp_lf = Pool(P, "g_lf", 2, [128, 512], F32)
    gp_omf = Pool(P, "g_omf", 1, [128, 512], F32)
    gp_ee = Pool(P, "g_ee", 1, [128, 512], F32)
    mo_p = Pool(P, "mo", 1, [128, 2048], BF16)
    qT_t, sgT_t = TL("qT", 4), TL("sgT", 4)
    vtok_t = TL("vtok", NCH)
    keT_t = [TL("keT%d_" % d, NCH) for d in range(2)]
    keF_t = [TL("keF%d_" % d, 32) for d in range(2)]
    qe_t = [TL("qe%d_" % d, 32) for d in range(2)]
    AB_t, Gs_t = T("ABt"), T("Gs")
    oT_t = TL("oT", 32)
    Tst_t = TL("Tst", 2)
    mout_t = T("mout")

    lb = P.sbuf("lb", [64, 2, 512], F32)
    oml = P.sbuf("oml", [64, 2, 512], F32)
    lbl_t, lb_t, oml_t, lsm_t = T("lbl"), T("lb"), T("oml"), T("lsm")
    lbl = keF[:].rearrange("p a b c -> p (a b c)").bitcast(F32)[0:64, 0:1536].rearrange("p (s n) -> p s n", s=3)
    lsm = qe[:].rearrange("p a b c -> p (a b c)").bitcast(F32)[0:64, 0:512]
    for d in range(2):
        P.dma("sp", lambda e, d=d: e.dma_start(out=lbl, in_=lbr[:, d * 1536:(d + 1) * 1536].rearrange("p (s n) -> p s n", s=3)), writes=[lbl_t])
        P.op("dve", lambda e: e.tensor_tensor(out=lsm, in0=lbl[:, 0, :], in1=lbl[:, 1, :], op=ALU.max), reads=[lbl_t], writes=[lsm_t])
        P.op("dve", lambda e: e.tensor_tensor(out=lsm, in0=lsm, in1=lbl[:, 2, :], op=ALU.max), reads=[lbl_t, lsm_t], writes=[lsm_t])
        P.op("dve", lambda e: e.tensor_tensor(out=lbl, in0=lbl, in1=lsm.unsqueeze(1).to_broadcast([64, 3, 512]),
                                              op=ALU.subtract), reads=[lbl_t, lsm_t], writes=[lbl_t])
        P.op("act", lambda e: e.activation(out=lbl, in_=lbl, func=AF.Exp), reads=[lbl_t], writes=[lbl_t])
        P.op("dve", lambda e: e.tensor_tensor(out=lsm, in0=lbl[:, 0, :], in1=lbl[:, 1, :], op=ALU.add), reads=[lbl_t], writes=[lsm_t])
        P.op("dve", lambda e: e.tensor_tensor(out=lsm, in0=lsm, in1=lbl[:, 2, :], op=ALU.add), reads=[lbl_t, lsm_t], writes=[lsm_t])
        P.op("dve", lambda e: e.reciprocal(out=lsm, in_=lsm), reads=[lsm_t], writes=[lsm_t])
        P.op("dve", lambda e, d=d: e.tensor_tensor(out=lb[:, d, :], in0=lbl[:, 0, :], in1=lsm, op=ALU.mult), reads=[lbl_t, lsm_t], writes=[lb_t])
    P.op("dve", lambda e: e.tensor_scalar(out=oml[:], in0=lb[:], scalar1=-1.0, scalar2=1.0, op0=ALU.mult, op1=ALU.add),
         reads=[lb_t], writes=[oml_t])

    if upto == 0.5:
        P.wait_all("sp", [lb_t, oml_t, idb_t, cw_t, gn_t, cst_t])
        return

    def load_wg(dst, dst_t, g0, ng, hd):
        for gi in range(ng):
            c0 = (g0 + gi) * 512 + hd * 128
            P.dma("pool", lambda e, gi=gi, c0=c0: e.dma_start(
                out=dst[:, :, gi * 128:(gi + 1) * 128], in_=w[:, c0:c0 + 128].rearrange("(k p) n -> p k n", p=128)), writes=[dst_t])

    wseq = [(0, 3, i) for i in range(4)] + [(5, 3, i) for i in range(4)]
    wtiles = {}

    def issue_w(i):
        wt_h, wt_t = wtm_p.get()
        load_wg(wt_h, wt_t, *wseq[i])
        wtiles[i] = (wt_h, wt_t)
    wf_h, wf_t = wfm_view, wfm_t
    if prefetch:
        issue_w(0)
        load_wg(wf_h, wf_t, 3, 2, 0)
    for hd in range(4):
        if not prefetch:
            issue_w(hd)
            load_wg(wf_h, wf_t, 3, 2, hd)
        wt_h, wt_t = wtiles[hd]
        for tb in range(4):
            for gi in range(2):
                ph, pt = pp.get()

                def mm(e, ph=ph, gi=gi, tb=tb, wf_h=wf_h):
                    for k in range(KC):
                        ins = e.matmul(ph[:, :], lhsT=wf_h[:, k, gi * 128:(gi + 1) * 128], rhs=hT[:, k, tb * 512:(tb + 1) * 512],
                                       start=(k == 0), stop=(k == KC - 1))
                    return ins
                P.op("pe", mm, reads=[wf_t] + hT_t, writes=[pt])
                if gi == 0:
                    P.op("dve", lambda e, ph=ph, tb=tb: e.tensor_copy(out=qT[:, tb * 512:(tb + 1) * 512], in_=ph[:, :]),
                         reads=[pt], writes=[qT_t[tb]])
                else:
                    P.op("act", lambda e, ph=ph, tb=tb: e.activation(out=sgT[:, tb * 512:(tb + 1) * 512], in_=ph[:, :], func=AF.Silu),
                         reads=[pt], writes=[sgT_t[tb]])
        if upto == 0.7:
            P.wait_all("sp", qT_t + sgT_t)
            return
        if prefetch:
            issue_w(hd + 1)
            if hd + 1 < 4:
                load_wg(wf_h, wf_t, 3, 2, hd + 1)
        P.op("pool", lambda e: e.memset(oT[:], 0.0), writes=oT_t)
        P.op("pool", lambda e: e.memset(Gs[:], 0.0), writes=[Gs_t])
        lb_h = lb[:, :, hd * 128:(hd + 1) * 128].unsqueeze(2).to_broadcast([64, 2, 2, 128])
        oml_h = oml[:, :, hd * 128:(hd + 1) * 128].unsqueeze(2).to_broadcast([64, 2, 2, 128])

        def v4(h):
            return h[0:64, :].rearrange("p (c d n) -> p d c n", d=2, c=2)

        for j in range(18):
            c0 = 2 * j
            pa, pa_t = pp.get()
            pb, pb_t = pp.get()

            def mm(e, pa=pa, pb=pb, c0=c0, wt_h=wt_h):
                for ci in range(2):
                    tok0 = (c0 + ci) * 64
                    for k in range(KC):
                        e.matmul(pa[0:64, ci * 256:(ci + 1) * 256], lhsT=hT[:, k, tok0:tok0 + 64], rhs=wt_h[:, k, 0:256],
                                 start=(k == 0), stop=(k == KC - 1))
                    for k in range(KC):
                        ins = e.matmul(pb[0:64, ci * 128:(ci + 1) * 128], lhsT=hT[:, k, tok0:tok0 + 64], rhs=wt_h[:, k, 256:384],
                                       start=(k == 0), stop=(k == KC - 1))
                return ins
            P.op("pe", mm, reads=[wt_t] + hT_t, writes=[pa_t, pb_t])
            P.op("act", lambda e, pb=pb, c0=c0: e.activation(out=vtok[:, c0:c0 + 2, :], in_=pb[0:64, 0:256].rearrange("p (c n) -> p c n", c=2),
                                                             func=AF.Copy), reads=[pb_t], writes=vtok_t[c0:c0 + 2])
            sg_h, sg_t = gp_sig.get()
            P.op("act", lambda e, pa=pa, sg_h=sg_h: e.activation(out=sg_h[0:64, :], in_=pa[0:64, :], func=AF.Sigmoid), reads=[pa_t], writes=[sg_t])
            t1_h, t1_t = gp_t1.get()
            P.op("dve", lambda e, sg_h=sg_h, t1_h=t1_h, oml_h=oml_h: e.tensor_tensor(out=v4(t1_h), in0=v4(sg_h), in1=oml_h, op=ALU.mult),
                 reads=[sg_t, oml_t], writes=[t1_t])
            P.op("pool", lambda e, t1_h=t1_h, lb_h=lb_h: e.tensor_tensor(out=v4(t1_h), in0=v4(t1_h), in1=lb_h, op=ALU.add),
                 reads=[t1_t, lb_t], writes=[t1_t])
            lf_h, lf_t = gp_lf.get()
            P.op("act", lambda e, t1_h=t1_h, lf_h=lf_h: e.activation(out=lf_h[0:64, :], in_=t1_h[0:64, :], func=AF.Ln), reads=[t1_t], writes=[lf_t])
            om_h, om_t = gp_omf.get()
            P.op("dve", lambda e, t1_h=t1_h, om_h=om_h: e.tensor_scalar(out=om_h[0:64, :], in0=t1_h[0:64, :], scalar1=-1.0, scalar2=1.0,
                                                                        op0=ALU.mult, op1=ALU.add), reads=[t1_t], writes=[om_t])
            if upto == 0.8:
                P.wait_all("sp", [om_t, lf_t] + vtok_t[0:2])
                return
            pe2, pe2_t = pp.get()

            def mm2(e, pe2=pe2, lf_h=lf_h):
                for d in range(2):
                    for ci in range(2):
                        o0 = ci * 256 + d * 128
                        ins = e.matmul(pe2[0:64, o0:o0 + 128], lhsT=pm[:, d, 0:64], rhs=lf_h[0:64, o0:o0 + 128], start=True, stop=True)
                return ins
            P.op("pe", mm2, reads=[cst_t, lf_t], writes=[pe2_t])
            ee_h, ee_t = gp_ee.get()
            P.op("act", lambda e, pe2=pe2, ee_h=ee_h: e.activation(out=ee_h[0:64, :], in_=pe2[0:64, :], func=AF.Exp, scale=-1.0), reads=[pe2_t], writes=[ee_t])
            P.op("dve", lambda e, ee_h=ee_h, om_h=om_h, c0=c0: e.tensor_tensor(out=keT[:, :, c0:c0 + 2, :], in0=v4(ee_h), in1=v4(om_h), op=ALU.mult),
                 reads=[ee_t, om_t], writes=keT_t[0][c0:c0 + 2] + keT_t[1][c0:c0 + 2])
            pe1, pe1_t = pp.get()

            def mm1(e, pe1=pe1, lf_h=lf_h):
                for d in range(2):
                    for ci in range(2):
                        o0 = (d * 2 + ci) * 128
                        ins = e.matmul(pe1[:, o0:o0 + 66], lhsT=v4(lf_h)[:, d, ci, :], rhs=pm[:, d, :], start=True, stop=True)
                return ins
            P.op("pe", mm1, reads=[cst_t, lf_t], writes=[pe1_t])
            pe1v = pe1[:, 0:512].rearrange("p (d c n) -> p d c n", d=2, c=2)
            P.op("dve", lambda e, pe1v=pe1v, c0=c0: e.tensor_copy(out=ABt[:, :, c0:c0 + 2, :], in_=pe1v[:, :, :, 64:66]), reads=[pe1_t], writes=[AB_t])
            if upto == 0.9:
                P.wait_all("sp", [AB_t] + keT_t[0][0:2])
                return
            if j < 16:
                eq_h, eq_t = gp_eq.get()
                eqv = eq_h[:, :].rearrange("p (d c n) -> p d c n", d=2, c=2)
                P.op("dve", lambda e, pe1v=pe1v, eqv=eqv: e.tensor_copy(out=eqv, in_=pe1v[:, :, :, 0:64]), reads=[pe1_t], writes=[eq_t])
                P.op("act", lambda e, eq_h=eq_h: e.activation(out=eq_h[:, :], in_=eq_h[:, :], func=AF.Exp), reads=[eq_t], writes=[eq_t])
                if upto in (0.93, 0.931, 0.932, 0.933):
                    mo_h, mo_t = mo_p.get()
                    P.op("dve", lambda e, mo_h=mo_h, eq_h=eq_h: e.tensor_copy(out=mo_h[:, 0:256], in_=eq_h[:, :]), reads=[eq_t], writes=[mo_t])
                    P.op("dve", lambda e, mo_h=mo_h, lf_h=lf_h: e.tensor_copy(out=mo_h[0:64, 256:768], in_=lf_h[0:64, :]), reads=[lf_t], writes=[mo_t])
                    P.op("dve", lambda e, mo_h=mo_h: e.tensor_copy(out=mo_h[0:64, 768:1280].rearrange("p (d n) -> p d n", d=2), in_=lb[:, :, 0:256]), reads=[lb_t], writes=[mo_t])
                    P.dma("sp", lambda e, mo_h=mo_h: e.dma_start(out=m_out[0:128, 0:2048], in_=mo_h[:, :]), reads=[mo_t], writes=[T("dbg")], semtile=mo_t)
                    P.wait_all("sp", [mo_t])
                    return
                qv = qT[:, c0 * 64:(c0 + 2) * 64].rearrange("p (c n) -> p c n", c=2)
                for d in range(2):
                    P.op("dve", lambda e, eqv=eqv, qv=qv, c0=c0, d=d: e.tensor_tensor(out=qe[:, d, c0:c0 + 2, :], in0=eqv[:, d, :, :], in1=qv, op=ALU.mult),
                         reads=[eq_t, qT_t[j // 4]], writes=qe_t[d][c0:c0 + 2])
                if upto == 0.95:
                    P.wait_all("sp", qe_t[0][0:2])
                    return
                pk, pk_t = pp.get()

                def mmk(e, pk=pk, c0=c0):
                    for d in range(2):
                        for ci in range(2):
                            o0 = (d * 2 + ci) * 64
                            ins = e.matmul(pk[:, o0:o0 + 64], lhsT=keT[:, d, c0 + ci, :], rhs=idb[:, :], start=True, stop=True)
                    return ins
                P.op("pe", mmk, reads=[idb_t] + keT_t[0][c0:c0 + 2] + keT_t[1][c0:c0 + 2], writes=[pk_t])
                for d in range(2):
                    P.op("act", lambda e, pk=pk, c0=c0, d=d: e.activation(out=keF[:, d, c0:c0 + 2, :], in_=pk[:, d * 128:(d + 1) * 128].rearrange("p (c n) -> p c n", c=2),
                                                                      func=AF.Copy), reads=[pk_t], writes=keF_t[d][c0:c0 + 2])
        if upto == 1:
            dt_ = T("dbg")
            allt = qe_t[0] + qe_t[1] + keF_t[0] + keF_t[1]
            for i_, (src, d) in enumerate(((qe, 0), (keF, 0), (qe, 1), (keF, 1))):
                P.dma("sp", lambda e, src=src, d=d, i_=i_: e.dma_start(out=m_out[i_ * 128:(i_ + 1) * 128, :], in_=src[:, d, :, :].rearrange("p c n -> p (c n)")),
                      reads=allt, writes=[dt_], semtile=dt_)
            P.wait_all("sp", [dt_])
            return
        A_ = ABt[:, :, :, 0]
        B_ = ABt[:, :, :, 1]
        for (d, o0, o1, b0) in ((0, 32, 35, 33), (0, 35, 36, 0), (0, 0, 31, 1), (1, 33, 36, 32), (1, 32, 33, 31), (1, 1, 32, 0)):
            n = o1 - o0
            P.op("dve", lambda e, d=d, o0=o0, o1=o1, b0=b0, n=n: e.tensor_tensor(out=Gs[:, d, o0:o1], in0=A_[:, d, o0:o1], in1=B_[:, d, b0:b0 + n], op=ALU.add),
                 reads=[AB_t], writes=[Gs_t])
        P.op("act", lambda e: e.activation(out=Gs[:], in_=Gs[:], func=AF.Exp), reads=[Gs_t], writes=[Gs_t])
        seqs = (SEQ_F, SEQ_B)
        pend = []

        def stage1(i, d):
            c = seqs[d][i]
            prev = seqs[d][i - 1] if i > 0 else None
            pb_, pb_t = pp.get()
            info = None
            if c < 32:
                if d == 0:
                    blks = ((slice(0, 64), slice(32, 64)), (slice(0, 32), slice(0, 32)))
                else:
                    blks = ((slice(0, 64), slice(0, 32)), (slice(32, 64), slice(32, 64)))

                def mm1(e, pb_=pb_, d=d, c=c, blks=blks):
                    e.matmul(pb_[:, 0:128], lhsT=keT[:, d, c, :], rhs=vtok[:, c, :], start=True, stop=True)
                    for (ss, tt_) in blks:
                        ins = e.matmul(pb_[ss, 128 + tt_.start:128 + tt_.stop], lhsT=keF[:, d, c, ss], rhs=qe[:, d, c, tt_], start=True, stop=True)
                    return ins
                P.op("pe", mm1, reads=[keT_t[d][c], vtok_t[c], keF_t[d][c], qe_t[d][c]], writes=[pb_t])
                sb_h, sb_t = sbf_p[d].get()
                P.op("act", lambda e, sb_h=sb_h, d=d, prev=prev: e.activation(out=sb_h[:, :], in_=Tst[d][:, :], func=AF.Copy, scale=Gs[:, d, prev:prev + 1]),
                     reads=[Tst_t[d], Gs_t], writes=[sb_t])
                am_h, am_t = att_p[d].get()
                for (ss, tt_) in blks:
                    P.op("dve", lambda e, pb_=pb_, am_h=am_h, d=d, ss=ss, tt_=tt_: e.tensor_tensor(
                        out=am_h[ss, tt_], in0=pb_[ss, 128 + tt_.start:128 + tt_.stop], in1=mk[ss, d, tt_], op=ALU.mult), reads=[pb_t, cst_t], writes=[am_t])
                info = (pb_, pb_t, sb_h, sb_t, am_h, am_t, d, c)
            else:
                P.op("pe", lambda e, pb_=pb_, d=d, c=c: e.matmul(pb_[:, 0:128], lhsT=keT[:, d, c, :], rhs=vtok[:, c, :], start=True, stop=True),
                     reads=[keT_t[d][c], vtok_t[c]], writes=[pb_t])
            if i == 0:
                P.op("dve", lambda e, pb_=pb_, d=d: e.tensor_copy(out=Tst[d][:, :], in_=pb_[:, 0:128]), reads=[pb_t], writes=[Tst_t[d]])
            else:
                P.op("dve", lambda e, pb_=pb_, d=d, prev=prev: e.scalar_tensor_tensor(
                    out=Tst[d][:, :], in0=Tst[d][:, :], scalar=Gs[:, d, prev:prev + 1], in1=pb_[:, 0:128], op0=ALU.mult, op1=ALU.add),
                    reads=[pb_t, Tst_t[d], Gs_t], writes=[Tst_t[d]])
            return info

        def stage2(info):
            pb_, pb_t, sb_h, sb_t, am_h, am_t, d, c = info

            def mmo(e):
                e.matmul(pb_[:, 192:256], lhsT=sb_h[:, :], rhs=qe[:, d, c, :], start=True, stop=False)
                return e.matmul(pb_[:, 192:256], lhsT=vtok[:, c, :], rhs=am_h[:, :], start=False, stop=True)
            P.op("pe", mmo, reads=[sb_t, qe_t[d][c], vtok_t[c], am_t], writes=[pb_t])
            P.op("dve", lambda e: e.tensor_tensor(out=oT[:, c * 64:(c + 1) * 64], in0=pb_[:, 192:256], in1=oT[:, c * 64:(c + 1) * 64], op=ALU.add),
                 reads=[pb_t, oT_t[c]], writes=[oT_t[c]])

        for i in range(NCH):
            cur = [stage1(i, d) for d in range(2)]
            for info in pend:
                if info is not None:
                    stage2(info)
            pend = cur
        for info in pend:
            if info is not None:
                stage2(info)
        if upto == 2:
            mo_h, mo_t = mo_p.get()
            P.op("act", lambda e, mo_h=mo_h: e.activation(out=mo_h[:, :], in_=oT[:, :], func=AF.Copy), reads=oT_t, writes=[mo_t])
            P.dma("sp", lambda e, mo_h=mo_h: e.dma_start(out=m_out[0:128, :], in_=mo_h[:, :]), reads=[mo_t], writes=[T("dbg")], semtile=mo_t)
            P.wait_all("sp", [mo_t])
            return
        mo_h, mo_t = mo_p.get()
        for tb in range(4):
            sl = slice(tb * 512, (tb + 1) * 512)
            sq, sqt = sqpool.get()
            P.op("act", lambda e, sq=sq, sl=sl: e.activation(out=sq[:, 0:512], in_=oT[:, sl], func=AF.Square), reads=oT_t[tb * 8:(tb + 1) * 8], writes=[sqt])
            ph, pt = pp.get()
            P.op("pe", lambda e, ph=ph, sq=sq: e.matmul(ph[:, :], lhsT=ones[:], rhs=sq[:, 0:512], start=True, stop=True), reads=[sqt, ones_t], writes=[pt])
            th, tt = tmpp.get()
            P.op("act", lambda e, ph=ph, th=th: e.activation(out=th[:, 0:512], in_=ph[:, :], func=AF.Sqrt, scale=1.0 / 128, bias=epsb[:, 0:1]),
                 reads=[pt, eps_t], writes=[tt])
            P.op("dve", lambda e, th=th: e.reciprocal(out=th[:, 0:512], in_=th[:, 0:512]), reads=[tt], writes=[tt])
            P.op("dve", lambda e, th=th, sl=sl, hd=hd: e.scalar_tensor_tensor(out=th[:, 0:512], in0=oT[:, sl], scalar=gn[:, hd:hd + 1], in1=th[:, 0:512],
                                                                              op0=ALU.mult, op1=ALU.mult), reads=[tt, gn_t] + oT_t[tb * 8:(tb + 1) * 8], writes=[tt])
            P.op("pool", lambda e, th=th, sl=sl, mo_h=mo_h: e.tensor_tensor(out=mo_h[:, sl], in0=th[:, 0:512], in1=sgT[:, sl], op=ALU.mult),
                 reads=[tt, sgT_t[tb]], writes=[mo_t])
        P.dma("sp", lambda e, mo_h=mo_h, hd=hd: e.dma_start(out=m_out[mrow[0] + hd * 128:mrow[0] + (hd + 1) * 128, :], in_=mo_h[:, :]), reads=[mo_t], writes=[mout_t], semtile=mo_t)

    if upto == 3:
        P.wait_all("sp", [b[1] for b in mo_p.bufs])
        return
    cv_g, cv_t, cv_a = gp_sig, gp_t1, gp_lf
    for cc in range(4):
        if not prefetch:
            issue_w(4 + cc)
        elif cc + 1 < 4:
            issue_w(4 + cc + 1)
        wt_h, wt_t = wtiles[4 + cc]
        mo_h, mo_t = mo_p.get()
        for tb in range(4):
            sl = slice(tb * 512, (tb + 1) * 512)
            pss = []
            for gi in range(3):
                ph, pt = pp.get()

                def mm(e, ph=ph, gi=gi, sl=sl, wt_h=wt_h):
                    for k in range(KC):
                        ins = e.matmul(ph[:, :], lhsT=wt_h[:, k, gi * 128:(gi + 1) * 128], rhs=hT[:, k, sl], start=(k == 0), stop=(k == KC - 1))
                    return ins
                P.op("pe", mm, reads=[wt_t] + hT_t, writes=[pt])
                pss.append((ph, pt))
            (pu, pu_t), (pgb, pgb_t), (pgc, pgc_t) = pss
            g_h, g_t = cv_g.get()
            P.op("act", lambda e, pgc=pgc, g_h=g_h: e.activation(out=g_h[:, :], in_=pgc[:, :], func=AF.Copy), reads=[pgc_t], writes=[g_t])
            t_h, t_t = cv_t.get()
            P.op("dve", lambda e, pu=pu, g_h=g_h, t_h=t_h: e.tensor_tensor(out=t_h[:, :], in0=pu[:, :], in1=g_h[:, :], op=ALU.mult), reads=[pu_t, g_t], writes=[t_t])
            a_h, a_t = cv_a.get()
            P.op("act", lambda e, t_h=t_h, a_h=a_h, cc=cc: e.activation(out=a_h[:, :], in_=t_h[:, :], func=AF.Copy, scale=cw[:, cc * 3 + 1:cc * 3 + 2]),
                 reads=[t_t, cw_t], writes=[a_t])
            tv = t_h[:, :].rearrange("p (r j) -> p r j", j=64)
            av = a_h[:, :].rearrange("p (r j) -> p r j", j=64)
            P.op("dve", lambda e, tv=tv, av=av, cc=cc: e.scalar_tensor_tensor(out=av[:, :, 1:64], in0=tv[:, :, 0:63], scalar=cw[:, cc * 3:cc * 3 + 1], in1=av[:, :, 1:64],
                                                                              op0=ALU.mult, op1=ALU.add), reads=[t_t, a_t, cw_t], writes=[a_t])
            P.op("dve", lambda e, tv=tv, av=av, cc=cc: e.scalar_tensor_tensor(out=av[:, :, 0:63], in0=tv[:, :, 1:64], scalar=cw[:, cc * 3 + 2:cc * 3 + 3], in1=av[:, :, 0:63],
                                                                              op0=ALU.mult, op1=ALU.add), reads=[t_t, a_t, cw_t], writes=[a_t])
            P.op("dve", lambda e, pgb=pgb, a_h=a_h, mo_h=mo_h, sl=sl: e.tensor_tensor(out=mo_h[:, sl], in0=pgb[:, :], in1=a_h[:, :], op=ALU.mult),
                 reads=[pgb_t, a_t], writes=[mo_t])
        P.dma("sp", lambda e, mo_h=mo_h, cc=cc: e.dma_start(out=m_out[mrow[1] + cc * 128:mrow[1] + (cc + 1) * 128, :], in_=mo_h[:, :]),
              reads=[mo_t], writes=[mout_t], semtile=mo_t)
    P.wait_all("sp", [b[1] for b in mo_p.bufs] + hsv_t)
    P.wait_all("act", hsv_t)


def build_mix0(upto=99):
    nc = bass.Bass("TRN2", target_bir_lowering=False)
    xT = dram_in(nc, "i_xT", [D, NTT])
    scal_d = dram_in(nc, "i_scal", [128, 5 * KC])
    w = dram_in(nc, "i_w", [D, 4096])
    lbr = dram_in(nc, "i_lb", [64, 3072])
    cw_d = dram_in(nc, "i_cw", [128, 12])
    gn_d = dram_in(nc, "i_gn", [128, 4])
    cst_d = dram_in(nc, "i_cst", [64, 324])
    m_out = dram_out(nc, "o_m", [1024, 2048], BF16)
    P = Prog(nc)
    emit_mix0(P, nc, xT, scal_d, w, lbr, cw_d, gn_d, cst_d, m_out, upto)
    P.emit()
    P.close()
    return nc


L = 2048
NF = 4096


def hy_tables():
    t = np.arange(L, dtype=np.float64)
    k = np.arange(L, dtype=np.float64) + 0.5
    ang = 2.0 * np.pi * np.outer(t, k) / NF
    Cm, Sm = np.cos(ang), np.sin(ang)
    def fwd(M):
        return np.ascontiguousarray(M.reshape(16, 128, 16, 128).transpose(2, 1, 0, 3)).astype(NPBF)
    Ci = (2.0 / NF) * Cm.T.reshape(16, 128, 4, 512)
    Si = (2.0 / NF) * Sm.T.reshape(16, 128, 4, 512)
    inv = np.concatenate([Ci, Si], axis=0).transpose(2, 1, 0, 3)
    return fwd(Cm), fwd(Sm), np.ascontiguousarray(inv).astype(NPBF)


def hy_consts(hh):
    pos = np.arange(L, dtype=np.float32)
    t = np.linspace(0.0, 1.0, L, dtype=np.float32)
    w = (2.0 * np.pi * pos / L).astype(np.float32)
    bands = np.linspace(1e-4, 15.0, 16, dtype=np.float32)
    ang = w[:, None] * bands[None, :]
    zemb = np.concatenate([t[:, None], np.cos(ang), -np.sin(ang)], axis=-1).astype(np.float32)
    deltas = np.abs(np.linspace(np.log(1e-2) / 1.5, np.log(1e-2) / 0.3, D, dtype=np.float32))
    win = np.exp(-t[:, None] * deltas[None, hh * 1024:(hh + 1) * 1024]).astype(np.float32)
    return np.ascontiguousarray(zemb.T), win


def emit_hy(P, nc, hT_d, w, sw_d, skip_d, fw1_d, fw23_d, fvec_d, fw4_d, zemb_d, win_d, id_d, cf_d, sf_d, inv_d, z3_out, upto=99):
    PI = float(np.pi)
    x1s = nc.dram_tensor(P.pfx + "s_x1", [1024, L], BF16).ap()
    x2s = nc.dram_tensor(P.pfx + "s_x2", [1024, L], BF16).ap()
    zfs = [nc.dram_tensor(P.pfx + "s_zf%d" % i, [1024, L], BF16).ap() for i in range(2)]
    x1s_t, x2s_t = TL("sx1_", 8), TL("sx2_", 8)
    zfs_t = [TL("szf%d_" % i, 8) for i in range(2)]

    hT = P.sbuf("hT", [128, KC, L], BF16)
    hT_t = TL("hT", KC)
    Yv = hT[:].rearrange("p a (b n) -> p (a b) n", b=2)
    big = P.sbuf("big64", [128, 2, 16384], BF16)
    big_t = TL("big", 2)
    z_tm = P.sbuf("z_tm", [128, 16, 1024], BF16)
    ztm_t = TL("ztm", 16)
    arena = P.sbuf("arena", [128, 16384], BF16)

    def carve(off_kib, nbytes, dt, pat=None, **kw):
        a = arena[:, off_kib * 512:off_kib * 512 + nbytes // 2]
        if dt == F32:
            a = a.bitcast(F32)
        return a.rearrange(pat, **kw) if pat else a
    wpool = Pool.views("hw", [carve(0, 8192, BF16, "p (k n) -> p k n", k=KC), carve(8, 8192, BF16, "p (k n) -> p k n", k=KC)])
    pp = Pool(P, "pp", 8, [128, 512], F32, psum=True)
    idf = P.sbuf("idf", [128, 128], F32)
    idb = P.sbuf("idb", [128, 128], BF16)
    sw = P.sbuf("sw", [128, 72], F32)
    skip = P.sbuf("skip", [128, 16], F32)
    onesf = P.sbuf("onesf", [128, 1], F32)
    idf_t, idb_t, sw_t, skip_t, onesf_t = T("idf"), T("idb"), T("sw"), T("skip"), T("onesf")
    P.dma("sp", lambda e: e.dma_start(out=idf[:], in_=id_d), writes=[idf_t])
    P.op("dve", lambda e: e.tensor_copy(out=idb[:], in_=idf[:]), reads=[idf_t], writes=[idb_t])
    P.dma("sp", lambda e: e.dma_start(out=sw[:], in_=sw_d), writes=[sw_t])
    P.dma("sp", lambda e: e.dma_start(out=skip[:], in_=skip_d), writes=[skip_t])
    P.op("pool", lambda e: e.memset(onesf[:], 1.0), writes=[onesf_t])
    for k in range(KC):
        q = "sp"
        P.dma(q, lambda e, k=k: e.dma_start(out=hT[:, k, :], in_=hT_d[k * 128:(k + 1) * 128, :]), writes=[hT_t[k]])

    cvo = Pool.views("cvo", [carve(16, 4096, BF16), carve(20, 4096, BF16)])
    cva = Pool.views("cva", [carve(24, 2048, F32), carve(26, 2048, F32)])
    wnext = load_w_block(P, wpool, w, KC, 0, 256)
    for blk in range(12):
        wh, wt = wnext
        if blk + 1 < 12:
            wnext = load_w_block(P, wpool, w, KC, (blk + 1) * 256, 256)
        for ci in range(2):
            j = blk * 2 + ci
            grp, cc = j // 8, j % 8
            oh, ot = cvo.get()
            for tb in range(4):
                sl = slice(tb * 512, (tb + 1) * 512)
                ph, pt = pp.get()

                def mm(e, ph=ph, wh=wh, ci=ci, sl=sl):
                    for k in range(KC):
                        ins = e.matmul(ph[:, :], lhsT=wh[:, k, ci * 128:(ci + 1) * 128], rhs=hT[:, k, sl], start=(k == 0), stop=(k == KC - 1))
                    return ins
                P.op("pe", mm, reads=[wt] + hT_t, writes=[pt])
                ah, at = cva.get()
                P.op("act", lambda e, ph=ph, ah=ah, j=j: e.activation(out=ah[:, :], in_=ph[:, :], func=AF.Copy, scale=sw[:, j * 3 + 1:j * 3 + 2]),
                     reads=[pt, sw_t], writes=[at])
                pv = ph[:, :].rearrange("p (r c) -> p r c", c=64)
                av = ah[:, :].rearrange("p (r c) -> p r c", c=64)
                P.op("dve", lambda e, pv=pv, av=av, j=j: e.scalar_tensor_tensor(out=av[:, :, 1:64], in0=pv[:, :, 0:63], scalar=sw[:, j * 3:j * 3 + 1], in1=av[:, :, 1:64],
                                                                                op0=ALU.mult, op1=ALU.add), reads=[pt, at, sw_t], writes=[at])
                P.op("dve", lambda e, pv=pv, av=av, j=j: e.scalar_tensor_tensor(out=av[:, :, 0:63], in0=pv[:, :, 1:64], scalar=sw[:, j * 3 + 2:j * 3 + 3], in1=av[:, :, 0:63],
                                                                                op0=ALU.mult, op1=ALU.add), reads=[pt, at, sw_t], writes=[at])
                P.op("pool", lambda e, ah=ah, oh=oh, sl=sl: e.tensor_copy(out=oh[:, sl], in_=ah[:, :]), reads=[at], writes=[ot])
            dst, dst_t = ((x1s, x1s_t), (x2s, x2s_t), (zfs[0], zfs_t[0]))[grp]
            P.dma("sp", lambda e, oh=oh, dst=dst, cc=cc: e.dma_start(out=dst[cc * 128:(cc + 1) * 128, :], in_=oh[:, :]), reads=[ot], writes=[dst_t[cc]], semtile=ot)
            if grp == 2:
                for tq in range(4):
                    ph, pt = pp.get()

                    def mmt(e, ph=ph, oh=oh, tq=tq):
                        for i in range(4):
                            tc = tq * 4 + i
                            ins = e.matmul(ph[:, i * 128:(i + 1) * 128], lhsT=oh[:, tc * 128:(tc + 1) * 128], rhs=idb[:, :], start=True, stop=True)
                        return ins
                    P.op("pe", mmt, reads=[ot, idb_t], writes=[pt])
                    P.op("dve", lambda e, ph=ph, tq=tq, cc=cc: e.tensor_copy(out=z_tm[:, tq * 4:(tq + 1) * 4, cc * 128:(cc + 1) * 128],
                                                                             in_=ph[:, :].rearrange("p (i n) -> p i n", i=4)), reads=[pt], writes=ztm_t[tq * 4:(tq + 1) * 4])

    fw1 = P.sbuf("fw1", [33, 64], F32)
    fw23 = P.sbuf("fw23", [64, 128], F32)
    fvec = P.sbuf("fvec", [64, 5], F32)
    P.barrier()
    zemb = carve(0, 8192, F32)[0:33, :]
    hda = P.sbuf("hda", [64, L], F32)
    hdb = carve(8, 8192, F32)[0:64, :]
    fw1_t, fw23_t, fvec_t, fw4_t, zemb_t, hda_t, hdb_t = T("fw1"), T("fw23"), T("fvec"), T("fw4"), T("zemb"), T("hda"), T("hdb")
    ep_o = None
    P.dma("sp", lambda e: e.dma_start(out=fw1[:], in_=fw1_d), writes=[fw1_t])
    P.dma("sp", lambda e: e.dma_start(out=fw23[:], in_=fw23_d), writes=[fw23_t])
    P.dma("sp", lambda e: e.dma_start(out=fvec[:], in_=fvec_d), writes=[fvec_t])
    P.dma("sp", lambda e: e.dma_start(out=zemb, in_=zemb_d), writes=[zemb_t])
    f2pi = P.sbuf("f2pi", [64, 1], F32)
    f2pi_t, ri_t, rf_t = T("f2pi"), T("rint_i"), T("rint_f")
    P.op("dve", lambda e: e.tensor_scalar(out=f2pi[:, :], in0=fvec[:, 3:4], scalar1=1.0 / (2.0 * PI), scalar2=None, op0=ALU.mult), reads=[fvec_t], writes=[f2pi_t])
    rint_i = carve(16, 2048, F32)[0:64, :].bitcast(mybir.dt.int32)
    rint_f = carve(18, 2048, F32)[0:64, :]
    layers = ((fw1[:, :], fw1_t, zemb, zemb_t, 33, hda, hda_t, 0), (fw23[:, 0:64], fw23_t, hda, hda_t, 64, hdb, hdb_t, 1),
              (fw23[:, 64:128], fw23_t, hdb, hdb_t, 64, hda, hda_t, 2))
    for (wl, wl_t, src, src_t, kin, dst, dst_t, li) in layers:
        for tb in range(4):
            sl = slice(tb * 512, (tb + 1) * 512)
            ph, pt = pp.get()
            P.op("pe", lambda e, ph=ph, wl=wl, src=src, kin=kin, sl=sl: e.matmul(ph[0:64, :], lhsT=wl, rhs=src[0:kin, sl], start=True, stop=True),
                 reads=[wl_t, src_t], writes=[pt])
            P.op("dve", lambda e, ph=ph, dst=dst, sl=sl, li=li: e.tensor_scalar(out=dst[:, sl], in0=ph[0:64, :], scalar1=fvec[:, li:li + 1], scalar2=f2pi[:, 0:1],
                                                                               op0=ALU.add, op1=ALU.mult), reads=[pt, fvec_t, f2pi_t], writes=[dst_t])
            P.op("dve", lambda e, dst=dst, sl=sl: e.tensor_copy(out=rint_i[:, :], in_=dst[:, sl]), reads=[dst_t], writes=[ri_t])
            P.op("dve", lambda e: e.tensor_copy(out=rint_f[:, :], in_=rint_i[:, :]), reads=[ri_t], writes=[rf_t])
            P.op("dve", lambda e, dst=dst, sl=sl: e.tensor_tensor(out=dst[:, sl], in0=dst[:, sl], in1=rint_f[:, :], op=ALU.subtract), reads=[dst_t, rf_t], writes=[dst_t])
            P.op("act", lambda e, dst=dst, sl=sl: e.activation(out=dst[:, sl], in_=dst[:, sl], func=AF.Sin, scale=2.0 * PI * (1.0 - 1e-6)), reads=[dst_t], writes=[dst_t])
    hd3, hd3_t = hda, hda_t
    if upto == 0:
        dbg_t = T("dbg")
        P.barrier()
        th, tt = carve(16, 4096, BF16), T("dbgt")
        P.op("dve", lambda e, th=th: e.tensor_copy(out=th[0:64, :], in_=hd3[:, :]), reads=[hd3_t], writes=[tt])
        P.dma("sp", lambda e, th=th: e.dma_start(out=z3_out[0:64, :], in_=th[0:64, :]), reads=[tt], writes=[dbg_t], semtile=tt)
        P.wait_all("sp", [tt] + x1s_t + x2s_t + zfs_t[0])
        return

    acc_t = T("nacc")
    rnorm = P.sbuf("rnorm", [128, 16], F32)
    rnorm_t = T("rnorm")
    Av = big[:, 0, :].rearrange("p (c n) -> p c n", c=16)
    Bv = big[:, 1, :].rearrange("p (c n) -> p c n", c=16)
    out_t = T("z3out")
    for o in range(2):
        P.barrier()
        fw4 = carve(0, 8192, F32)
        P.dma("sp", lambda e, fw4=fw4, o=o: e.dma_start(out=fw4[0:64, :], in_=fw4_d[:, o * 2048:(o + 1) * 2048]), writes=[fw4_t])
        wn_p = Pool.views("wn%d" % o, [carve(8, 4096, F32), carve(12, 4096, F32)])
        fwv_p = Pool.views("fwv%d" % o, [carve(16, 4096, F32)])
        bwv_p = Pool.views("bwv%d" % o, [carve(20, 4096, F32)])
        abs_p = Pool.views("abs%d" % o, [carve(24, 4096, F32)])
        acc = carve(28, 4096, F32)
        for pc in range(16):
            wnh, wnt = wn_p.get()
            P.dma("sp", lambda e, wnh=wnh, pc=pc: e.dma_start(out=wnh[:, :], in_=win_d[pc * 128:(pc + 1) * 128, :]), writes=[wnt])
            vals = []
            for side, pool_ in ((0, fwv_p), (1, bwv_p)):
                vh, vt = pool_.get()
                for cb in range(2):
                    ph, pt = pp.get()
                    c0 = side * 1024 + cb * 512
                    P.op("pe", lambda e, ph=ph, pc=pc, c0=c0, fw4=fw4: e.matmul(ph[:, :], lhsT=hd3[:, pc * 128:(pc + 1) * 128], rhs=fw4[0:64, c0:c0 + 512], start=True, stop=True),
                         reads=[hd3_t, fw4_t], writes=[pt])
                    P.op("dve", lambda e, ph=ph, vh=vh, wnh=wnh, cb=cb: e.tensor_tensor(out=vh[:, cb * 512:(cb + 1) * 512], in0=ph[:, :], in1=wnh[:, cb * 512:(cb + 1) * 512], op=ALU.mult),
                         reads=[pt, wnt], writes=[vt])
                vals.append((vh, vt))
            (fh, ft), (bh, bt) = vals
            P.op("pool", lambda e, fh=fh, bh=bh, pc=pc: e.tensor_tensor(out=Av[:, pc, :], in0=fh[:, :], in1=bh[:, :], op=ALU.add), reads=[ft, bt], writes=[big_t[0]])
            P.op("pool", lambda e, fh=fh, bh=bh, pc=pc: e.tensor_tensor(out=Bv[:, pc, :], in0=fh[:, :], in1=bh[:, :], op=ALU.subtract), reads=[ft, bt], writes=[big_t[1]])
            abh, abt = abs_p.get()
            P.op("act", lambda e, fh=fh, abh=abh: e.activation(out=abh[:, :], in_=fh[:, :], func=AF.Abs), reads=[ft], writes=[abt])
            P.op("act", lambda e, bh=bh: e.activation(out=bh[:, :], in_=bh[:, :], func=AF.Abs), reads=[bt], writes=[bt])
            if pc == 0:
                P.op("dve", lambda e, fh=fh, bh=bh, abh=abh: e.tensor_tensor(out=abh[:, :], in0=abh[:, :], in1=bh[:, :], op=ALU.add), reads=[abt, bt], writes=[abt])
                P.op("act", lambda e, abh=abh: e.activation(out=abh[0:1, :], in_=Av[0:1, 0, :], func=AF.Abs), reads=[big_t[0], abt], writes=[abt])
                P.op("dve", lambda e, abh=abh: e.tensor_copy(out=acc[:, :], in_=abh[:, :]), reads=[abt], writes=[acc_t])
            else:
                P.op("dve", lambda e, fh=fh, bh=bh, abh=abh: e.tensor_tensor(out=abh[:, :], in0=abh[:, :], in1=bh[:, :], op=ALU.add), reads=[abt, bt], writes=[abt])
                P.op("dve", lambda e, abh=abh: e.tensor_tensor(out=acc[:, :], in0=acc[:, :], in1=abh[:, :], op=ALU.add), reads=[abt, acc_t], writes=[acc_t])
        ph, pt = pp.get()

        def mmn(e, ph=ph):
            for cc in range(8):
                ins = e.matmul(ph[:, cc:cc + 1], lhsT=acc[:, cc * 128:(cc + 1) * 128], rhs=onesf[:, 0:1], start=True, stop=True)
            return ins
        P.op("pe", mmn, reads=[acc_t, onesf_t], writes=[pt])
        P.op("dve", lambda e, ph=ph, o=o: e.reciprocal(out=rnorm[:, o * 8:(o + 1) * 8], in_=ph[:, 0:8]), reads=[pt], writes=[rnorm_t])
        if upto == 1:
            P.barrier()
            th, tt = carve(0, 4096, BF16), T("dbgt")
            P.op("dve", lambda e, th=th: e.tensor_copy(out=th[:, 0:1024], in_=Av[:, 0, :]), reads=[big_t[0]], writes=[tt])
            P.op("dve", lambda e, th=th: e.tensor_copy(out=th[:, 1024:1032], in_=rnorm[:, 0:8]), reads=[rnorm_t], writes=[tt])
            P.dma("sp", lambda e, th=th: e.dma_start(out=z3_out[0:128, :], in_=th[:, :]), reads=[tt], writes=[T("dbg")], semtile=tt)
            P.wait_all("sp", [tt] + x1s_t + x2s_t + zfs_t[0])
            return
        P.barrier()
        tab_p = Pool.views("ftab%d" % o, [carve(0, 8192, BF16, "p (a c n) -> p a c n", a=2, c=16), carve(8, 8192, BF16, "p (a c n) -> p a c n", a=2, c=16)])
        hsb_p = Pool.views("hsb%d" % o, [carve(16, 8192, F32, "p (a n) -> p a n", a=2)])
        tm_p = Pool.views("tmul%d" % o, [carve(24, 8192, F32, "p (a n) -> p a n", a=2)])
        for kc in range(16):
            th_, tt_ = tab_p.get()
            P.dma("sp", lambda e, th_=th_, kc=kc: e.dma_start(out=th_[:, 0, :, :], in_=cf_d[kc]), writes=[tt_])
            P.dma("sp", lambda e, th_=th_, kc=kc: e.dma_start(out=th_[:, 1, :, :], in_=sf_d[kc]), writes=[tt_])
            banks = [pp.get() for _ in range(8)]

            def mmf(e, th_=th_, banks=banks, lo=0, hi=8):
                for bi in range(lo, hi):
                    which, trig, cb = bi // 4, (bi // 2) % 2, bi % 2
                    for tc in range(16):
                        if which == 0:
                            rhs = z_tm[:, tc, cb * 512:(cb + 1) * 512]
                        else:
                            rhs = (Av if trig == 0 else Bv)[:, tc, cb * 512:(cb + 1) * 512]
                        ins = e.matmul(banks[bi][0][:, :], lhsT=th_[:, trig, tc, :], rhs=rhs, start=(tc == 0), stop=(tc == 15))
                return ins
            P.op("pe", lambda e, mmf=mmf: mmf(e, lo=4, hi=8), reads=[tt_] + big_t, writes=[b_[1] for b_ in banks[4:8]])
            P.op("pe", lambda e, mmf=mmf: mmf(e, lo=0, hi=4), reads=[tt_] + ztm_t, writes=[b_[1] for b_ in banks[0:4]])
            hh_, ht_ = hsb_p.get()
            for trig in range(2):
                for cb in range(2):
                    P.op("act", lambda e, hh_=hh_, trig=trig, cb=cb, banks=banks: e.activation(out=hh_[:, trig, cb * 512:(cb + 1) * 512], in_=banks[4 + trig * 2 + cb][0][:, :], func=AF.Copy),
                         reads=[banks[4 + trig * 2 + cb][1]], writes=[ht_])
            tmh, tmt = tm_p.get()
            for half, pairs, op_, yk in ((0, ((0, 0), (1, 1)), ALU.subtract, kc), (1, ((0, 1), (1, 0)), ALU.add, 16 + kc)):
                for ti, (zi, hi) in enumerate(pairs):
                    for cb in range(2):
                        P.op("dve", lambda e, tmh=tmh, ti=ti, zi=zi, hi=hi, cb=cb, banks=banks, hh_=hh_: e.tensor_tensor(
                            out=tmh[:, ti, cb * 512:(cb + 1) * 512], in0=banks[zi * 2 + cb][0][:, :], in1=hh_[:, hi, cb * 512:(cb + 1) * 512], op=ALU.mult),
                            reads=[banks[zi * 2 + cb][1], ht_], writes=[tmt])
                P.op("pool", lambda e, tmh=tmh, yk=yk, op_=op_: e.tensor_tensor(out=Yv[:, yk, :], in0=tmh[:, 0, :], in1=tmh[:, 1, :], op=op_), reads=[tmt], writes=[hT_t[yk // 2]])
        P.barrier()
        ep_z = Pool.views("ep_z%d" % o, [carve(0, 1024, BF16), carve(1, 1024, BF16)])
        ep_g = Pool.views("ep_g%d" % o, [carve(2, 1024, BF16), carve(3, 1024, BF16)])
        ep_f = Pool.views("ep_f%d" % o, [carve(4, 2048, F32), carve(6, 2048, F32)])
        ep_u = Pool.views("ep_u%d" % o, [carve(8, 2048, F32), carve(10, 2048, F32)])
        ep_o = Pool.views("ep_o%d" % o, [carve(12, 1024, BF16), carve(13, 1024, BF16)])
        gsrc, gsrc_t = (x1s, x1s_t) if o == 0 else (x2s, x2s_t)
        pend_tr = []

        def do_transposes(oh, ot, nb, cc):
            ph2, pt2 = pp.get()

            def mmt2(e, ph2=ph2, oh=oh):
                for i in range(4):
                    ins = e.matmul(ph2[:, i * 128:(i + 1) * 128], lhsT=oh[:, i * 128:(i + 1) * 128], rhs=idb[:, :], start=True, stop=True)
                return ins
            P.op("pe", mmt2, reads=[ot, idb_t], writes=[pt2])
            P.op("act", lambda e, ph2=ph2, nb=nb, cc=cc: e.activation(out=z_tm[:, nb * 4:(nb + 1) * 4, cc * 128:(cc + 1) * 128],
                                                                     in_=ph2[:, :].rearrange("p (i n) -> p i n", i=4), func=AF.Copy), reads=[pt2], writes=ztm_t[nb * 4:(nb + 1) * 4])
        for nb in range(4):
            tabv = big[:, nb % 2, :].rearrange("p (c n) -> p c n", c=32)
            P.dma("sp", lambda e, tabv=tabv, nb=nb: e.dma_start(out=tabv[:, 0:16, :], in_=inv_d[nb][:, 0:16, :]), writes=[big_t[nb % 2]])
            P.dma("sp", lambda e, tabv=tabv, nb=nb: e.dma_start(out=tabv[:, 16:32, :], in_=inv_d[nb][:, 16:32, :]), writes=[big_t[nb % 2]])
            sl = slice(nb * 512, (nb + 1) * 512)
            for cc in range(8):
                zh, zt = ep_z.get()
                gh, gt = ep_g.get()
                P.dma("sp", lambda e, zh=zh, cc=cc, sl=sl, o=o: e.dma_start(out=zh[:, :], in_=zfs[o][cc * 128:(cc + 1) * 128, sl]), reads=[zfs_t[o][cc]], writes=[zt])
                P.dma("sp", lambda e, gh=gh, cc=cc, sl=sl, gsrc=gsrc: e.dma_start(out=gh[:, :], in_=gsrc[cc * 128:(cc + 1) * 128, sl]), reads=[gsrc_t[cc]], writes=[gt])
                ph, pt = pp.get()

                def mmi(e, ph=ph, tabv=tabv, cc=cc):
                    for kc in range(32):
                        ins = e.matmul(ph[:, :], lhsT=Yv[:, kc, cc * 128:(cc + 1) * 128], rhs=tabv[:, kc, :], start=(kc == 0), stop=(kc == 31))
                    return ins
                P.op("pe", mmi, reads=hT_t + [big_t[nb % 2]], writes=[pt])
                fh_, ft_ = ep_f.get()
                P.op("act", lambda e, zh=zh, fh_=fh_, o=o, cc=cc: e.activation(out=fh_[:, :], in_=zh[:, :], func=AF.Copy, scale=skip[:, o * 8 + cc:o * 8 + cc + 1]),
                     reads=[zt, skip_t], writes=[ft_])
                uh, ut = ep_u.get()
                P.op("dve", lambda e, ph=ph, uh=uh, fh_=fh_, o=o, cc=cc: e.scalar_tensor_tensor(out=uh[:, :], in0=ph[:, :], scalar=rnorm[:, o * 8 + cc:o * 8 + cc + 1], in1=fh_[:, :],
                                                                                              op0=ALU.mult, op1=ALU.add), reads=[pt, rnorm_t, ft_], writes=[ut])
                oh, ot = ep_o.get()
                P.op("pool", lambda e, uh=uh, gh=gh, oh=oh: e.tensor_tensor(out=oh[:, :], in0=uh[:, :], in1=gh[:, :], op=ALU.mult), reads=[ut, gt], writes=[ot])
                if o == 0:
                    P.dma("pool", lambda e, oh=oh, cc=cc, sl=sl: e.dma_start(out=zfs[1][cc * 128:(cc + 1) * 128, sl], in_=oh[:, :]), reads=[ot], writes=[zfs_t[1][cc]], semtile=ot)
                    pend_tr.append((oh, ot, nb, cc))
                    if len(pend_tr) > 1:
                        do_transposes(*pend_tr.pop(0))
                else:
                    P.dma("pool", lambda e, oh=oh, cc=cc, sl=sl: e.dma_start(out=z3_out[cc * 128:(cc + 1) * 128, sl], in_=oh[:, :]), reads=[ot], writes=[out_t], semtile=ot)
        while pend_tr:
            do_transposes(*pend_tr.pop(0))
        if upto == 2 and o == 0:
            P.wait_all("sp", zfs_t[1] + x1s_t + x2s_t + zfs_t[0])
            return
    P.wait_all("sp", [b_[1] for b_ in ep_o.bufs] + zfs_t[1])


def build_hy(upto=99):
    nc = bass.Bass("TRN2", target_bir_lowering=False)
    hT_d = dram_in(nc, "i_hT", [D, L], BF16)
    w = dram_in(nc, "i_w", [D, 3072])
    sw_d = dram_in(nc, "i_sw", [128, 72])
    skip_d = dram_in(nc, "i_skip", [128, 16])
    fw1_d = dram_in(nc, "i_fw1", [33, 64])
    fw23_d = dram_in(nc, "i_fw23", [64, 128])
    fvec_d = dram_in(nc, "i_fvec", [64, 5])
    fw4_d = dram_in(nc, "i_fw4", [64, 4096])
    zemb_d = dram_in(nc, "i_zemb", [33, L])
    win_d = dram_in(nc, "i_win", [L, 1024])
    id_d = dram_in(nc, "i_id", [128, 128])
    cf_d = dram_in(nc, "i_cf", [16, 128, 16, 128], BF16)
    sf_d = dram_in(nc, "i_sf", [16, 128, 16, 128], BF16)
    inv_d = dram_in(nc, "i_inv", [4, 128, 32, 512], BF16)
    z3_out = dram_out(nc, "o_z3", [1024, L], BF16)
    P = Prog(nc)
    emit_hy(P, nc, hT_d, w, sw_d, skip_d, fw1_d, fw23_d, fvec_d, fw4_d, zemb_d, win_d, id_d, cf_d, sf_d, inv_d, z3_out, upto)
    P.emit()
    P.close()
    return nc


NCHK = 192


def emit_mod_full(P, nc, cT, aws, ab, s_mod):
    cs = P.sbuf("cs", [128, KC, 2], F32)
    sil = P.sbuf("sil", [128, KC, 2], BF16)
    sg = P.sbuf("sg", [128, KC, 2], F32)
    bs = P.sbuf("bs", [128, NCHK], F32)
    res = P.sbuf("res", [128, 2, NCHK], F32)
    ps = P.psum("mps", [128, 512])
    wpool = Pool(P, "mw", 4, [128, KC, 512], BF16)
    tcs, tsil, tsg, tbs, tres, tps, tout = T("cs"), T("sil"), T("sg"), T("bs"), T("res"), T("mps"), T("mout")
    P.dma("sp", lambda e: e.dma_start(out=cs[:], in_=cT.rearrange("(k p) j -> p k j", p=128)), writes=[tcs])
    P.dma("sp", lambda e: e.dma_start(out=bs[:], in_=ab), writes=[tbs])
    P.op("act", lambda e: e.activation(out=sg[:], in_=cs[:], func=AF.Sigmoid), reads=[tcs], writes=[tsg])
    P.op("dve", lambda e: e.tensor_tensor(out=sil[:], in0=cs[:], in1=sg[:], op=ALU.mult), reads=[tcs, tsg], writes=[tsil])
    for blk in range(48):
        wh, wt = wpool.get()
        aw = aws[blk // 24]
        c0 = (blk % 24) * 512
        q = "pool"
        P.dma(q, lambda e, wh=wh, aw=aw, c0=c0: e.dma_start(out=wh[:], in_=aw[:, c0:c0 + 512].rearrange("(k p) n -> p k n", p=128)), writes=[wt])

        def mm(e, wh=wh, blk=blk):
            for ci in range(4):
                i0 = (blk * 4 + ci) * 2
                for k in range(KC):
                    ins = e.matmul(ps[:, i0:i0 + 2], lhsT=wh[:, k, ci * 128:(ci + 1) * 128], rhs=sil[:, k, :], start=(k == 0), stop=(k == KC - 1))
            return ins
        P.op("pe", mm, reads=[wt, tsil], writes=[tps])
    psv = ps[:, 0:2 * NCHK].rearrange("p (c j) -> p c j", j=2)
    for j in range(2):
        P.op("dve", lambda e, j=j: e.tensor_tensor(out=res[:, j, :], in0=psv[:, :, j], in1=bs[:, :], op=ALU.add), reads=[tps, tbs], writes=[tres])
    P.dma("sp", lambda e: e.dma_start(out=s_mod, in_=res[:]), reads=[tres], writes=[tout])
    P.wait_all("sp", [tout])


def build_fused():
    nc = bass.Bass("TRN2", target_bir_lowering=False)
    I = lambda name, shape, dt=F32: dram_in(nc, name, shape, dt)
    xT = I("i_xT", [D, NTT])
    cT = I("i_cT", [D, 2])
    aws = [I("i_aw0", [D, 6 * D]), I("i_aw1", [D, 6 * D])]
    ab = I("i_ab", [128, NCHK])
    ng = I("i_ng", [128, 6 * KC])
    wB = [I("i_w0", [D, 4096]), I("i_w1", [D, 4096])]
    lbB = [I("i_lb0", [64, 3072]), I("i_lb1", [64, 3072])]
    cwB = [I("i_cw0", [128, 12]), I("i_cw1", [128, 12])]
    gnB = [I("i_gn0", [128, 4]), I("i_gn1", [128, 4])]
    cst = I("i_cst", [64, 324])
    wout = [I("i_wout0", [D, D]), I("i_wout1", [D, D])]
    w1 = [I("i_w1_0", [D, 4 * D]), I("i_w1_1", [D, 4 * D])]
    w2 = [I("i_w2_0", [4 * D, D]), I("i_w2_1", [4 * D, D])]
    hw = [I("i_hw0", [D, 3072]), I("i_hw1", [D, 3072])]
    hsw = [I("i_sw0", [128, 72]), I("i_sw1", [128, 72])]
    hsk = [I("i_skip0", [128, 16]), I("i_skip1", [128, 16])]
    fw1 = I("i_fw1", [33, 64])
    fw23 = I("i_fw23", [64, 128])
    fvec = I("i_fvec", [64, 5])
    fw4 = [I("i_fw4_0", [64, 4096]), I("i_fw4_1", [64, 4096])]
    zemb = I("i_zemb", [33, L])
    win = [I("i_win0", [L, 1024]), I("i_win1", [L, 1024])]
    idm = I("i_id", [128, 128])
    cf = I("i_cf", [16, 128, 16, 128], BF16)
    sf = I("i_sf", [16, 128, 16, 128], BF16)
    inv = I("i_inv", [4, 128, 32, 512], BF16)
    out = dram_out(nc, "o_out", [D, L])
    s_mod = nc.dram_tensor("s_mod", [128, 2, NCHK], F32).ap()
    s_m = nc.dram_tensor("s_m", [D, L], BF16).ap()
    s_x = nc.dram_tensor("s_x", [D, L], F32).ap()
    s_h1 = nc.dram_tensor("s_h1", [D, L], BF16).ap()
    s_z3 = nc.dram_tensor("s_z3", [D, L], BF16).ap()
    s_h0 = nc.dram_tensor("s_h0", [D, NTT], BF16).ap()

    def md(l, i, j):
        c0 = l * 96 + i * 16
        return s_mod[:, j, c0:c0 + 16]

    def ngc(i):
        return ng[:, i * KC:(i + 1) * KC]

    def stage(pfx, fn):
        with nc.cleanup_on_exit():
            P = Prog(nc, pfx)
            fn(P)
            P.emit()
            nc.all_engine_barrier()

    stage("A_", lambda P: emit_mod_full(P, nc, cT, aws, ab, s_mod))
    stage("H_", lambda P: emit_mix0(P, nc, xT, [ngc(0), md(0, 1, 0), md(0, 0, 0), md(0, 1, 1), md(0, 0, 1)], wB[0], lbB[0], cwB[0], gnB[0], cst, s_m,
                                    upto=-1, h_save=s_h0))
    for hh in range(2):
        stage("B%d_" % hh, lambda P, hh=hh: emit_mix0(
            P, nc, xT, None, wB[hh], lbB[hh], cwB[hh], gnB[hh], cst, s_m, mrow=(hh * 512, 1024 + hh * 512), h_load=s_h0))
    for th in range(2):
        sl = slice(th * NT, (th + 1) * NT)
        stage("C%d_" % th, lambda P, sl=sl: emit_tok(
            P, nc, s_m[:, sl], xT[:, sl], [md(0, 2, 0), ngc(1), md(0, 4, 0), md(0, 3, 0), md(0, 5, 0), ngc(2), md(1, 1, 0), md(1, 0, 0)],
            wout[0], w1[0], w2[0], s_x[:, sl], s_h1[:, sl], False))
    for hh in range(2):
        stage("D%d_" % hh, lambda P, hh=hh: emit_hy(
            P, nc, s_h1, hw[hh], hsw[hh], hsk[hh], fw1, fw23, fvec, fw4[hh], zemb, win[hh], idm, cf, sf, inv, s_z3[hh * 1024:(hh + 1) * 1024, :]))
    for th in range(2):
        sl = slice(th * NT, (th + 1) * NT)
        stage("E%d_" % th, lambda P, sl=sl: emit_tok(
            P, nc, s_z3[:, sl], s_x[:, sl], [md(1, 2, 0), ngc(3), md(1, 4, 0), md(1, 3, 0), md(1, 5, 0), ngc(4), ngc(5), ngc(5)],
            wout[1], w1[1], w2[1], out[:, sl], None, True))
    return nc


_PROGS = {}


def _pk(v):
    return np.asarray(v, np.float32).reshape(16, 128).T


def kernel(x, c, ctx, c_ctx, ada_w, ada_b, norm_g, lb_logits, ab_w_in, ab_conv_w, ab_gnorm_g, ab_w_out, hy_in_w, hy_short_w,
           hy_out_w, hy_fw1, hy_fb1, hy_fw2, hy_fb2, hy_fw3, hy_fb3, hy_fw4, hy_freq, hy_skip, mlp_w1, mlp_w2, final_g):
    f32 = lambda a: np.asarray(a, np.float32)
    C_ = np.ascontiguousarray
    x, c, ctx, c_ctx, ada_w, ada_b, norm_g = map(f32, (x, c, ctx, c_ctx, ada_w, ada_b, norm_g))
    if "fused" not in _PROGS:
        _PROGS["fused"] = build_fused()
    nc = _PROGS["fused"]
    shared = {}
    shared["i_aw0"], shared["i_aw1"] = ada_w[0], ada_w[1]
    shared["i_ab"] = C_(np.concatenate([ada_b[0], ada_b[1]]).reshape(NCHK, 128).T)
    shared["i_ng"] = C_(np.concatenate([_pk(norm_g[0, 0]), _pk(norm_g[0, 1]), _pk(norm_g[1, 0]), _pk(norm_g[1, 1]), _pk(f32(final_g)),
                                        _pk(np.zeros(D, np.float32))], 1))
    w_in = f32(ab_w_in)[0]
    in_w = f32(hy_in_w)[0]
    cf, sf, inv = hy_tables()
    shared.update({"i_cst": mix0_consts(), "i_fw1": f32(hy_fw1)[0], "i_fw23": C_(np.concatenate([f32(hy_fw2)[0], f32(hy_fw3)[0]], 1)),
                   "i_fvec": np.stack([f32(hy_fb1)[0], f32(hy_fb2)[0], f32(hy_fb3)[0], f32(hy_freq)[0], np.full(64, -np.pi, np.float32)], 1).astype(np.float32),
                   "i_id": np.eye(128, dtype=np.float32), "i_cf": cf, "i_sf": sf, "i_inv": inv,
                   "i_wout0": f32(ab_w_out)[0], "i_wout1": f32(hy_out_w)[0], "i_w1_0": f32(mlp_w1)[0], "i_w1_1": f32(mlp_w1)[1],
                   "i_w2_0": f32(mlp_w2)[0], "i_w2_1": f32(mlp_w2)[1]})
    for hh in range(2):
        cols = np.concatenate([np.arange(g * 1024 + hh * 512, g * 1024 + hh * 512 + 512) for g in range(8)])
        shared["i_w%d" % hh] = C_(w_in[:, cols])
        shared["i_lb%d" % hh] = C_(np.broadcast_to(f32(lb_logits)[:, :, hh * 512:(hh + 1) * 512].reshape(1, -1), (64, 3072)))
        shared["i_cw%d" % hh] = C_(f32(ab_conv_w)[0][:, hh * 512:(hh + 1) * 512].T.reshape(4, 128, 3).transpose(1, 0, 2).reshape(128, 12))
        shared["i_gn%d" % hh] = C_(f32(ab_gnorm_g)[0][hh * 512:(hh + 1) * 512].reshape(4, 128).T)
        cols = np.concatenate([np.arange(g * 2048 + hh * 1024, g * 2048 + hh * 1024 + 1024) for g in range(3)])
        shared["i_hw%d" % hh] = C_(in_w[:, cols])
        shared["i_sw%d" % hh] = C_(f32(hy_short_w)[0][:, cols].T.reshape(24, 128, 3).transpose(1, 0, 2).reshape(128, 72))
        shared["i_skip%d" % hh] = C_(f32(hy_skip)[0][:, hh * 1024:(hh + 1) * 1024].reshape(2, 8, 128).transpose(2, 0, 1).reshape(128, 16))
        shared["i_fw4_%d" % hh] = C_(f32(hy_fw4)[0].reshape(64, 2, 2, 2048)[:, :, :, hh * 1024:(hh + 1) * 1024].reshape(64, 4096))
        zembT, win = hy_consts(hh)
        shared["i_zemb"] = zembT
        shared["i_win%d" % hh] = win
    maps = []
    for b in range(4):
        m = dict(shared)
        m["i_xT"] = C_(np.concatenate([x[b], ctx[b]], 0).T)
        m["i_cT"] = C_(np.stack([c[b], c_ctx], 1))
        maps.append(m)
    res = run_bass_kernel_spmd(nc, maps, core_ids=list(range(4))).results
    return np.stack([C_(res[b]["o_out"].T) for b in range(4)], 0).astype(np.float32)
```
